# Optimizing a Trainium2 kernel written in Bass

```python
import jax, jax.numpy as jnp
from jax import lax
import numpy as np

D_MODEL = 1024
BATCH = 8
SEQ = 4096
DEPTH = 2

HEAD_DIM = 64
ATT_HEADS = 8
ATT_KV_HEADS = 2
ATT_GROUP = ATT_HEADS // ATT_KV_HEADS
ATT_WIDTH = ATT_HEADS * HEAD_DIM
KV_WIDTH = ATT_KV_HEADS * HEAD_DIM
WINDOW = 128
BLOCK = 128
RWKV_HEADS = 8
RWKV_WIDTH = RWKV_HEADS * HEAD_DIM
DECAY_LORA = 64
AAA_LORA = 64
MV_LORA = 32
GATE_LORA = 128
MIX_WIDTH = ATT_WIDTH + RWKV_WIDTH
ATT_IN = ATT_WIDTH + 2 * KV_WIDTH
RWKV_IN_SIZES = (RWKV_WIDTH, RWKV_WIDTH, RWKV_WIDTH, DECAY_LORA, AAA_LORA, GATE_LORA)
SHIFT_WIDTH = sum(RWKV_IN_SIZES)
IN_WIDTH = ATT_IN + SHIFT_WIDTH
D_FF = 2816
CONV_WIDTH = 3
RMS_EPS = 1e-5
LNX_EPS = 64e-5
NEG_INF = -1e30

kernel_name = "hymba_swa_sink_alibi_rwkv7_convglu"


def _split(z, sizes):
    idx = [int(i) for i in np.cumsum(sizes)[:-1]]
    return jnp.split(z, idx, axis=-1)


def rms_norm(x, g):
    xf = x.astype(jnp.float32)
    y = xf * lax.rsqrt(jnp.mean(xf * xf, axis=-1, keepdims=True) + RMS_EPS)
    return (y * g.astype(jnp.float32)).astype(x.dtype)


def alibi_slopes(n_heads):
    return 2.0 ** (-8.0 * jnp.arange(1, n_heads + 1, dtype=jnp.float32) / n_heads)


def sliding_window_attention(q, k, v, sinks):
    dt = q.dtype
    B, T, _ = q.shape
    nb = T // BLOCK
    f32 = jnp.float32
    qb = q.astype(f32).reshape(B, nb, BLOCK, ATT_KV_HEADS, ATT_GROUP, HEAD_DIM)
    kc = k.astype(f32).reshape(B, nb, BLOCK, ATT_KV_HEADS, HEAD_DIM)
    vc = v.astype(f32).reshape(B, nb, BLOCK, ATT_KV_HEADS, HEAD_DIM)
    pad = ((0, 0), (1, 0), (0, 0), (0, 0), (0, 0))
    kb = jnp.concatenate([jnp.pad(kc, pad)[:, :-1], kc], axis=2)
    vb = jnp.concatenate([jnp.pad(vc, pad)[:, :-1], vc], axis=2)
    s = jnp.einsum('bnqkgd,bnskd->bnkgqs', qb, kb) * (HEAD_DIM ** -0.5)
    qpos = jnp.arange(BLOCK)[:, None]
    kpos = jnp.arange(2 * BLOCK)[None, :]
    dist = qpos + BLOCK - kpos
    key_abs = jnp.arange(nb)[:, None, None] * BLOCK - BLOCK + kpos[None]
    valid = (dist >= 0)[None] & (dist < WINDOW)[None] & (key_abs >= 0)
    slopes = alibi_slopes(ATT_HEADS).reshape(ATT_KV_HEADS, ATT_GROUP)
    s = s - slopes[:, :, None, None] * dist.astype(f32)
    s = jnp.where(valid[None, :, None, None], s, NEG_INF)
    sink = sinks.astype(f32).reshape(ATT_KV_HEADS, ATT_GROUP)[None, None, :, :, None, None]
    m = jnp.maximum(jnp.max(s, axis=-1, keepdims=True), sink)
    p = jnp.exp(s - m)
    denom = jnp.sum(p, axis=-1, keepdims=True) + jnp.exp(sink - m)
    o = jnp.einsum('bnkgqs,bnskd->bnqkgd', p / denom, vb)
    return o.reshape(B, T, ATT_WIDTH).astype(dt)


def token_shift(u, mu):
    prev = jnp.pad(u, ((0, 0), (1, 0), (0, 0)))[:, :-1]
    return u + (prev - u) * mu


def rwkv7_time_mix(r, k, v, wd, ad, gd, w0, w2, a0, a2, g2, k_k, k_a, r_k,
                   lnx_g, lnx_b, v_first, v_mix):
    dt = r.dtype
    f32 = jnp.float32
    B, T, C = r.shape
    H, N = RWKV_HEADS, HEAD_DIM
    w = -jax.nn.softplus(-(w0 + jnp.tanh(wd) @ w2)) - 0.5
    decay = jnp.exp(-jnp.exp(w.astype(f32)))
    a = jax.nn.sigmoid(a0 + ad @ a2)
    g = jax.nn.sigmoid(gd) @ g2
    kk = (k * k_k).astype(f32).reshape(B, T, H, N)
    kk = kk / jnp.maximum(jnp.linalg.norm(kk, axis=-1, keepdims=True), 1e-12)
    kk = kk.reshape(B, T, C)
    k = k * (1 + (a - 1) * k_a)
    if v_mix is None:
        v_first = v
    else:
        v0, v1, v2 = v_mix
        v = v + (v_first - v) * jax.nn.sigmoid(v0 + (v @ v1) @ v2)

    def heads(z):
        return z.astype(f32).reshape(B, T, H, N).transpose(1, 0, 2, 3)

    xs = (heads(r), heads(decay), heads(k), heads(v), heads(-kk), heads(kk * a))

    def step(S, inp):
        r_t, w_t, k_t, v_t, a_t, b_t = inp
        sa = jnp.einsum('bhvk,bhk->bhv', S, a_t)
        S = S * w_t[:, :, None, :] + sa[..., None] * b_t[:, :, None, :] \
            + v_t[..., None] * k_t[:, :, None, :]
        return S, jnp.einsum('bhvk,bhk->bhv', S, r_t)

    S0 = jnp.zeros((B, H, N, N), f32)
    _, y = lax.scan(step, S0, xs)
    y = y.transpose(1, 0, 2, 3)
    mu = jnp.mean(y, axis=-1, keepdims=True)
    var = jnp.mean(jnp.square(y - mu), axis=-1, keepdims=True)
    y = ((y - mu) * lax.rsqrt(var + LNX_EPS)).reshape(B, T, C)
    y = y * lnx_g.astype(f32) + lnx_b.astype(f32)
    r4 = r.astype(f32).reshape(B, T, H, N)
    k4 = k.astype(f32).reshape(B, T, H, N)
    v4 = v.astype(f32).reshape(B, T, H, N)
    bonus = jnp.sum(r4 * k4 * r_k.astype(f32).reshape(H, N), axis=-1, keepdims=True) * v4
    y = (y + bonus.reshape(B, T, C)) * g.astype(f32)
    return y.astype(dt), v_first


def conv_glu_ffn(h, w_gate, w_up, conv_w, conv_b, w_down):
    T = h.shape[1]
    u = h @ w_gate
    up = jnp.pad(u, ((0, 0), (CONV_WIDTH - 1, 0), (0, 0)))
    c = conv_b + sum(conv_w[j] * up[:, j:j + T] for j in range(CONV_WIDTH))
    return (jax.nn.silu(c) * (h @ w_up)) @ w_down


def setup_inputs(seed: int = 0) -> dict:
    key = jax.random.key(seed)
    ks = iter(jax.random.split(key, 40))
    f32 = jnp.float32
    L, C, D, F = DEPTH, RWKV_WIDTH, D_MODEL, D_FF

    def nrm(shape, scale):
        return jax.random.normal(next(ks), shape, f32) * scale

    def unif(shape, lo, hi):
        return jax.random.uniform(next(ks), shape, f32, minval=lo, maxval=hi)

    return {
        "x": nrm((BATCH, SEQ, D), 1.0),
        "norm1_g": 1.0 + nrm((L, D), 0.01),
        "w_in": nrm((L, D, IN_WIDTH), D ** -0.5),
        "attn_sinks": nrm((L, ATT_HEADS), 0.5),
        "shift_mu": unif((L, SHIFT_WIDTH), 0.0, 1.0),
        "w0": unif((L, C), -5.0, 1.0),
        "w2": nrm((L, DECAY_LORA, C), 0.1),
        "a0": nrm((L, C), 0.1),
        "a2": nrm((L, AAA_LORA, C), 0.1),
        "g2": nrm((L, GATE_LORA, C), GATE_LORA ** -0.5),
        "k_k": 0.85 + nrm((L, C), 0.05),
        "k_a": 1.0 + nrm((L, C), 0.05),
        "r_k": nrm((L, C), 0.1),
        "lnx_g": 1.0 + nrm((L, C), 0.01),
        "lnx_b": nrm((L, C), 0.01),
        "v0": nrm((L - 1, C), 0.1),
        "v1": nrm((L - 1, C, MV_LORA), C ** -0.5),
        "v2": nrm((L - 1, MV_LORA, C), 0.1),
        "w_out": nrm((L, MIX_WIDTH, D), MIX_WIDTH ** -0.5),
        "norm2_g": 1.0 + nrm((L, D), 0.01),
        "ffn_w_gate": nrm((L, D, F), D ** -0.5),
        "ffn_w_up": nrm((L, D, F), D ** -0.5),
        "conv_w": nrm((L, CONV_WIDTH, F), 0.5),
        "conv_b": nrm((L, F), 0.01),
        "ffn_w_down": nrm((L, F, D), F ** -0.5),
        "final_g": 1.0 + nrm((D,), 0.01),
    }


def reference(x, norm1_g, w_in, attn_sinks, shift_mu, w0, w2, a0, a2, g2, k_k, k_a,
              r_k, lnx_g, lnx_b, v0, v1, v2, w_out, norm2_g, ffn_w_gate, ffn_w_up,
              conv_w, conv_b, ffn_w_down, final_g):
    v_first = None
    for l in range(DEPTH):
        h = rms_norm(x, norm1_g[l])
        proj = h @ w_in[l]
        q, ka, va = _split(proj[..., :ATT_IN], (ATT_WIDTH, KV_WIDTH, KV_WIDTH))
        rest = token_shift(proj[..., ATT_IN:], shift_mu[l])
        r, k, v, wd, ad, gd = _split(rest, RWKV_IN_SIZES)
        att = sliding_window_attention(q, ka, va, attn_sinks[l])
        v_mix = None if l == 0 else (v0[l - 1], v1[l - 1], v2[l - 1])
        rw, v_first = rwkv7_time_mix(r, k, v, wd, ad, gd, w0[l], w2[l], a0[l], a2[l],
                                     g2[l], k_k[l], k_a[l], r_k[l], lnx_g[l], lnx_b[l],
                                     v_first, v_mix)
        x = x + jnp.concatenate([att, rw], axis=-1) @ w_out[l]
        h = rms_norm(x, norm2_g[l])
        x = x + conv_glu_ffn(h, ffn_w_gate[l], ffn_w_up[l], conv_w[l], conv_b[l], ffn_w_down[l])
    return rms_norm(x, final_g)
```

```python
import os as _os
import numpy as np
from contextlib import ExitStack
import concourse.bass as bass
import concourse.mybir as mybir
from concourse.bass_utils import run_bass_kernel_spmd
from concourse.alu_op_type import AluOpType as ALU

AF = mybir.ActivationFunctionType
F32 = mybir.dt.float32
BF16 = mybir.dt.bfloat16

T = 4096
D = 1024
L = 2
NIN = 2560
FF = 2816
NFC = 22
TM = 256
NTM = T // TM
SUBM = TM // 128
CH = 64
NCH = TM // CH
TF = 512
NTF = T // TF
SUBF = TF // 128
NCV = 150
CV_G1, CV_G2, CV_MU, CV_W0, CV_A0, CV_KK, CV_KA, CV_RK, CV_LG, CV_LB, CV_V0, CV_CW, CV_CB = \
    0, 8, 16, 30, 34, 38, 42, 46, 50, 54, 58, 62, 128
RMS_EPS = 1e-5
LNX_EPS = 64e-5
EM05 = float(np.exp(-0.5))


class Buf:
    __slots__ = ("name", "w", "r")

    def __init__(self, name):
        self.name = name
        self.w = None
        self.r = {}


class Sched:
    NDMA = 32

    def __init__(self, nc):
        self.nc = nc
        self.names = ["pe", "dve", "act", "pool", "sp"]
        self.engobj = {"pe": nc.tensor, "dve": nc.vector, "act": nc.scalar, "pool": nc.gpsimd, "sp": nc.sync}
        self.count = {e: 0 for e in self.names}
        self.waited = {e: {} for e in self.names}
        self.sem_by_id = {}
        for e in self.names:
            self.sem_by_id[("e", e)] = nc.alloc_semaphore("es_" + e)
        for i in range(self.NDMA):
            self.sem_by_id[("d", i)] = nc.alloc_semaphore("ds_%d" % i)
        self.duse = [0] * self.NDMA
        self.drr = 0
        self.ninst = 0

    def _deps(self, eng, reads, writes):
        deps = []
        for b in reads:
            if b.w is not None:
                deps.append(b.w)
        for b in writes:
            if b.w is not None:
                deps.append(b.w)
            deps.extend(b.r.values())
        waits = []
        for (sid, val, src) in deps:
            if src == "pe" and eng == "pe":
                continue
            if self.waited[eng].get(sid, 0) >= val:
                continue
            self.waited[eng][sid] = val
            waits.append((sid, val))
        return waits

    def _commit(self, eng, ev, reads, writes):
        key = eng if ev[2] != "dma" else ev[0]
        for b in reads:
            b.r[key] = ev
        for b in writes:
            b.w = ev
            b.r = {}

    def _emit(self, name, waits, meth, kw, inc):
        eng = self.engobj[name]
        for (sid, val) in waits:
            eng.wait_ge(self.sem_by_id[sid], val)
            self.ninst += 1
        if meth is not None:
            ins = getattr(eng, meth)(**kw)
            ins.then_inc(self.sem_by_id[inc[0]], inc[1])
            self.ninst += 1

    def op(self, eng, meth, kw, reads=(), writes=()):
        waits = self._deps(eng, reads, writes)
        self.count[eng] += 1
        ev = (("e", eng), self.count[eng], eng)
        self._emit(eng, waits, meth, kw, (("e", eng), 1))
        self._commit(eng, ev, reads, writes)

    def dma(self, out, in_, reads=(), writes=(), eng="sp"):
        waits = self._deps(eng, reads, writes)
        j = self.drr
        self.drr = (j + 1) % self.NDMA
        sid = ("d", j)
        if self.duse[j] > 0:
            val = 16 * self.duse[j]
            if self.waited[eng].get(sid, 0) < val:
                self.waited[eng][sid] = val
                waits.append((sid, val))
        self.duse[j] += 1
        ev = (sid, 16 * self.duse[j], "dma")
        self._emit(eng, waits, "dma_start", dict(out=out, in_=in_), (sid, 16))
        self._commit(eng, ev, reads, writes)

    def barrier(self):
        evs = []
        for j in range(self.NDMA):
            if self.duse[j] > 0:
                evs.append((("d", j), 16 * self.duse[j]))
        for e in self.names:
            if self.count[e] > 0:
                evs.append((("e", e), self.count[e]))
        for e in self.names:
            w = []
            for (sid, val) in evs:
                if self.waited[e].get(sid, 0) < val:
                    self.waited[e][sid] = val
                    w.append((sid, val))
            self._emit(e, w, None, None, None)

    def finish(self):
        self.barrier()


def build_program(dbg=None):
    nc = bass.Bass("TRN2", target_bir_lowering=False)
    S = Sched(nc)

    def din(name, shape, dt=F32):
        return nc.dram_tensor(name, list(shape), dt, kind="ExternalInput").ap()

    x_in = din("x", [T, D])
    winL = din("winL", [L, 128, 8, NIN])
    woAL = din("woAL", [L, 64, 8, D])
    woRL = din("woRL", [L, 128, 4, D])
    wguL = din("wguL", [L, 8, 128, 2, 11 * 256])
    wdL = din("wdL", [L, 128, NFC * D])
    w2a2L = din("w2a2L", [L, 128, 512])
    g2L = din("g2L", [L, 128, 512])
    v1L = din("v1L", [128, 4, 32])
    v2L = din("v2L", [32, 512])
    cvL = din("cvL", [L, 128, NCV])
    sinkL = din("sinkL", [L, 1, 8])
    fing = din("fing", [1, D])
    c_ident = din("c_ident", [128, 128])
    c_maskA = din("c_maskA", [64, 512])
    c_reset = din("c_reset", [128, TM])
    c_identI8 = din("c_identI8", [64, 512])
    c_bones = din("c_bones", [128, 128])
    c_amask = din("c_amask", [128, 2, 512])
    c_qaug = din("c_qaug", [3, 8, TM])
    c_kaug = din("c_kaug", [3, 2, 128 + TM])
    out_d = nc.dram_tensor("out", [T, D], F32, kind="ExternalOutput").ap()
    xs = nc.dram_tensor("xs", [T, D], F32, kind="Internal").ap()
    vf = nc.dram_tensor("vf", [128, 4, T], F32, kind="Internal").ap()
    winB = [nc.dram_tensor("winB%d" % l, [128, 8, NIN], BF16, kind="Internal").ap() for l in range(L)]
    woAB = [nc.dram_tensor("woAB%d" % l, [64, 8 * D], BF16, kind="Internal").ap() for l in range(L)]
    woRB = [nc.dram_tensor("woRB%d" % l, [128, 4 * D], BF16, kind="Internal").ap() for l in range(L)]
    wguB = [nc.dram_tensor("wguB%d" % l, [NFC, 128, 8, 256], BF16, kind="Internal").ap() for l in range(L)]
    wdB = [nc.dram_tensor("wdB%d" % l, [128, NFC * D], BF16, kind="Internal").ap() for l in range(L)]
    dbg_out = {}
    if dbg:
        for nm, shp in dbg.items():
            if nm.startswith("_"):
                continue
            dbg_out[nm] = nc.dram_tensor("dbg_" + nm, list(shp), F32, kind="ExternalOutput").ap()

    b_x = Buf("x")
    b_xs = [Buf("xs%d" % i) for i in range(NTM)]
    b_out = Buf("out")
    b_vf = [Buf("vf%d" % i) for i in range(NTM)]
    b_wscr = {}

    PB = [nc.alloc_psum_tensor("pb%d" % i, [128, 512], F32) for i in range(8)]
    bPB = [Buf("pb%d" % i) for i in range(8)]

    uid = {"i": 0}

    def tiles(st, specs):
        res = {}
        uid["i"] += 1
        for nm, shp, dt in specs:
            res[nm] = st.enter_context(nc.sbuf_tensor("%s_%d" % (nm, uid["i"]), list(shp), dt))
        if dbg and dbg.get("_mem"):
            print("SBUF after tiles group", uid["i"], nc.bytes_allocated(res[nm].space), "of", nc.space_capacity(res[nm].space))
        return res

    rr = {"i": 0}

    def ew():
        rr["i"] += 1
        return ("dve", "pool")[rr["i"] % 2]

    cst = ExitStack()
    C = tiles(cst, [
        ("identf", [128, 128], F32), ("identb", [128, 128], BF16), ("bones", [128, 128], F32),
        ("maskA", [64, 256], F32), ("resetm", [128, TM], F32), ("identI8", [64, 512], BF16),
        ("amask", [128, 2, 512], BF16),
    ])
    bC = Buf("consts")
    S.dma(C["identf"][:], c_ident, writes=[bC])
    S.dma(C["bones"][:], c_bones, writes=[bC])
    S.dma(C["maskA"][:], c_maskA[:, 0:256], writes=[bC])
    S.dma(C["resetm"][:], c_reset, writes=[bC])
    with ExitStack() as st0:
        Cs = tiles(st0, [("i8f", [64, 512], F32), ("amf", [128, 2, 512], F32)])
        bCs = Buf("cstage")
        S.dma(Cs["i8f"][:], c_identI8, writes=[bCs])
        S.dma(Cs["amf"][:], c_amask, writes=[bCs])
        S.op("dve", "tensor_copy", dict(out=C["identI8"][:], in_=Cs["i8f"][:]), [bCs], [bC])
        S.op("dve", "tensor_copy", dict(out=C["amask"][:], in_=Cs["amf"][:]), [bCs], [bC])
        S.barrier()
    S.op("dve", "tensor_copy", dict(out=C["identb"][:], in_=C["identf"][:]), [bC], [bC])

    def drive(gens, weights=None):
        items = [[g, (weights[i] if weights else 1)] for i, g in enumerate(gens) if g is not None]
        while items:
            for itm in list(items):
                for _ in range(itm[1]):
                    try:
                        next(itm[0])
                    except StopIteration:
                        items.remove(itm)
                        break

    def pre_tiles(nstg):
        return ([("stg%d" % i, [128, FF], F32) for i in range(nstg)] + [("ob%d" % i, [128, FF], BF16) for i in range(nstg)]
                + [("cvp", [128, NCV], F32)])

    def prepass_gen(l, W, store_eng, lazy_store, load_eng="sp", engs=("dve", "pool"), NSTG=3):
        bs = [Buf("stg%d" % i) for i in range(NSTG)]
        bo = [Buf("ob%d" % i) for i in range(NSTG)]
        bcv = Buf("cvp")
        S.dma(W["cvp"][:], cvL[l], writes=[bcv])
        bw = {k: [] for k in ("win", "woA", "woR", "wgu", "wd")}
        b_wscr[l] = bw
        pieces = []
        for c in range(8):
            pieces.append((winL[l, :, c, :], winB[l][:, c, :], 128, NIN, W["cvp"][:, CV_G1 + c:CV_G1 + c + 1], bw["win"], None))
        woA_src = woAL[l].rearrange("p h n -> p (h n)")
        for i in range(4):
            pieces.append((woA_src[:, i * 2048:(i + 1) * 2048], woAB[l][:, i * 2048:(i + 1) * 2048], 64, 2048, None, bw["woA"], None))
        woR_src = woRL[l].rearrange("p h n -> p (h n)")
        for i in range(2):
            pieces.append((woR_src[:, i * 2048:(i + 1) * 2048], woRB[l][:, i * 2048:(i + 1) * 2048], 128, 2048, None, bw["woR"], None))
        for c in range(8):
            for hf in range(2):
                dst = wguB[l][hf * 11:(hf + 1) * 11, :, c, :].rearrange("f p j -> p f j")
                pieces.append((wguL[l, c, :, hf, :], dst, 128, FF, W["cvp"][:, CV_G2 + c:CV_G2 + c + 1], bw["wgu"], 256))
        for i in range(8):
            pieces.append((wdL[l][:, i * FF:(i + 1) * FF], wdB[l][:, i * FF:(i + 1) * FF], 128, FF, None, bw["wd"], None))
        pending = None

        def store(pd):
            (dst, ob_ap, bok, wb, dv3) = pd
            if dv3 is None:
                S.dma(dst, ob_ap, reads=[bok], writes=[wb], eng=store_eng)
            else:
                S.dma(dst, ob_ap.rearrange("p (a b) -> p a b", b=dv3), reads=[bok], writes=[wb], eng=store_eng)

        for i, (src, dst, np_, n, scale, wbl, dv3) in enumerate(pieces):
            wb = Buf("wpiece")
            wbl.append(wb)
            k = i % NSTG
            stg = W["stg%d" % k]
            ob = W["ob%d" % k]
            if pending is not None:
                store(pending)
                pending = None
            S.dma(stg[0:np_, 0:n], src, writes=[bs[k]], eng=load_eng)
            e = engs[i % len(engs)]
            if scale is None:
                S.op(e, "tensor_copy", dict(out=ob[0:np_, 0:n], in_=stg[0:np_, 0:n]), [bs[k]], [bo[k]])
            else:
                S.op(e, "tensor_scalar", dict(out=ob[0:np_, 0:n], in0=stg[0:np_, 0:n], scalar1=scale, scalar2=None, op0=ALU.mult),
                     [bs[k], bcv], [bo[k]])
            pd = (dst, ob[0:np_, 0:n], bo[k], wb, dv3)
            if lazy_store:
                pending = pd
            else:
                store(pd)
            yield
        if pending is not None:
            store(pending)

    def prepass(l):
        with ExitStack() as st:
            W = tiles(st, pre_tiles(4))
            drive([prepass_gen(l, W, "act", False, engs=("dve",), NSTG=4)])
        S.barrier()

    def mixer(l, src, b_src_fn):
        with ExitStack() as st:
            W = tiles(st, [
                ("WinT", [128, 8, NIN], BF16), ("WoA", [64, 8, D], BF16), ("WoR", [128, 4, D], BF16),
                ("w2a2b", [128, 512], BF16), ("g2b", [128, 512], BF16),
                ("v1b", [128, 4, 32], BF16), ("v2b", [32, 512], BF16),
                ("cv", [128, NCV], F32), ("omka", [128, 4], F32), ("esk", [64, 8], F32), ("onesb", [128, 64], BF16),
                ("ulast", [128, 14], F32), ("H", [128, 4, 64], F32), ("Hb", [128, 4, 64], BF16),
                ("kT", [67, 2, 128 + TM], BF16), ("Vaug", [128, 1 + SUBM, 2, 65], BF16), ("qT", [67, 8, TM], BF16),
                ("xt", [128, SUBM, D], F32),
                ("ss", [128, SUBM], F32), ("rstd", [128, SUBM], F32),
                ("xnb", [128, D], BF16), ("xT", [128, 8, TM], BF16),
                ("u", [128, 1 + TM], F32), ("dsh", [128, TM], F32),
                ("twa", [128, TM], BF16), ("sgd0", [128, TM], BF16), ("sgd1", [128, TM], BF16),
                ("vall", [128, 4, TM], F32), ("vbf", [128, 4, TM], BF16), ("lvb", [32, TM], BF16),
                ("rk", [128, 2, TM], F32),
                ("tbig", [128, 12, TM], F32),
                ("gC0", [128, 4, NCH], F32), ("gC1", [128, 4, NCH], F32),
                ("AR0", [128, 4, NCH, 2, CH], BF16), ("BT0", [128, 4, TM], BF16), ("KT0", [128, 4, TM], BF16),
                ("AR1", [128, 4, NCH, 2, CH], BF16), ("BT1", [128, 4, TM], BF16), ("KT1", [128, 4, TM], BF16),
                ("bp", [128, TM], BF16), ("kpp", [128, TM], BF16),
                ("TM30", [64, 8, NCH, 3, 64], BF16), ("TM31", [64, 8, NCH, 3, 64], BF16),
                ("bonus0", [128, 4, TM], F32), ("bonus1", [128, 4, TM], F32), ("yT", [128, 4, TM], F32),
                ("Amat", [64, 8, 256], BF16), ("Nm", [64, 8, 64], BF16), ("NTm", [64, 8, 64], BF16),
                ("Nm2", [64, 8, 64], BF16), ("NTm2", [64, 8, 64], BF16),
                ("Pm", [64, 8, 64], BF16), ("Pm2", [64, 8, 64], BF16), ("Zb", [64, 8, 64], BF16), ("Ub", [64, 8, 64], BF16),
                ("sT", [128, 1, 512], F32), ("PT", [128, 2, 512], BF16),
                ("den", [65, 512], F32), ("attT0", [64, 8, TM], BF16), ("attT1", [64, 8, TM], BF16), ("rwT", [128, 4, TM], BF16),
                ("xos0", [128, 512], F32), ("xos1", [128, 512], F32),
                ("pp0", [128, TM], F32), ("pp1", [128, TM], F32),
            ])
            B = {k: Buf(k) for k in W}
            for i in range(12):
                W["t%d" % i] = W["tbig"][:, i, :]
                B["t%d" % i] = Buf("t%d" % i)
            xtf = W["xt"][:].rearrange("p j d -> p (j d)")
            W["w2a2f"] = xtf[:, 0:512]
            W["g2f"] = xtf[:, 512:1024]
            W["v1f"] = xtf[:, 1024:1152].rearrange("p (a b) -> p a b", b=32)
            W["v2f"] = xtf[0:32, 1152:1664]
            W["augf"] = xtf[0:67, 0:8 * TM].rearrange("p (a b) -> p a b", b=TM)
            tbf = W["tbig"][:].rearrange("p a b -> p (a b)")
            W["kaugf"] = tbf[0:67, 0:2 * (128 + TM)].rearrange("p (a b) -> p a b", b=128 + TM)
            for nm in ("w2a2f", "g2f", "v1f", "v2f", "augf"):
                B[nm] = B["xt"]
            B["kaugf"] = B["t0"]
            byT = [Buf("yT%d" % i) for i in range(4)]
            W["OTc"] = W["den"][0:64, :]
            B["OTc"] = Buf("OTc")
            W["Z2b"] = W["Ub"][:].rearrange("p h c -> p (h c)")
            B["Z2b"] = B["Ub"]
            bw = b_wscr[l]
            S.dma(W["WinT"][:], winB[l], reads=bw["win"], writes=[B["WinT"]])
            S.dma(W["WoA"][:].rearrange("p h n -> p (h n)"), woAB[l], reads=bw["woA"], writes=[B["WoA"]])
            S.dma(W["WoR"][:].rearrange("p h n -> p (h n)"), woRB[l], reads=bw["woR"], writes=[B["WoR"]])
            S.dma(W["w2a2f"][:], w2a2L[l], writes=[B["w2a2f"]])
            S.dma(W["g2f"][:], g2L[l], writes=[B["g2f"]])
            S.dma(W["cv"][:], cvL[l], writes=[B["cv"]])
            S.op("dve", "tensor_copy", dict(out=W["w2a2b"][:], in_=W["w2a2f"][:]), [B["w2a2f"]], [B["w2a2b"]])
            S.op("dve", "tensor_copy", dict(out=W["g2b"][:], in_=W["g2f"][:]), [B["g2f"]], [B["g2b"]])
            if l > 0:
                S.dma(W["v1f"][:], v1L, writes=[B["v1f"]])
                S.dma(W["v2f"][:], v2L, writes=[B["v2f"]])
                S.op("dve", "tensor_copy", dict(out=W["v1b"][:], in_=W["v1f"][:]), [B["v1f"]], [B["v1b"]])
                S.op("dve", "tensor_copy", dict(out=W["v2b"][:], in_=W["v2f"][:]), [B["v2f"]], [B["v2b"]])
            cv = W["cv"]

            def col(base, i):
                return cv[:, base + i:base + i + 1]
            S.op("dve", "tensor_scalar", dict(out=W["omka"][:], in0=cv[:, CV_KA:CV_KA + 4], scalar1=-1.0, scalar2=1.0,
                                              op0=ALU.mult, op1=ALU.add), [B["cv"]], [B["omka"]])
            S.dma(W["esk"][:], sinkL[l].broadcast_to([64, 8]), writes=[B["esk"]])
            S.op("act", "activation", dict(out=W["esk"][:], in_=W["esk"][:], func=AF.Exp), [B["esk"]], [B["esk"]])
            S.op("pool", "memset", dict(ap=W["onesb"][:], constant=1.0), [], [B["onesb"]])
            S.dma(W["augf"][64:67, :, :], c_qaug, writes=[B["augf"]])
            S.dma(W["kaugf"][64:67, :, :], c_kaug, writes=[B["kaugf"], B["t1"], B["t2"], B["t3"]])
            S.op("act", "activation", dict(out=W["qT"][64:67, :, :], in_=W["augf"][64:67, :, :], func=AF.Copy), [B["augf"]], [B["qT"]])
            S.op("act", "activation", dict(out=W["kT"][64:67, :, :], in_=W["kaugf"][64:67, :, :], func=AF.Copy), [B["kaugf"], B["t1"], B["t2"], B["t3"]], [B["kT"]])
            S.op("pool", "memset", dict(ap=W["ulast"][:], constant=0.0), [], [B["ulast"]])
            S.op("pool", "memset", dict(ap=W["H"][:], constant=0.0), [], [B["H"]])
            S.op("pool", "memset", dict(ap=W["Hb"][:], constant=0.0), [], [B["Hb"]])
            S.op("pool", "memset", dict(ap=W["Vaug"][:], constant=1.0), [], [B["Vaug"]])
            S.op("pool", "memset", dict(ap=W["kT"][0:64, :, :], constant=0.0), [], [B["kT"]])

            pj = {"i": 0, "b": 0}

            def fbank():
                pj["i"] += 1
                return pj["i"] % 2

            def bbank():
                pj["b"] += 1
                return 3 + pj["b"] % 2

            def proj_fm(col0, ncols):
                k = fbank()
                for c in range(8):
                    S.op("pe", "matmul", dict(out=PB[k][0:ncols, 0:TM], lhsT=W["WinT"][:, c, col0:col0 + ncols], rhs=W["xT"][:, c, :],
                                              start=(c == 0), stop=(c == 7)), [B["WinT"], B["xT"]], [bPB[k]])
                return k

            def shift_chunk(k, ci, dst, bdst):
                u = W["u"]
                S.op("act", "activation", dict(out=u[:, 1:1 + TM], in_=PB[k][:, 0:TM], func=AF.Copy), [bPB[k]], [B["u"]])
                S.op("pool", "tensor_copy", dict(out=u[:, 0:1], in_=W["ulast"][:, ci:ci + 1]), [B["ulast"]], [B["u"]])
                S.op("dve", "tensor_tensor", dict(out=W["dsh"][:], in0=u[:, 0:TM], in1=u[:, 1:1 + TM], op=ALU.subtract), [B["u"]], [B["dsh"]])
                S.op("dve", "scalar_tensor_tensor", dict(out=dst, in0=W["dsh"][:], scalar=col(CV_MU, ci), in1=u[:, 1:1 + TM],
                                                          op0=ALU.mult, op1=ALU.add), [B["dsh"], B["u"], B["cv"]], [bdst])
                S.op("pool", "tensor_copy", dict(out=W["ulast"][:, ci:ci + 1], in_=u[:, TM:TM + 1]), [B["u"]], [B["ulast"]])

            RW0 = 768

            def front(it):
                par = it % 2
                AR, BT, KT, TM3, gCt, bonus, sgd = (W["AR%d" % par], W["BT%d" % par], W["KT%d" % par], W["TM3%d" % par],
                                                    W["gC%d" % par], W["bonus%d" % par], W["sgd%d" % par])
                attT, battT = W["attT%d" % par], B["attT%d" % par]
                bAR, bBT, bKT, bTM3, bgC, bbonus, bsgd = (B["AR%d" % par], B["BT%d" % par], B["KT%d" % par], B["TM3%d" % par],
                                                          B["gC%d" % par], B["bonus%d" % par], B["sgd%d" % par])
                tok0 = it * TM
                bsrc = b_src_fn(it)
                S.dma(W["xt"][:], src[tok0:tok0 + TM, :].rearrange("(j p) d -> p j d", p=128), reads=[bsrc], writes=[B["xt"]])
                for j in range(SUBM):
                    S.op("act", "activation", dict(out=W["xnb"][:], in_=W["xt"][:, j, :], func=AF.Square, accum_out=W["ss"][:, j:j + 1]),
                         [B["xt"]], [B["xnb"], B["ss"]])
                S.op("dve", "tensor_scalar", dict(out=W["rstd"][:], in0=W["ss"][:], scalar1=1.0 / D, scalar2=RMS_EPS, op0=ALU.mult, op1=ALU.add),
                     [B["ss"]], [B["rstd"]])
                S.op("act", "activation", dict(out=W["rstd"][:], in_=W["rstd"][:], func=AF.Sqrt), [B["rstd"]], [B["rstd"]])
                S.op("dve", "reciprocal", dict(out=W["rstd"][:], in_=W["rstd"][:]), [B["rstd"]], [B["rstd"]])
                yield
                for j in range(SUBM):
                    S.op("dve", "tensor_scalar", dict(out=W["xnb"][:], in0=W["xt"][:, j, :], scalar1=W["rstd"][:, j:j + 1], scalar2=None, op0=ALU.mult),
                         [B["xt"], B["rstd"]], [B["xnb"]])
                    pT = PB[2][:].bitcast(BF16)
                    for c in range(8):
                        S.op("pe", "transpose", dict(out=pT[:, c * 128:(c + 1) * 128], in_=W["xnb"][:, c * 128:(c + 1) * 128], identity=C["identb"][:]),
                             [B["xnb"], bC], [bPB[2]])
                    S.op("act", "activation", dict(out=W["xT"][:, :, j * 128:(j + 1) * 128], in_=pT[:, 0:1024].rearrange("p (c t) -> p c t", t=128),
                                                   func=AF.Copy), [bPB[2]], [B["xT"]])
                    yield
                for h in range(8):
                    k = proj_fm(h * 64, 64)
                    S.op("act", "activation", dict(out=W["qT"][0:64, h, :], in_=PB[k][0:64, 0:TM], func=AF.Copy, scale=0.125), [bPB[k]], [B["qT"]])
                    yield
                for kv in range(2):
                    k = proj_fm(512 + kv * 64, 64)
                    S.op("dve", "tensor_copy", dict(out=W["kT"][0:64, kv, 128:128 + TM], in_=PB[k][0:64, 0:TM]), [bPB[k]], [B["kT"]])
                    yield
                for j in range(SUBM):
                    k = fbank()
                    for c in range(8):
                        S.op("pe", "matmul", dict(out=PB[k][:, 0:128], lhsT=W["xT"][:, c, j * 128:(j + 1) * 128], rhs=W["WinT"][:, c, 640:768],
                                                  start=(c == 0), stop=(c == 7)), [B["WinT"], B["xT"]], [bPB[k]])
                    S.op("dve", "tensor_copy", dict(out=W["Vaug"][:, 1 + j, :, 0:64], in_=PB[k][:, 0:128].rearrange("p (a b) -> p a b", b=64)),
                         [bPB[k]], [B["Vaug"]])
                    yield
                k = proj_fm(RW0 + 1536, 128)
                shift_chunk(k, 12, W["t0"], B["t0"])
                S.op("act", "activation", dict(out=W["twa"][0:64, :], in_=W["t0"][0:64, :], func=AF.Tanh), [B["t0"]], [B["twa"]])
                S.op("dve", "tensor_copy", dict(out=W["twa"][64:128, :], in_=W["t0"][64:128, :]), [B["t0"]], [B["twa"]])
                yield
                k = proj_fm(RW0 + 1664, 128)
                shift_chunk(k, 13, W["t0"], B["t0"])
                S.op("act", "activation", dict(out=sgd[:], in_=W["t0"], func=AF.Sigmoid), [B["t0"]], [bsgd])
                yield
                for cc in range(4):
                    k = proj_fm(RW0 + 1024 + cc * 128, 128)
                    shift_chunk(k, 8 + cc, W["vall"][:, cc, :], B["vall"])
                    yield
                if l == 0:
                    S.dma(vf[:, :, tok0:tok0 + TM], W["vall"][:], reads=[B["vall"]], writes=[b_vf[it]], eng="sp")
                else:
                    vfst = bonus
                    S.dma(vfst[:], vf[:, :, tok0:tok0 + TM], reads=[b_vf[it]], writes=[bbonus])
                    S.op("pool", "tensor_copy", dict(out=W["vbf"][:], in_=W["vall"][:]), [B["vall"]], [B["vbf"]])
                    k = fbank()
                    for cc in range(4):
                        S.op("pe", "matmul", dict(out=PB[k][0:32, 0:TM], lhsT=W["v1b"][:, cc, :], rhs=W["vbf"][:, cc, :], start=(cc == 0), stop=(cc == 3)),
                             [B["v1b"], B["vbf"]], [bPB[k]])
                    S.op("act", "activation", dict(out=W["lvb"][:], in_=PB[k][0:32, 0:TM], func=AF.Copy), [bPB[k]], [B["lvb"]])
                    yield
                    for cc in range(4):
                        k = fbank()
                        S.op("pe", "matmul", dict(out=PB[k][:, 0:TM], lhsT=W["v2b"][0:32, cc * 128:(cc + 1) * 128], rhs=W["lvb"][:], start=True, stop=True),
                             [B["v2b"], B["lvb"]], [bPB[k]])
                        S.op("act", "activation", dict(out=W["t0"], in_=PB[k][:, 0:TM], func=AF.Sigmoid, bias=col(CV_V0, cc)), [bPB[k], B["cv"]], [B["t0"]])
                        S.op("dve", "tensor_tensor", dict(out=W["t1"], in0=vfst[:, cc, :], in1=W["vall"][:, cc, :], op=ALU.subtract),
                             [bbonus, B["vall"]], [B["t1"]])
                        S.op("dve", "tensor_tensor", dict(out=W["t1"], in0=W["t1"], in1=W["t0"], op=ALU.mult), [B["t1"], B["t0"]], [B["t1"]])
                        S.op("dve", "tensor_tensor", dict(out=W["vall"][:, cc, :], in0=W["vall"][:, cc, :], in1=W["t1"], op=ALU.add),
                             [B["vall"], B["t1"]], [B["vall"]])
                        yield
                S.op("pool", "tensor_copy", dict(out=W["vbf"][:], in_=W["vall"][:]), [B["vall"]], [B["vbf"]])
                yield
                for cc in range(4):
                    k = proj_fm(RW0 + cc * 128, 128)
                    shift_chunk(k, cc, W["rk"][:, 0, :], B["rk"])
                    yield
                    k = proj_fm(RW0 + 512 + cc * 128, 128)
                    shift_chunk(k, 4 + cc, W["rk"][:, 1, :], B["rk"])
                    yield
                    r_ = W["rk"][:, 0, :]
                    k_ = W["rk"][:, 1, :]
                    v_ = W["vall"][:, cc, :]
                    ld, cl, g, gi, gm1, a, kk, kkn, kp, ba, tA, tB = [W["t%d" % i] for i in range(12)]
                    bl = [B["t%d" % i] for i in range(12)]
                    kw = fbank()
                    S.op("pe", "matmul", dict(out=PB[kw][:, 0:TM], lhsT=W["w2a2b"][0:64, cc * 128:(cc + 1) * 128], rhs=W["twa"][0:64, :], start=True, stop=True),
                         [B["w2a2b"], B["twa"]], [bPB[kw]])
                    ka_ = fbank()
                    S.op("pe", "matmul", dict(out=PB[ka_][:, 0:TM], lhsT=W["w2a2b"][64:128, cc * 128:(cc + 1) * 128], rhs=W["twa"][64:128, :], start=True, stop=True),
                         [B["w2a2b"], B["twa"]], [bPB[ka_]])
                    S.op("act", "activation", dict(out=ld, in_=PB[kw][:, 0:TM], func=AF.Sigmoid, bias=col(CV_W0, cc)), [bPB[kw], B["cv"]], [bl[0]])
                    S.op("act", "activation", dict(out=a, in_=PB[ka_][:, 0:TM], func=AF.Sigmoid, bias=col(CV_A0, cc)), [bPB[ka_], B["cv"]], [bl[5]])
                    S.op("pool", "tensor_scalar", dict(out=kk, in0=k_, scalar1=col(CV_KK, cc), scalar2=None, op0=ALU.mult), [B["rk"], B["cv"]], [bl[6]])
                    S.op("act", "activation", dict(out=tA, in_=kk, func=AF.Square), [bl[6]], [bl[10]])
                    yield
                    S.op("dve", "tensor_scalar", dict(out=kkn, in0=a, scalar1=col(CV_KA, cc), scalar2=W["omka"][:, cc:cc + 1], op0=ALU.mult, op1=ALU.add),
                         [bl[5], B["cv"], B["omka"]], [bl[7]])
                    S.op("pool", "tensor_tensor", dict(out=kp, in0=k_, in1=kkn, op=ALU.mult), [B["rk"], bl[7]], [bl[8]])
                    S.op("dve", "scalar_tensor_tensor", dict(out=ba, in0=r_, scalar=col(CV_RK, cc), in1=kp, op0=ALU.mult, op1=ALU.mult),
                         [B["rk"], B["cv"], bl[8]], [bl[9]])
                    yield
                    S.op("pool", "tensor_scalar", dict(out=ld, in0=ld, scalar1=-EM05, scalar2=None, op0=ALU.mult), [bl[0]], [bl[0]])
                    S.op("dve", "tensor_tensor_scan", dict(out=cl, data0=C["resetm"][:], data1=ld, initial=0.0, op0=ALU.mult, op1=ALU.add),
                         [bC, bl[0]], [bl[1]])
                    S.op("act", "activation", dict(out=g, in_=cl, func=AF.Exp), [bl[1]], [bl[2]])
                    S.op("act", "activation", dict(out=gi, in_=cl, func=AF.Exp, scale=-1.0), [bl[1]], [bl[3]])
                    S.op("pool", "tensor_tensor", dict(out=gm1, in0=cl, in1=ld, op=ALU.subtract), [bl[1], bl[0]], [bl[4]])
                    S.op("act", "activation", dict(out=gm1, in_=gm1, func=AF.Exp), [bl[4]], [bl[4]])
                    S.op("pool", "tensor_copy", dict(out=gCt[:, cc, :], in_=g.rearrange("p (n t) -> p n t", t=CH)[:, :, CH - 1]), [bl[2]], [bgC])
                    yield
                    k1 = fbank()
                    S.op("pe", "matmul", dict(out=PB[k1][:, 0:TM], lhsT=C["bones"][:], rhs=tA, start=True, stop=True), [bC, bl[10]], [bPB[k1]])
                    k2 = fbank()
                    S.op("pe", "matmul", dict(out=PB[k2][:, 0:TM], lhsT=C["bones"][:], rhs=ba, start=True, stop=True), [bC, bl[9]], [bPB[k2]])
                    S.op("dve", "tensor_scalar", dict(out=tB, in0=PB[k1][:, 0:TM], scalar1=1e-18, scalar2=None, op0=ALU.max), [bPB[k1]], [bl[11]])
                    S.op("act", "activation", dict(out=tB, in_=tB, func=AF.Ln), [bl[11]], [bl[11]])
                    S.op("act", "activation", dict(out=tB, in_=tB, func=AF.Exp, scale=-0.5), [bl[11]], [bl[11]])
                    S.op("dve", "tensor_tensor", dict(out=bonus[:, cc, :], in0=PB[k2][:, 0:TM], in1=v_, op=ALU.mult), [bPB[k2], B["vall"]], [bbonus])
                    yield
                    S.op("dve", "tensor_tensor", dict(out=kkn, in0=kk, in1=tB, op=ALU.mult), [bl[6], bl[11]], [bl[7]])
                    S.op("pool", "tensor_tensor", dict(out=AR[:, cc, :, 1, :], in0=r_.rearrange("p (n t) -> p n t", t=CH),
                                                       in1=g.rearrange("p (n t) -> p n t", t=CH), op=ALU.mult), [B["rk"], bl[2]], [bAR])
                    S.op("dve", "scalar_tensor_tensor", dict(out=AR[:, cc, :, 0, :], in0=kkn.rearrange("p (n t) -> p n t", t=CH), scalar=-1.0,
                                                              in1=gm1.rearrange("p (n t) -> p n t", t=CH), op0=ALU.mult, op1=ALU.mult),
                         [bl[7], bl[4]], [bAR])
                    S.op("dve", "tensor_tensor", dict(out=ba, in0=kkn, in1=a, op=ALU.mult), [bl[7], bl[5]], [bl[9]])
                    S.op("pool", "tensor_tensor", dict(out=BT[:, cc, :], in0=ba, in1=gi, op=ALU.mult), [bl[9], bl[3]], [bBT])
                    S.op("dve", "tensor_tensor", dict(out=KT[:, cc, :], in0=kp, in1=gi, op=ALU.mult), [bl[8], bl[3]], [bKT])
                    yield
                    for n in range(NCH):
                        S.op("dve", "tensor_scalar", dict(out=tB[:, n * CH:(n + 1) * CH], in0=gi[:, n * CH:(n + 1) * CH],
                                                          scalar1=g[:, n * CH + CH - 1:n * CH + CH], scalar2=None, op0=ALU.mult), [bl[3], bl[2]], [bl[11]])
                    S.op("pool", "tensor_tensor", dict(out=W["bp"][:], in0=ba, in1=tB, op=ALU.mult), [bl[9], bl[11]], [B["bp"]])
                    S.op("dve", "tensor_tensor", dict(out=W["kpp"][:], in0=kp, in1=tB, op=ALU.mult), [bl[8], bl[11]], [B["kpp"]])
                    yield
                    trp = PB[2][:].bitcast(BF16)
                    for hh in range(2):
                        h = cc * 2 + hh
                        pb = hh * 64
                        for n in range(NCH):
                            for oi, (srcT, bsrcT) in enumerate(((W["bp"][:], B["bp"]), (W["kpp"][:], B["kpp"]), (W["vbf"][:, cc, :], B["vbf"]))):
                                o0 = (n * 3 + oi) * 64
                                S.op("pe", "transpose", dict(out=trp[0:64, o0:o0 + 64], in_=srcT[pb:pb + 64, n * CH:(n + 1) * CH],
                                                             identity=C["identb"][pb:pb + 64, pb:pb + 64]), [bsrcT, bC], [bPB[2]])
                        S.op("act", "activation", dict(out=TM3[:, h, :, :, :].rearrange("p n o t -> p (n o t)"),
                                                       in_=trp[0:64, 0:NCH * 192], func=AF.Copy), [bPB[2]], [bTM3])
                        yield
                for j in range(SUBM):
                    gsub = it * SUBM + j
                    sbs = [1] if gsub == 0 else [0, 1]
                    for kv in range(2):
                        for sb in sbs:
                            kcol = (j + sb) * 128
                            K_ = 67 if sb == 0 else 66
                            S.op("pe", "matmul", dict(out=PB[sb][:, :], lhsT=W["kT"][0:K_, kv, kcol:kcol + 128],
                                                      rhs=W["qT"][0:K_, kv * 4:(kv + 1) * 4, j * 128:(j + 1) * 128], start=True, stop=False),
                                 [B["kT"], B["qT"]], [bPB[sb]])
                            S.op("pe", "matmul", dict(out=PB[sb][:, :], lhsT=C["identb"][:], rhs=C["amask"][:, sb, :], start=False, stop=True),
                                 [bC], [bPB[sb]])
                            S.op("act", "activation", dict(out=W["PT"][:, sb, :], in_=PB[sb][:, :], func=AF.Exp), [bPB[sb]], [B["PT"]])
                        yield
                        yield
                        for i, sb in enumerate(sbs):
                            S.op("pe", "matmul", dict(out=PB[2][0:65, :], lhsT=W["Vaug"][:, j + sb, kv, :], rhs=W["PT"][:, sb, :],
                                                      start=(i == 0), stop=(i == len(sbs) - 1)), [B["Vaug"], B["PT"]], [bPB[2]])
                        for i, sb in enumerate(sbs):
                            S.op("pe", "matmul", dict(out=PB[0][0:64, :], lhsT=W["onesb"][:, :], rhs=W["PT"][:, sb, :],
                                                      start=(i == 0), stop=(i == len(sbs) - 1)), [B["onesb"], B["PT"]], [bPB[0]])
                        for gg in range(4):
                            S.op("dve", "tensor_scalar", dict(out=W["OTc"][:, gg * 128:(gg + 1) * 128], in0=PB[0][0:64, gg * 128:(gg + 1) * 128],
                                                              scalar1=W["esk"][:, kv * 4 + gg:kv * 4 + gg + 1], scalar2=None, op0=ALU.add),
                                 [bPB[0], B["esk"]], [B["OTc"]])
                        S.op("act", "activation", dict(out=W["OTc"], in_=W["OTc"], func=AF.Ln), [B["OTc"]], [B["OTc"]])
                        S.op("act", "activation", dict(out=W["OTc"], in_=W["OTc"], func=AF.Exp, scale=-1.0), [B["OTc"]], [B["OTc"]])
                        yield
                        S.op("dve", "tensor_tensor", dict(out=attT[:, kv * 4:(kv + 1) * 4, j * 128:(j + 1) * 128],
                                                          in0=PB[2][0:64, :].rearrange("p (g t) -> p g t", t=128),
                                                          in1=W["OTc"].rearrange("p (g t) -> p g t", t=128), op=ALU.mult),
                             [B["OTc"], bPB[2]], [battT])
                        yield
                S.op("pool", "tensor_copy", dict(out=W["kT"][0:64, :, 0:128], in_=W["kT"][0:64, :, TM:TM + 128]), [B["kT"]], [B["kT"]])
                S.op("pool", "tensor_copy", dict(out=W["Vaug"][:, 0, :, :], in_=W["Vaug"][:, SUBM, :, :]), [B["Vaug"]], [B["Vaug"]])
                yield

            def back(it):
                par = it % 2
                AR, BT, KT, TM3, gCt, bonus, sgd = (W["AR%d" % par], W["BT%d" % par], W["KT%d" % par], W["TM3%d" % par],
                                                    W["gC%d" % par], W["bonus%d" % par], W["sgd%d" % par])
                attT, battT = W["attT%d" % par], B["attT%d" % par]
                bAR, bBT, bKT, bTM3, bgC, bbonus, bsgd = (B["AR%d" % par], B["BT%d" % par], B["KT%d" % par], B["TM3%d" % par],
                                                          B["gC%d" % par], B["bonus%d" % par], B["sgd%d" % par])
                tok0 = it * TM
                bsrc = b_src_fn(it)
                for n in range(NCH):
                    for hp in range(4):
                        for hh in range(2):
                            h = hp * 2 + hh
                            pb = hh * 64
                            bank = ((3, 5), (4, 7))[hh][hp % 2]
                            rhsAR = AR[pb:pb + 64, hp, n, :, :]
                            S.op("pe", "matmul", dict(out=PB[bank][0:64, 0:128], lhsT=BT[pb:pb + 64, hp, n * CH:(n + 1) * CH],
                                                      rhs=rhsAR, start=True, stop=True), [bBT, bAR], [bPB[bank]])
                            S.op("pe", "matmul", dict(out=PB[bank][0:64, 128:256], lhsT=KT[pb:pb + 64, hp, n * CH:(n + 1) * CH],
                                                      rhs=rhsAR, start=True, stop=True), [bKT, bAR], [bPB[bank]])
                            S.op("dve", "tensor_tensor", dict(out=W["Amat"][:, h, :], in0=PB[bank][0:64, 0:256],
                                                              in1=C["maskA"][:, 0:256], op=ALU.mult), [bPB[bank], bC], [B["Amat"]])
                        yield
                    trp = PB[5][:].bitcast(BF16)
                    for h in range(8):
                        S.op("pe", "transpose", dict(out=trp[0:64, h * 64:(h + 1) * 64], in_=W["Amat"][:, h, 0:64], identity=C["identb"][0:64, 0:64]),
                             [B["Amat"], bC], [bPB[5]])
                    S.op("act", "activation", dict(out=W["Nm"][:].rearrange("p h c -> p (h c)"), in_=trp[0:64, 0:512], func=AF.Copy), [bPB[5]], [B["Nm"]])
                    S.op("dve", "tensor_tensor", dict(out=W["Pm"][:].rearrange("p h c -> p (h c)"), in0=W["Nm"][:].rearrange("p h c -> p (h c)"),
                                                      in1=C["identI8"][:], op=ALU.add), [B["Nm"], bC], [B["Pm"]])
                    yield
                    M_, MT_, M2_, MT2_ = "Nm", "NTm", "Nm2", "NTm2"
                    P_, P2_ = "Pm", "Pm2"
                    for lev in range(5):
                        last = (lev == 4)
                        if lev == 0:
                            mt_ap = lambda h: W["Amat"][:, h, 0:64]
                            bmt = B["Amat"]
                        else:
                            mt_ap = lambda h, MT_=MT_: W[MT_][:, h, :]
                            bmt = B[MT_]
                        for h in range(8):
                            S.op("pe", "matmul", dict(out=PB[6][0:64, h * 64:(h + 1) * 64], lhsT=W[M_][:, h, :], rhs=mt_ap(h), start=True, stop=True),
                                 [B[M_], bmt], [bPB[6]])
                        S.op("act", "activation", dict(out=W[MT2_][:].rearrange("p h c -> p (h c)"), in_=PB[6][0:64, :], func=AF.Copy), [bPB[6]], [B[MT2_]])
                        yield
                        if not last:
                            for h in range(8):
                                S.op("pe", "matmul", dict(out=PB[7][0:64, h * 64:(h + 1) * 64], lhsT=mt_ap(h), rhs=W[M_][:, h, :], start=True, stop=True),
                                     [B[M_], bmt], [bPB[7]])
                            S.op("dve", "tensor_copy", dict(out=W[M2_][:].rearrange("p h c -> p (h c)"), in_=PB[7][0:64, :]), [bPB[7]], [B[M2_]])
                            yield
                        for h in range(8):
                            S.op("pe", "matmul", dict(out=PB[5][0:64, h * 64:(h + 1) * 64], lhsT=W[MT2_][:, h, :], rhs=W[P_][:, h, :], start=True, stop=True),
                                 [B[MT2_], B[P_]], [bPB[5]])
                        S.op("dve", "tensor_tensor", dict(out=W[P2_][:].rearrange("p h c -> p (h c)"), in0=PB[5][0:64, :],
                                                          in1=W[P_][:].rearrange("p h c -> p (h c)"), op=ALU.add), [bPB[5], B[P_]], [B[P2_]])
                        yield
                        M_, M2_ = M2_, M_
                        MT_, MT2_ = MT2_, MT_
                        P_, P2_ = P2_, P_
                    trp = PB[6][:].bitcast(BF16)
                    for h in range(8):
                        S.op("pe", "transpose", dict(out=trp[0:64, h * 64:(h + 1) * 64], in_=W[P_][:, h, :], identity=C["identb"][0:64, 0:64]),
                             [B[P_], bC], [bPB[6]])
                    S.op("act", "activation", dict(out=W[P2_][:].rearrange("p h c -> p (h c)"), in_=trp[0:64, 0:512], func=AF.Copy), [bPB[6]], [B[P2_]])
                    TinvT = P2_
                    yield
                    for h in range(8):
                        hp, hh = h // 2, h % 2
                        pb = hh * 64
                        S.op("pe", "matmul", dict(out=PB[3 + hh][0:64, hp * 64:(hp + 1) * 64], lhsT=AR[pb:pb + 64, hp, n, 0, :], rhs=W["Hb"][pb:pb + 64, hp, :],
                                                  start=True, stop=True), [bAR, B["Hb"]], [bPB[3 + hh]])
                        S.op("pe", "matmul", dict(out=PB[7][0:64, h * 64:(h + 1) * 64], lhsT=W["Amat"][:, h, 128:192], rhs=TM3[:, h, n, 2, :],
                                                  start=True, stop=True), [B["Amat"], bTM3], [bPB[7]])
                    S.op("act", "activation", dict(out=W["Z2b"], in_=PB[7][0:64, :], func=AF.Copy), [bPB[7]], [B["Z2b"]])
                    z2 = W["Z2b"].rearrange("p (q e c) -> p q e c", e=2, c=64)
                    zb4 = W["Zb"][:].rearrange("p (q e) c -> p q e c", e=2)
                    for hh in range(2):
                        S.op("dve", "tensor_tensor", dict(out=zb4[:, :, hh, :], in0=PB[3 + hh][0:64, 0:256].rearrange("p (q c) -> p q c", c=64),
                                                          in1=z2[:, :, hh, :], op=ALU.add), [bPB[3 + hh], B["Z2b"]], [B["Zb"]])
                    yield
                    for h in range(8):
                        S.op("pe", "matmul", dict(out=PB[6][0:64, h * 64:(h + 1) * 64], lhsT=W[TinvT][:, h, :], rhs=W["Zb"][:, h, :], start=True, stop=True),
                             [B[TinvT], B["Zb"]], [bPB[6]])
                    S.op("act", "activation", dict(out=W["Ub"][:].rearrange("p h c -> p (h c)"), in_=PB[6][0:64, :], func=AF.Copy), [bPB[6]], [B["Ub"]])
                    yield
                    for h in range(8):
                        hp, hh = h // 2, h % 2
                        pb = hh * 64
                        S.op("pe", "matmul", dict(out=PB[3 + hh][pb:pb + 64, 256 + hp * 64:256 + (hp + 1) * 64], lhsT=W["Hb"][pb:pb + 64, hp, :],
                                                  rhs=AR[pb:pb + 64, hp, n, 1, :], start=True, stop=True), [B["Hb"], bAR], [bPB[3 + hh]])
                        yo = PB[5][pb:pb + 64, hp * 64:(hp + 1) * 64]
                        S.op("pe", "matmul", dict(out=yo, lhsT=W["Ub"][:, h, :], rhs=W["Amat"][:, h, 64:128], start=True, stop=False),
                             [B["Ub"], B["Amat"]], [bPB[5]])
                        S.op("pe", "matmul", dict(out=yo, lhsT=TM3[:, h, n, 2, :], rhs=W["Amat"][:, h, 192:256], start=False, stop=True),
                             [bTM3, B["Amat"]], [bPB[5]])
                        ho = PB[5][pb:pb + 64, 256 + hp * 64:256 + (hp + 1) * 64]
                        S.op("pe", "matmul", dict(out=ho, lhsT=TM3[:, h, n, 0, :], rhs=W["Ub"][:, h, :], start=True, stop=False),
                             [bTM3, B["Ub"]], [bPB[5]])
                        S.op("pe", "matmul", dict(out=ho, lhsT=TM3[:, h, n, 1, :], rhs=TM3[:, h, n, 2, :], start=False, stop=True),
                             [bTM3], [bPB[5]])
                        if h % 2 == 1:
                            yield
                    S.op("act", "activation", dict(out=W["yT"][:, :, n * CH:(n + 1) * CH], in_=PB[5][:, 0:256].rearrange("p (c t) -> p c t", t=64), func=AF.Copy),
                         [bPB[5]], byT)
                    for hh in range(2):
                        pb = hh * 64
                        S.op("dve", "tensor_tensor", dict(out=W["yT"][pb:pb + 64, :, n * CH:(n + 1) * CH],
                                                          in0=PB[3 + hh][pb:pb + 64, 256:512].rearrange("p (c t) -> p c t", t=64),
                                                          in1=W["yT"][pb:pb + 64, :, n * CH:(n + 1) * CH], op=ALU.add), [bPB[3 + hh]] + byT, byT)
                    for hp in range(4):
                        S.op("dve", "scalar_tensor_tensor", dict(out=W["H"][:, hp, :], in0=W["H"][:, hp, :], scalar=gCt[:, hp, n:n + 1],
                                                                  in1=PB[5][:, 256 + hp * 64:256 + (hp + 1) * 64], op0=ALU.mult, op1=ALU.add),
                             [B["H"], bgC, bPB[5]], [B["H"]])
                    S.op("pool", "tensor_copy", dict(out=W["Hb"][:], in_=W["H"][:]), [B["H"]], [B["Hb"]])
                    yield
                def gate_mm(cc):
                    S.op("pe", "matmul", dict(out=PB[5][:, (cc % 2) * 256:(cc % 2) * 256 + TM], lhsT=W["g2b"][:, cc * 128:(cc + 1) * 128], rhs=sgd[:],
                                              start=True, stop=True), [B["g2b"], bsgd], [bPB[5]])
                yield
                for c0 in (0, 2):
                    ccs = (c0, c0 + 1)
                    m1 = {c0: 6, c0 + 1: 3}
                    m2 = {c0: 7, c0 + 1: 4}
                    tmp = {c0: W["pp0"][:], c0 + 1: W["pp1"][:]}
                    btmp = {c0: B["pp0"], c0 + 1: B["pp1"]}
                    for cc in ccs:
                        S.op("pe", "matmul", dict(out=PB[m1[cc]][:, 0:TM], lhsT=C["bones"][:], rhs=W["yT"][:, cc, :], start=True, stop=True),
                             [bC, byT[cc]], [bPB[m1[cc]]])
                    if c0 == 0:
                        gate_mm(0)
                        gate_mm(1)
                    yield
                    for cc in ccs:
                        S.op("dve", "scalar_tensor_tensor", dict(out=W["yT"][:, cc, :], in0=PB[m1[cc]][:, 0:TM], scalar=-1.0 / 64, in1=W["yT"][:, cc, :],
                                                                  op0=ALU.mult, op1=ALU.add), [bPB[m1[cc]], byT[cc]], [byT[cc]])
                        S.op("act", "activation", dict(out=tmp[cc], in_=W["yT"][:, cc, :], func=AF.Square), [byT[cc]], [btmp[cc]])
                    yield
                    yield
                    for cc in ccs:
                        S.op("pe", "matmul", dict(out=PB[m2[cc]][:, 0:TM], lhsT=C["bones"][:], rhs=tmp[cc], start=True, stop=True), [bC, btmp[cc]], [bPB[m2[cc]]])
                    yield
                    for cc in ccs:
                        S.op("dve", "tensor_scalar", dict(out=tmp[cc], in0=PB[m2[cc]][:, 0:TM], scalar1=1.0 / 64, scalar2=LNX_EPS, op0=ALU.mult, op1=ALU.add),
                             [bPB[m2[cc]]], [btmp[cc]])
                        S.op("act", "activation", dict(out=tmp[cc], in_=tmp[cc], func=AF.Ln), [btmp[cc]], [btmp[cc]])
                        S.op("act", "activation", dict(out=tmp[cc], in_=tmp[cc], func=AF.Exp, scale=-0.5), [btmp[cc]], [btmp[cc]])
                    yield
                    for cc in ccs:
                        yn = W["yT"][:, cc, :]
                        S.op("dve", "tensor_tensor", dict(out=yn, in0=yn, in1=tmp[cc], op=ALU.mult), [byT[cc], btmp[cc]], [byT[cc]])
                        S.op("dve", "tensor_scalar", dict(out=yn, in0=yn, scalar1=col(CV_LG, cc), scalar2=col(CV_LB, cc), op0=ALU.mult, op1=ALU.add),
                             [byT[cc], B["cv"]], [byT[cc]])
                        S.op("pool", "tensor_tensor", dict(out=yn, in0=yn, in1=bonus[:, cc, :], op=ALU.add), [byT[cc], bbonus], [byT[cc]])
                    yield
                    for cc in ccs:
                        S.op("dve", "tensor_tensor", dict(out=W["rwT"][:, cc, :], in0=PB[5][:, (cc % 2) * 256:(cc % 2) * 256 + TM], in1=W["yT"][:, cc, :], op=ALU.mult),
                             [bPB[5], byT[cc]], [B["rwT"]])
                    if c0 == 0:
                        gate_mm(2)
                        gate_mm(3)
                    yield
                if dbg and "rw" in dbg_out and it == dbg.get("_dbgtile", 0):
                    S.op("pool", "tensor_copy", dict(out=W["yT"][:], in_=W["rwT"][:]), [B["rwT"]], byT)
                    S.dma(dbg_out["rw"], W["yT"][:], reads=byT, writes=[b_out])
                    S.op("pool", "tensor_copy", dict(out=W["yT"][0:64, :, :].rearrange("p a b -> p (a b)"), in_=attT[:, 0:4, :].rearrange("p a b -> p (a b)")),
                         [battT], byT)
                    S.dma(dbg_out["att"], W["yT"][0:64, :, :], reads=byT, writes=[b_out])
                kk_ = 0
                for j in range(SUBM):
                    for nh in range(2):
                        xo = W["xos%d" % (kk_ % 2)]
                        bxo = B["xos%d" % (kk_ % 2)]
                        kk_ += 1
                        r0 = tok0 + j * 128
                        S.dma(xo[:], src[r0:r0 + 128, nh * 512:(nh + 1) * 512], reads=[bsrc], writes=[bxo])
                        k = bbank()
                        for h in range(8):
                            S.op("pe", "matmul", dict(out=PB[k][:, :], lhsT=attT[:, h, j * 128:(j + 1) * 128], rhs=W["WoA"][:, h, nh * 512:(nh + 1) * 512],
                                                      start=(h == 0), stop=False), [battT, B["WoA"]], [bPB[k]])
                        for cc in range(4):
                            S.op("pe", "matmul", dict(out=PB[k][:, :], lhsT=W["rwT"][:, cc, j * 128:(j + 1) * 128], rhs=W["WoR"][:, cc, nh * 512:(nh + 1) * 512],
                                                      start=False, stop=(cc == 3)), [B["rwT"], B["WoR"]], [bPB[k]])
                        S.op("dve", "tensor_tensor", dict(out=xo[:], in0=PB[k][:, :], in1=xo[:], op=ALU.add), [bPB[k], bxo], [bxo])
                        yield
                        S.dma(xs[r0:r0 + 128, nh * 512:(nh + 1) * 512], xo[:], reads=[bxo], writes=[b_xs[it]], eng="sp")

            ntm = dbg.get("_ntm", NTM) if dbg else NTM
            drive([front(0)])
            for it in range(ntm):
                drive([back(it), front(it + 1) if it + 1 < ntm else None], weights=[int(_os.environ.get("MK_RB", "2")), int(_os.environ.get("MK_RF", "1"))])
        S.barrier()

    def ffn(l, last):
        with ExitStack() as st:
            W = tiles(st, [
                ("xt0", [128, SUBF, D], F32), ("xt1", [128, SUBF, D], F32), ("ss", [128, SUBF], F32), ("rstd", [128, SUBF], F32),
                ("ss2", [128, SUBF], F32), ("rstd2", [128, SUBF], F32),
                ("xnb", [128, D], BF16), ("xnb1", [128, D], BF16), ("xT0", [128, 8, TF], BF16), ("xT1", [128, 8, TF], BF16),
                ("wgu0", [128, 8, 256], BF16), ("wgu1", [128, 8, 256], BF16), ("wgu2", [128, 8, 256], BF16), ("Wd", [128, NFC, D], BF16),
                ("gc", [128, 2 + TF], F32), ("gcar", [128, NFC, 2], F32),
                ("c1", [128, TF], F32), ("c2", [128, TF], F32), ("c3", [128, TF], F32), ("sl", [128, TF], F32),
                ("hT", [128, NFC, TF], BF16), ("xo", [128, SUBF, D], F32), ("cv", [128, NCV], F32), ("fingb", [128, D], F32),
            ])
            B = {k: Buf(k) for k in W}
            bw = b_wscr[l]
            cv = W["cv"]
            S.dma(cv[:], cvL[l], writes=[B["cv"]])
            S.dma(W["fingb"][:], fing.broadcast_to([128, D]), writes=[B["fingb"]])
            S.op("pool", "memset", dict(ap=W["gcar"][:], constant=0.0), [], [B["gcar"]])
            bWd = [Buf("Wd%d" % i) for i in range(NFC)]

            def head_dma(it):
                par = it % 2
                tok0 = it * TF
                xt, bxt = W["xt%d" % par], B["xt%d" % par]
                S.dma(xt[:], xs[tok0:tok0 + TF, :].rearrange("(j p) d -> p j d", p=128), reads=[b_xs[2 * it], b_xs[2 * it + 1]], writes=[bxt])

            def head_load(it):
                par = it % 2
                xt, bxt = W["xt%d" % par], B["xt%d" % par]
                for j in range(SUBF):
                    S.op("act", "activation", dict(out=W["xnb"][:], in_=xt[:, j, :], func=AF.Square, accum_out=W["ss"][:, j:j + 1]),
                         [bxt], [B["xnb"], B["ss"]])
                S.op("dve", "tensor_scalar", dict(out=W["rstd"][:], in0=W["ss"][:], scalar1=1.0 / D, scalar2=RMS_EPS, op0=ALU.mult, op1=ALU.add),
                     [B["ss"]], [B["rstd"]])
                S.op("act", "activation", dict(out=W["rstd"][:], in_=W["rstd"][:], func=AF.Sqrt), [B["rstd"]], [B["rstd"]])
                S.op("dve", "reciprocal", dict(out=W["rstd"][:], in_=W["rstd"][:]), [B["rstd"]], [B["rstd"]])

            def head_sub_a(it, j):
                par = it % 2
                xt, bxt = W["xt%d" % par], B["xt%d" % par]
                xn = "xnb" if j % 2 == 0 else "xnb1"
                S.op("dve", "tensor_scalar", dict(out=W[xn][:], in0=xt[:, j, :], scalar1=W["rstd"][:, j:j + 1], scalar2=None, op0=ALU.mult),
                     [bxt, B["rstd"]], [B[xn]])

            def head_sub_b(it, j):
                par = it % 2
                xT, bxT = W["xT%d" % par], B["xT%d" % par]
                pT = PB[4][:].bitcast(BF16)
                xn = "xnb" if j % 2 == 0 else "xnb1"
                for c in range(8):
                    S.op("pe", "transpose", dict(out=pT[:, c * 128:(c + 1) * 128], in_=W[xn][:, c * 128:(c + 1) * 128], identity=C["identb"][:]),
                         [B[xn], bC], [bPB[4]])
                S.op("act", "activation", dict(out=xT[:, :, j * 128:(j + 1) * 128], in_=pT[:, 0:1024].rearrange("p (c t) -> p c t", t=128),
                                               func=AF.Copy), [bPB[4]], [bxT])

            def ffn_gen():
                head_dma(0)
                head_load(0)
                for j in range(SUBF):
                    head_sub_a(0, j)
                    head_sub_b(0, j)
                    yield
                for it in range(NTF):
                    tok0 = it * TF
                    par = it % 2
                    xt, bxt = W["xt%d" % par], B["xt%d" % par]
                    xT, bxT = W["xT%d" % par], B["xT%d" % par]
                    for fc in range(NFC):
                        wk = "wgu%d" % (fc % 3)
                        S.dma(W[wk][:], wguB[l][fc], reads=bw["wgu"], writes=[B[wk]])
                        if fc < 11:
                            for f2 in (2 * fc, 2 * fc + 1):
                                S.dma(W["Wd"][:, f2, :], wdB[l][:, f2 * D:(f2 + 1) * D], reads=bw["wd"], writes=[bWd[f2]])
                        if it + 1 < NTF:
                            if fc == 1:
                                head_dma(it + 1)
                            if fc == 9:
                                head_load(it + 1)
                            if fc in (10, 11):
                                head_sub_a(it + 1, fc - 10)
                            if fc in (14, 16):
                                head_sub_a(it + 1, 2 + (fc - 14) // 2)
                            if fc in (13, 15, 17, 19):
                                head_sub_b(it + 1, (fc - 13) // 2)
                        pg = (fc % 2) * 2
                        pu = pg + 1
                        for c in range(8):
                            S.op("pe", "matmul", dict(out=PB[pg][:, :], lhsT=W[wk][:, c, 0:128], rhs=xT[:, c, :], start=(c == 0), stop=(c == 7)),
                                 [B[wk], bxT], [bPB[pg]])
                        for c in range(8):
                            S.op("pe", "matmul", dict(out=PB[pu][:, :], lhsT=W[wk][:, c, 128:256], rhs=xT[:, c, :], start=(c == 0), stop=(c == 7)),
                                 [B[wk], bxT], [bPB[pu]])
                        gc = W["gc"]
                        S.op("act", "activation", dict(out=gc[:, 2:2 + TF], in_=PB[pg][:, :], func=AF.Copy), [bPB[pg]], [B["gc"]])
                        S.op("pool", "tensor_copy", dict(out=gc[:, 0:2], in_=W["gcar"][:, fc, :]), [B["gcar"]], [B["gc"]])
                        S.op("pool", "tensor_scalar", dict(out=W["c1"][:], in0=gc[:, 0:TF], scalar1=cv[:, CV_CW + fc:CV_CW + fc + 1],
                                                           scalar2=cv[:, CV_CB + fc:CV_CB + fc + 1], op0=ALU.mult, op1=ALU.add), [B["gc"], B["cv"]], [B["c1"]])
                        S.op("dve", "scalar_tensor_tensor", dict(out=W["c2"][:], in0=gc[:, 1:1 + TF], scalar=cv[:, CV_CW + NFC + fc:CV_CW + NFC + fc + 1],
                                                                  in1=W["c1"][:], op0=ALU.mult, op1=ALU.add), [B["gc"], B["cv"], B["c1"]], [B["c2"]])
                        S.op("dve", "scalar_tensor_tensor", dict(out=W["c3"][:], in0=gc[:, 2:2 + TF], scalar=cv[:, CV_CW + 2 * NFC + fc:CV_CW + 2 * NFC + fc + 1],
                                                                  in1=W["c2"][:], op0=ALU.mult, op1=ALU.add), [B["gc"], B["cv"], B["c2"]], [B["c3"]])
                        S.op("pool", "tensor_copy", dict(out=W["gcar"][:, fc, :], in_=gc[:, TF:TF + 2]), [B["gc"]], [B["gcar"]])
                        S.op("act", "activation", dict(out=W["sl"][:], in_=W["c3"][:], func=AF.Silu), [B["c3"]], [B["sl"]])
                        S.op("dve", "tensor_tensor", dict(out=W["hT"][:, fc, :], in0=PB[pu][:, :], in1=W["sl"][:], op=ALU.mult), [bPB[pu], B["sl"]], [B["hT"]])
                        yield
                    kk = 0
                    for j in range(SUBF):
                        for nh in range(2):
                            k = 5 + (kk % 3)
                            kk += 1
                            for fc in range(NFC):
                                S.op("pe", "matmul", dict(out=PB[k][:, :], lhsT=W["hT"][:, fc, j * 128:(j + 1) * 128], rhs=W["Wd"][:, fc, nh * 512:(nh + 1) * 512],
                                                          start=(fc == 0), stop=(fc == NFC - 1)), [B["hT"], bWd[fc]], [bPB[k]])
                            S.op("dve", "tensor_tensor", dict(out=W["xo"][:, j, nh * 512:(nh + 1) * 512], in0=PB[k][:, :], in1=xt[:, j, nh * 512:(nh + 1) * 512],
                                                              op=ALU.add), [bPB[k], bxt], [B["xo"]])
                            yield
                    if not last:
                        S.dma(xs[tok0:tok0 + TF, :].rearrange("(j p) d -> p j d", p=128), W["xo"][:], reads=[B["xo"]], writes=[b_xs[2 * it], b_xs[2 * it + 1]], eng="sp")
                    else:
                        for j in range(SUBF):
                            S.op("act", "activation", dict(out=W["xnb"][:], in_=W["xo"][:, j, :], func=AF.Square, accum_out=W["ss2"][:, j:j + 1]),
                                 [B["xo"]], [B["xnb"], B["ss2"]])
                        S.op("dve", "tensor_scalar", dict(out=W["rstd2"][:], in0=W["ss2"][:], scalar1=1.0 / D, scalar2=RMS_EPS, op0=ALU.mult, op1=ALU.add),
                             [B["ss2"]], [B["rstd2"]])
                        S.op("act", "activation", dict(out=W["rstd2"][:], in_=W["rstd2"][:], func=AF.Sqrt), [B["rstd2"]], [B["rstd2"]])
                        S.op("dve", "reciprocal", dict(out=W["rstd2"][:], in_=W["rstd2"][:]), [B["rstd2"]], [B["rstd2"]])
                        for j in range(SUBF):
                            S.op("dve", "scalar_tensor_tensor", dict(out=W["xo"][:, j, :], in0=W["xo"][:, j, :], scalar=W["rstd2"][:, j:j + 1], in1=W["fingb"][:],
                                                                      op0=ALU.mult, op1=ALU.mult), [B["xo"], B["rstd2"], B["fingb"]], [B["xo"]])
                        S.dma(out_d[tok0:tok0 + TF, :].rearrange("(j p) d -> p j d", p=128), W["xo"][:], reads=[B["xo"]], writes=[b_out], eng="sp")

            if l + 1 < L:
                Wp = tiles(st, pre_tiles(2))
                drive([ffn_gen(), prepass_gen(l + 1, Wp, "sp", True, load_eng="act", engs=("dve",), NSTG=2)], weights=[6, 1])
            else:
                drive([ffn_gen()])
        S.barrier()

    nlayers = dbg.get("_layers", L) if dbg else L
    stop_after = dbg.get("_stop", None) if dbg else None
    for l in range(nlayers):
        if l == 0:
            prepass(l)
        if stop_after == ("prepass", l):
            break
        if l == 0:
            mixer(l, x_in, lambda it: b_x)
        else:
            mixer(l, xs, lambda it: b_xs[it])
        if stop_after == ("mixer", l):
            break
        ffn(l, last=(l == L - 1))
    S.finish()
    cst.close()
    return nc, S


def _colmajor(v):
    n = v.shape[0] // 128
    return np.ascontiguousarray(v.reshape(n, 128).T)


def host_layout(inp):
    f = np.float32
    d = {}
    w_in = inp["w_in"]
    d["winL"] = np.ascontiguousarray(w_in.reshape(L, 8, 128, NIN).transpose(0, 2, 1, 3))
    w_out = inp["w_out"]
    d["woAL"] = np.ascontiguousarray(w_out[:, :512].reshape(L, 8, 64, D).transpose(0, 2, 1, 3))
    d["woRL"] = np.ascontiguousarray(w_out[:, 512:].reshape(L, 4, 128, D).transpose(0, 2, 1, 3))
    wg = inp["ffn_w_gate"].reshape(L, 8, 128, NFC, 128)
    wu = inp["ffn_w_up"].reshape(L, 8, 128, NFC, 128)
    wgu = np.stack([wg, wu], axis=4)
    d["wguL"] = np.ascontiguousarray(wgu.reshape(L, 8, 128, 2, 11 * 256))
    d["wdL"] = np.ascontiguousarray(inp["ffn_w_down"].reshape(L, NFC, 128, D).transpose(0, 2, 1, 3).reshape(L, 128, NFC * D))
    d["w2a2L"] = np.ascontiguousarray(np.concatenate([inp["w2"], inp["a2"]], axis=1))
    d["g2L"] = np.ascontiguousarray(inp["g2"])
    d["v1L"] = np.ascontiguousarray(inp["v1"][0].reshape(4, 128, 32).transpose(1, 0, 2))
    d["v2L"] = np.ascontiguousarray(inp["v2"][0])
    cv = np.zeros((L, 128, NCV), f)
    for l in range(L):
        cv[l, :, CV_G1:CV_G1 + 8] = _colmajor(inp["norm1_g"][l])
        cv[l, :, CV_G2:CV_G2 + 8] = _colmajor(inp["norm2_g"][l])
        cv[l, :, CV_MU:CV_MU + 14] = _colmajor(inp["shift_mu"][l])
        for nm, o in (("w0", CV_W0), ("a0", CV_A0), ("k_k", CV_KK), ("k_a", CV_KA), ("r_k", CV_RK), ("lnx_g", CV_LG), ("lnx_b", CV_LB)):
            cv[l, :, o:o + 4] = _colmajor(inp[nm][l])
        if l > 0:
            cv[l, :, CV_V0:CV_V0 + 4] = _colmajor(inp["v0"][l - 1])
        for j in range(3):
            cv[l, :, CV_CW + j * NFC:CV_CW + (j + 1) * NFC] = _colmajor(inp["conv_w"][l, j])
        cv[l, :, CV_CB:CV_CB + NFC] = _colmajor(inp["conv_b"][l])
    d["cvL"] = cv
    d["sinkL"] = np.ascontiguousarray(inp["attn_sinks"].reshape(L, 1, 8))
    d["fing"] = np.ascontiguousarray(inp["final_g"].reshape(1, D))
    d["c_ident"] = np.eye(128, dtype=f)
    s = np.arange(64)[:, None]
    t = np.arange(64)[None, :]
    strict = (t > s).astype(f)
    incl = (t >= s).astype(f)
    m256 = np.concatenate([strict, incl, strict, incl], axis=1)
    d["c_maskA"] = np.ascontiguousarray(np.concatenate([m256, m256], axis=1))
    rm = np.ones((128, TM), f)
    rm[:, ::CH] = 0.0
    d["c_reset"] = rm
    d["c_identI8"] = np.ascontiguousarray(np.tile(np.eye(64, dtype=f), (1, 8)))
    bo = np.zeros((128, 128), f)
    bo[:64, :64] = 1.0
    bo[64:, 64:] = 1.0
    d["c_bones"] = bo
    s = np.arange(128)[:, None]
    t = np.arange(128)[None, :]
    NEG = -30000.0
    m_prev = np.where(s > t, 0.0, NEG).astype(f)
    m_cur = np.where(s <= t, 0.0, NEG).astype(f)
    am = np.stack([np.tile(m_prev, (1, 4)), np.tile(m_cur, (1, 4))], axis=1)
    d["c_amask"] = np.ascontiguousarray(am)
    slopes = (2.0 ** (-8.0 * np.arange(1, 9) / 8)).astype(f)
    tt = (np.arange(TM) % 128).astype(f)
    qa = np.zeros((3, 8, TM), f)
    qa[0] = slopes[:, None]
    qa[1] = -slopes[:, None] * tt[None, :]
    qa[2] = -128.0 * slopes[:, None]
    d["c_qaug"] = qa
    ka = np.zeros((3, 2, 128 + TM), f)
    ka[0] = (np.arange(128 + TM) % 128).astype(f)[None, :]
    ka[1] = 1.0
    ka[2] = 1.0
    d["c_kaug"] = ka
    return d


_CACHE = {}


def kernel(**inputs):
    inp = {k: np.asarray(v, dtype=np.float32) for k, v in inputs.items()}
    shared = host_layout(inp)
    if "nc" not in _CACHE:
        _CACHE["nc"] = build_program()[0]
    nc = _CACHE["nc"]
    x = inp["x"]
    in_maps = []
    for b in range(8):
        m = dict(shared)
        m["x"] = np.ascontiguousarray(x[b])
        in_maps.append(m)
    res = run_bass_kernel_spmd(nc, in_maps, core_ids=list(range(8)))
    out = np.stack([np.asarray(res.results[b]["out"], dtype=np.float32) for b in range(8)], axis=0)
    return out
```

```python
import os as _os
import numpy as np
from contextlib import ExitStack
import concourse.bass as bass
import concourse.mybir as mybir
from concourse.bass_utils import run_bass_kernel_spmd
from concourse.alu_op_type import AluOpType as ALU

AF = mybir.ActivationFunctionType
F32 = mybir.dt.float32
BF16 = mybir.dt.bfloat16

T = 4096
D = 1024
L = 2
NIN = 2560
FF = 2816
NFC = 22
TM = 256
NTM = T // TM
SUBM = TM // 128
CH = 64
NCH = TM // CH
TF = 512
NTF = T // TF
SUBF = TF // 128
NCV = 150
CV_G1, CV_G2, CV_MU, CV_W0, CV_A0, CV_KK, CV_KA, CV_RK, CV_LG, CV_LB, CV_V0, CV_CW, CV_CB = \
    0, 8, 16, 30, 34, 38, 42, 46, 50, 54, 58, 62, 128
RMS_EPS = 1e-5
LNX_EPS = 64e-5
EM05 = float(np.exp(-0.5))


class Buf:
    __slots__ = ("name", "w", "r")

    def __init__(self, name):
        self.name = name
        self.w = None
        self.r = {}


class Sched:
    NDMA = 32

    def __init__(self, nc):
        self.nc = nc
        self.names = ["pe", "dve", "act", "pool", "sp"]
        self.engobj = {"pe": nc.tensor, "dve": nc.vector, "act": nc.scalar, "pool": nc.gpsimd, "sp": nc.sync}
        self.count = {e: 0 for e in self.names}
        self.waited = {e: {} for e in self.names}
        self.sem_by_id = {}
        for e in self.names:
            self.sem_by_id[("e", e)] = nc.alloc_semaphore("es_" + e)
        for i in range(self.NDMA):
            self.sem_by_id[("d", i)] = nc.alloc_semaphore("ds_%d" % i)
        self.duse = [0] * self.NDMA
        self.drr = 0
        self.ninst = 0

    def _deps(self, eng, reads, writes):
        deps = []
        for b in reads:
            if b.w is not None:
                deps.append(b.w)
        for b in writes:
            if b.w is not None:
                deps.append(b.w)
            deps.extend(b.r.values())
        waits = []
        for (sid, val, src) in deps:
            if src == "pe" and eng == "pe":
                continue
            if self.waited[eng].get(sid, 0) >= val:
                continue
            self.waited[eng][sid] = val
            waits.append((sid, val))
        return waits

    def _commit(self, eng, ev, reads, writes):
        key = eng if ev[2] != "dma" else ev[0]
        for b in reads:
            b.r[key] = ev
        for b in writes:
            b.w = ev
            b.r = {}

    def _emit(self, name, waits, meth, kw, inc):
        eng = self.engobj[name]
        for (sid, val) in waits:
            eng.wait_ge(self.sem_by_id[sid], val)
            self.ninst += 1
        if meth is not None:
            ins = getattr(eng, meth)(**kw)
            ins.then_inc(self.sem_by_id[inc[0]], inc[1])
            self.ninst += 1

    def op(self, eng, meth, kw, reads=(), writes=()):
        waits = self._deps(eng, reads, writes)
        self.count[eng] += 1
        ev = (("e", eng), self.count[eng], eng)
        self._emit(eng, waits, meth, kw, (("e", eng), 1))
        self._commit(eng, ev, reads, writes)

    def dma(self, out, in_, reads=(), writes=(), eng="sp"):
        waits = self._deps(eng, reads, writes)
        j = self.drr
        self.drr = (j + 1) % self.NDMA
        sid = ("d", j)
        if self.duse[j] > 0:
            val = 16 * self.duse[j]
            if self.waited[eng].get(sid, 0) < val:
                self.waited[eng][sid] = val
                waits.append((sid, val))
        self.duse[j] += 1
        ev = (sid, 16 * self.duse[j], "dma")
        self._emit(eng, waits, "dma_start", dict(out=out, in_=in_), (sid, 16))
        self._commit(eng, ev, reads, writes)

    def barrier(self):
        evs = []
        for j in range(self.NDMA):
            if self.duse[j] > 0:
                evs.append((("d", j), 16 * self.duse[j]))
        for e in self.names:
            if self.count[e] > 0:
                evs.append((("e", e), self.count[e]))
        for e in self.names:
            w = []
            for (sid, val) in evs:
                if self.waited[e].get(sid, 0) < val:
                    self.waited[e][sid] = val
                    w.append((sid, val))
            self._emit(e, w, None, None, None)

    def finish(self):
        self.barrier()


def build_program(dbg=None):
    nc = bass.Bass("TRN2", target_bir_lowering=False)
    S = Sched(nc)

    def din(name, shape, dt=F32):
        return nc.dram_tensor(name, list(shape), dt, kind="ExternalInput").ap()

    x_in = din("x", [T, D])
    winL = din("winL", [L, 128, 8, NIN])
    woAL = din("woAL", [L, 64, 8, D])
    woRL = din("woRL", [L, 128, 4, D])
    wguL = din("wguL", [L, 8, 128, 2, 11 * 256])
    wdL = din("wdL", [L, 128, NFC * D])
    w2a2L = din("w2a2L", [L, 128, 512])
    g2L = din("g2L", [L, 128, 512])
    v1L = din("v1L", [128, 4, 32])
    v2L = din("v2L", [32, 512])
    cvL = din("cvL", [L, 128, NCV])
    sinkL = din("sinkL", [L, 1, 8])
    fing = din("fing", [1, D])
    c_ident = din("c_ident", [128, 128])
    c_maskA = din("c_maskA", [64, 512])
    c_reset = din("c_reset", [128, TM])
    c_identI8 = din("c_identI8", [64, 512])
    c_bones = din("c_bones", [128, 128])
    c_amask = din("c_amask", [128, 2, 512])
    c_qaug = din("c_qaug", [3, 8, TM])
    c_kaug = din("c_kaug", [3, 2, 128 + TM])
    out_d = nc.dram_tensor("out", [T, D], F32, kind="ExternalOutput").ap()
    xs = nc.dram_tensor("xs", [T, D], F32, kind="Internal").ap()
    vf = nc.dram_tensor("vf", [128, 4, T], F32, kind="Internal").ap()
    winB = [nc.dram_tensor("winB%d" % l, [128, 8, NIN], BF16, kind="Internal").ap() for l in range(L)]
    woAB = [nc.dram_tensor("woAB%d" % l, [64, 8 * D], BF16, kind="Internal").ap() for l in range(L)]
    woRB = [nc.dram_tensor("woRB%d" % l, [128, 4 * D], BF16, kind="Internal").ap() for l in range(L)]
    wguB = [nc.dram_tensor("wguB%d" % l, [NFC, 128, 8, 256], BF16, kind="Internal").ap() for l in range(L)]
    wdB = [nc.dram_tensor("wdB%d" % l, [128, NFC * D], BF16, kind="Internal").ap() for l in range(L)]
    dbg_out = {}
    if dbg:
        for nm, shp in dbg.items():
            if nm.startswith("_"):
                continue
            dbg_out[nm] = nc.dram_tensor("dbg_" + nm, list(shp), F32, kind="ExternalOutput").ap()

    b_x = Buf("x")
    b_xs = [Buf("xs%d" % i) for i in range(NTM)]
    b_out = Buf("out")
    b_vf = [Buf("vf%d" % i) for i in range(NTM)]
    b_wscr = {}

    PB = [nc.alloc_psum_tensor("pb%d" % i, [128, 512], F32) for i in range(8)]
    bPB = [Buf("pb%d" % i) for i in range(8)]

    uid = {"i": 0}

    def tiles(st, specs):
        res = {}
        uid["i"] += 1
        for nm, shp, dt in specs:
            res[nm] = st.enter_context(nc.sbuf_tensor("%s_%d" % (nm, uid["i"]), list(shp), dt))
        if dbg and dbg.get("_mem"):
            print("SBUF after tiles group", uid["i"], nc.bytes_allocated(res[nm].space), "of", nc.space_capacity(res[nm].space))
        return res

    rr = {"i": 0}

    def ew():
        rr["i"] += 1
        return ("dve", "pool")[rr["i"] % 2]

    cst = ExitStack()
    C = tiles(cst, [
        ("identf", [128, 128], F32), ("identb", [128, 128], BF16), ("bones", [128, 128], F32),
        ("maskA", [64, 256], F32), ("resetm", [128, TM], F32), ("identI8", [64, 512], BF16),
        ("amask", [128, 2, 512], BF16),
    ])
    bC = Buf("consts")
    S.dma(C["identf"][:], c_ident, writes=[bC])
    S.dma(C["bones"][:], c_bones, writes=[bC])
    S.dma(C["maskA"][:], c_maskA[:, 0:256], writes=[bC])
    S.dma(C["resetm"][:], c_reset, writes=[bC])
    with ExitStack() as st0:
        Cs = tiles(st0, [("i8f", [64, 512], F32), ("amf", [128, 2, 512], F32)])
        bCs = Buf("cstage")
        S.dma(Cs["i8f"][:], c_identI8, writes=[bCs])
        S.dma(Cs["amf"][:], c_amask, writes=[bCs])
        S.op("dve", "tensor_copy", dict(out=C["identI8"][:], in_=Cs["i8f"][:]), [bCs], [bC])
        S.op("dve", "tensor_copy", dict(out=C["amask"][:], in_=Cs["amf"][:]), [bCs], [bC])
        S.barrier()
    S.op("dve", "tensor_copy", dict(out=C["identb"][:], in_=C["identf"][:]), [bC], [bC])

    def drive(gens, weights=None):
        items = [[g, (weights[i] if weights else 1)] for i, g in enumerate(gens) if g is not None]
        while items:
            for itm in list(items):
                for _ in range(itm[1]):
                    try:
                        next(itm[0])
                    except StopIteration:
                        items.remove(itm)
                        break

    def pre_tiles(nstg):
        return ([("stg%d" % i, [128, FF], F32) for i in range(nstg)] + [("ob%d" % i, [128, FF], BF16) for i in range(nstg)]
                + [("cvp", [128, NCV], F32)])

    def prepass_gen(l, W, store_eng, lazy_store, load_eng="sp", engs=("dve", "pool"), NSTG=3):
        bs = [Buf("stg%d" % i) for i in range(NSTG)]
        bo = [Buf("ob%d" % i) for i in range(NSTG)]
        bcv = Buf("cvp")
        S.dma(W["cvp"][:], cvL[l], writes=[bcv])
        bw = {k: [] for k in ("win", "woA", "woR", "wgu", "wd")}
        b_wscr[l] = bw
        pieces = []
        for c in range(8):
            pieces.append((winL[l, :, c, :], winB[l][:, c, :], 128, NIN, W["cvp"][:, CV_G1 + c:CV_G1 + c + 1], bw["win"], None))
        woA_src = woAL[l].rearrange("p h n -> p (h n)")
        for i in range(4):
            pieces.append((woA_src[:, i * 2048:(i + 1) * 2048], woAB[l][:, i * 2048:(i + 1) * 2048], 64, 2048, None, bw["woA"], None))
        woR_src = woRL[l].rearrange("p h n -> p (h n)")
        for i in range(2):
            pieces.append((woR_src[:, i * 2048:(i + 1) * 2048], woRB[l][:, i * 2048:(i + 1) * 2048], 128, 2048, None, bw["woR"], None))
        for c in range(8):
            for hf in range(2):
                dst = wguB[l][hf * 11:(hf + 1) * 11, :, c, :].rearrange("f p j -> p f j")
                pieces.append((wguL[l, c, :, hf, :], dst, 128, FF, W["cvp"][:, CV_G2 + c:CV_G2 + c + 1], bw["wgu"], 256))
        for i in range(8):
            pieces.append((wdL[l][:, i * FF:(i + 1) * FF], wdB[l][:, i * FF:(i + 1) * FF], 128, FF, None, bw["wd"], None))
        pending = None

        def store(pd):
            (dst, ob_ap, bok, wb, dv3) = pd
            if dv3 is None:
                S.dma(dst, ob_ap, reads=[bok], writes=[wb], eng=store_eng)
            else:
                S.dma(dst, ob_ap.rearrange("p (a b) -> p a b", b=dv3), reads=[bok], writes=[wb], eng=store_eng)

        for i, (src, dst, np_, n, scale, wbl, dv3) in enumerate(pieces):
            wb = Buf("wpiece")
            wbl.append(wb)
            k = i % NSTG
            stg = W["stg%d" % k]
            ob = W["ob%d" % k]
            if pending is not None:
                store(pending)
                pending = None
            S.dma(stg[0:np_, 0:n], src, writes=[bs[k]], eng=load_eng)
            e = engs[i % len(engs)]
            if scale is None:
                S.op(e, "tensor_copy", dict(out=ob[0:np_, 0:n], in_=stg[0:np_, 0:n]), [bs[k]], [bo[k]])
            else:
                S.op(e, "tensor_scalar", dict(out=ob[0:np_, 0:n], in0=stg[0:np_, 0:n], scalar1=scale, scalar2=None, op0=ALU.mult),
                     [bs[k], bcv], [bo[k]])
            pd = (dst, ob[0:np_, 0:n], bo[k], wb, dv3)
            if lazy_store:
                pending = pd
            else:
                store(pd)
            yield
        if pending is not None:
            store(pending)

    def prepass(l):
        with ExitStack() as st:
            W = tiles(st, pre_tiles(4))
            drive([prepass_gen(l, W, "act", False, engs=("dve",), NSTG=4)])
        S.barrier()

    def mixer(l, src, b_src_fn):
        with ExitStack() as st:
            W = tiles(st, [
                ("WinT", [128, 8, NIN], BF16), ("WoA", [64, 8, D], BF16), ("WoR", [128, 4, D], BF16),
                ("w2a2b", [128, 512], BF16), ("g2b", [128, 512], BF16),
                ("v1b", [128, 4, 32], BF16), ("v2b", [32, 512], BF16),
                ("cv", [128, NCV], F32), ("omka", [128, 4], F32), ("esk", [64, 8], F32), ("onesb", [128, 64], BF16),
                ("ulast", [128, 14], F32), ("H", [128, 4, 64], F32), ("Hb", [128, 4, 64], BF16),
                ("kT", [67, 2, 128 + TM], BF16), ("Vaug", [128, 1 + SUBM, 2, 65], BF16), ("qT", [67, 8, TM], BF16),
                ("xt", [128, SUBM, D], F32),
                ("ss", [128, SUBM], F32), ("rstd", [128, SUBM], F32),
                ("xnb", [128, D], BF16), ("xT", [128, 8, TM], BF16),
                ("u", [128, 1 + TM], F32), ("dsh", [128, TM], F32),
                ("twa", [128, TM], BF16), ("sgd0", [128, TM], BF16), ("sgd1", [128, TM], BF16),
                ("vall", [128, 4, TM], F32), ("vbf", [128, 4, TM], BF16), ("lvb", [32, TM], BF16),
                ("rk", [128, 2, TM], F32),
                ("tbig", [128, 12, TM], F32),
                ("gC0", [128, 4, NCH], F32), ("gC1", [128, 4, NCH], F32),
                ("AR0", [128, 4, NCH, 2, CH], BF16), ("BT0", [128, 4, TM], BF16), ("KT0", [128, 4, TM], BF16),
                ("AR1", [128, 4, NCH, 2, CH], BF16), ("BT1", [128, 4, TM], BF16), ("KT1", [128, 4, TM], BF16),
                ("bp", [128, TM], BF16), ("kpp", [128, TM], BF16),
                ("TM30", [64, 8, NCH, 3, 64], BF16), ("TM31", [64, 8, NCH, 3, 64], BF16),
                ("bonus0", [128, 4, TM], F32), ("bonus1", [128, 4, TM], F32), ("yT", [128, 4, TM], F32),
                ("Amat", [64, 8, 256], BF16), ("Nm", [64, 8, 64], BF16), ("NTm", [64, 8, 64], BF16),
                ("Nm2", [64, 8, 64], BF16), ("NTm2", [64, 8, 64], BF16),
                ("Pm", [64, 8, 64], BF16), ("Pm2", [64, 8, 64], BF16), ("Zb", [64, 8, 64], BF16), ("Ub", [64, 8, 64], BF16),
                ("sT", [128, 1, 512], F32), ("PT", [128, 2, 512], BF16),
                ("den", [65, 512], F32), ("attT0", [64, 8, TM], BF16), ("attT1", [64, 8, TM], BF16), ("rwT", [128, 4, TM], BF16),
                ("xos0", [128, 512], F32), ("xos1", [128, 512], F32),
                ("pp0", [128, TM], F32), ("pp1", [128, TM], F32),
            ])
            B = {k: Buf(k) for k in W}
            for i in range(12):
                W["t%d" % i] = W["tbig"][:, i, :]
                B["t%d" % i] = Buf("t%d" % i)
            xtf = W["xt"][:].rearrange("p j d -> p (j d)")
            W["w2a2f"] = xtf[:, 0:512]
            W["g2f"] = xtf[:, 512:1024]
            W["v1f"] = xtf[:, 1024:1152].rearrange("p (a b) -> p a b", b=32)
            W["v2f"] = xtf[0:32, 1152:1664]
            W["augf"] = xtf[0:67, 0:8 * TM].rearrange("p (a b) -> p a b", b=TM)
            tbf = W["tbig"][:].rearrange("p a b -> p (a b)")
            W["kaugf"] = tbf[0:67, 0:2 * (128 + TM)].rearrange("p (a b) -> p a b", b=128 + TM)
            for nm in ("w2a2f", "g2f", "v1f", "v2f", "augf"):
                B[nm] = B["xt"]
            B["kaugf"] = B["t0"]
            byT = [Buf("yT%d" % i) for i in range(4)]
            W["OTc"] = W["den"][0:64, :]
            B["OTc"] = Buf("OTc")
            W["Z2b"] = W["Ub"][:].rearrange("p h c -> p (h c)")
            B["Z2b"] = B["Ub"]
            bw = b_wscr[l]
            S.dma(W["WinT"][:], winB[l], reads=bw["win"], writes=[B["WinT"]])
            S.dma(W["WoA"][:].rearrange("p h n -> p (h n)"), woAB[l], reads=bw["woA"], writes=[B["WoA"]])
            S.dma(W["WoR"][:].rearrange("p h n -> p (h n)"), woRB[l], reads=bw["woR"], writes=[B["WoR"]])
            S.dma(W["w2a2f"][:], w2a2L[l], writes=[B["w2a2f"]])
            S.dma(W["g2f"][:], g2L[l], writes=[B["g2f"]])
            S.dma(W["cv"][:], cvL[l], writes=[B["cv"]])
            S.op("dve", "tensor_copy", dict(out=W["w2a2b"][:], in_=W["w2a2f"][:]), [B["w2a2f"]], [B["w2a2b"]])
            S.op("dve", "tensor_copy", dict(out=W["g2b"][:], in_=W["g2f"][:]), [B["g2f"]], [B["g2b"]])
            if l > 0:
                S.dma(W["v1f"][:], v1L, writes=[B["v1f"]])
                S.dma(W["v2f"][:], v2L, writes=[B["v2f"]])
                S.op("dve", "tensor_copy", dict(out=W["v1b"][:], in_=W["v1f"][:]), [B["v1f"]], [B["v1b"]])
                S.op("dve", "tensor_copy", dict(out=W["v2b"][:], in_=W["v2f"][:]), [B["v2f"]], [B["v2b"]])
            cv = W["cv"]

            def col(base, i):
                return cv[:, base + i:base + i + 1]
            S.op("dve", "tensor_scalar", dict(out=W["omka"][:], in0=cv[:, CV_KA:CV_KA + 4], scalar1=-1.0, scalar2=1.0,
                                              op0=ALU.mult, op1=ALU.add), [B["cv"]], [B["omka"]])
            S.dma(W["esk"][:], sinkL[l].broadcast_to([64, 8]), writes=[B["esk"]])
            S.op("act", "activation", dict(out=W["esk"][:], in_=W["esk"][:], func=AF.Exp), [B["esk"]], [B["esk"]])
            S.op("pool", "memset", dict(ap=W["onesb"][:], constant=1.0), [], [B["onesb"]])
            S.dma(W["augf"][64:67, :, :], c_qaug, writes=[B["augf"]])
            S.dma(W["kaugf"][64:67, :, :], c_kaug, writes=[B["kaugf"], B["t1"], B["t2"], B["t3"]])
            S.op("act", "activation", dict(out=W["qT"][64:67, :, :], in_=W["augf"][64:67, :, :], func=AF.Copy), [B["augf"]], [B["qT"]])
            S.op("act", "activation", dict(out=W["kT"][64:67, :, :], in_=W["kaugf"][64:67, :, :], func=AF.Copy), [B["kaugf"], B["t1"], B["t2"], B["t3"]], [B["kT"]])
            S.op("pool", "memset", dict(ap=W["ulast"][:], constant=0.0), [], [B["ulast"]])
            S.op("pool", "memset", dict(ap=W["H"][:], constant=0.0), [], [B["H"]])
            S.op("pool", "memset", dict(ap=W["Hb"][:], constant=0.0), [], [B["Hb"]])
            S.op("pool", "memset", dict(ap=W["Vaug"][:], constant=1.0), [], [B["Vaug"]])
            S.op("pool", "memset", dict(ap=W["kT"][0:64, :, :], constant=0.0), [], [B["kT"]])

            pj = {"i": 0, "b": 0}

            def fbank():
                pj["i"] += 1
                return pj["i"] % 2

            def bbank():
                pj["b"] += 1
                return 3 + pj["b"] % 2

            def proj_fm(col0, ncols):
                k = fbank()
                for c in range(8):
                    S.op("pe", "matmul", dict(out=PB[k][0:ncols, 0:TM], lhsT=W["WinT"][:, c, col0:col0 + ncols], rhs=W["xT"][:, c, :],
                                              start=(c == 0), stop=(c == 7)), [B["WinT"], B["xT"]], [bPB[k]])
                return k

            def shift_chunk(k, ci, dst, bdst):
                u = W["u"]
                S.op("act", "activation", dict(out=u[:, 1:1 + TM], in_=PB[k][:, 0:TM], func=AF.Copy), [bPB[k]], [B["u"]])
                S.op("pool", "tensor_copy", dict(out=u[:, 0:1], in_=W["ulast"][:, ci:ci + 1]), [B["ulast"]], [B["u"]])
                S.op("dve", "tensor_tensor", dict(out=W["dsh"][:], in0=u[:, 0:TM], in1=u[:, 1:1 + TM], op=ALU.subtract), [B["u"]], [B["dsh"]])
                S.op("dve", "scalar_tensor_tensor", dict(out=dst, in0=W["dsh"][:], scalar=col(CV_MU, ci), in1=u[:, 1:1 + TM],
                                                          op0=ALU.mult, op1=ALU.add), [B["dsh"], B["u"], B["cv"]], [bdst])
                S.op("pool", "tensor_copy", dict(out=W["ulast"][:, ci:ci + 1], in_=u[:, TM:TM + 1]), [B["u"]], [B["ulast"]])

            RW0 = 768

            def front(it):
                par = it % 2
                AR, BT, KT, TM3, gCt, bonus, sgd = (W["AR%d" % par], W["BT%d" % par], W["KT%d" % par], W["TM3%d" % par],
                                                    W["gC%d" % par], W["bonus%d" % par], W["sgd%d" % par])
                attT, battT = W["attT%d" % par], B["attT%d" % par]
                bAR, bBT, bKT, bTM3, bgC, bbonus, bsgd = (B["AR%d" % par], B["BT%d" % par], B["KT%d" % par], B["TM3%d" % par],
                                                          B["gC%d" % par], B["bonus%d" % par], B["sgd%d" % par])
                tok0 = it * TM
                bsrc = b_src_fn(it)
                S.dma(W["xt"][:], src[tok0:tok0 + TM, :].rearrange("(j p) d -> p j d", p=128), reads=[bsrc], writes=[B["xt"]])
                for j in range(SUBM):
                    S.op("act", "activation", dict(out=W["xnb"][:], in_=W["xt"][:, j, :], func=AF.Square, accum_out=W["ss"][:, j:j + 1]),
                         [B["xt"]], [B["xnb"], B["ss"]])
                S.op("dve", "tensor_scalar", dict(out=W["rstd"][:], in0=W["ss"][:], scalar1=1.0 / D, scalar2=RMS_EPS, op0=ALU.mult, op1=ALU.add),
                     [B["ss"]], [B["rstd"]])
                S.op("act", "activation", dict(out=W["rstd"][:], in_=W["rstd"][:], func=AF.Sqrt), [B["rstd"]], [B["rstd"]])
                S.op("dve", "reciprocal", dict(out=W["rstd"][:], in_=W["rstd"][:]), [B["rstd"]], [B["rstd"]])
                yield
                for j in range(SUBM):
                    S.op("dve", "tensor_scalar", dict(out=W["xnb"][:], in0=W["xt"][:, j, :], scalar1=W["rstd"][:, j:j + 1], scalar2=None, op0=ALU.mult),
                         [B["xt"], B["rstd"]], [B["xnb"]])
                    pT = PB[2][:].bitcast(BF16)
                    for c in range(8):
                        S.op("pe", "transpose", dict(out=pT[:, c * 128:(c + 1) * 128], in_=W["xnb"][:, c * 128:(c + 1) * 128], identity=C["identb"][:]),
                             [B["xnb"], bC], [bPB[2]])
                    S.op("act", "activation", dict(out=W["xT"][:, :, j * 128:(j + 1) * 128], in_=pT[:, 0:1024].rearrange("p (c t) -> p c t", t=128),
                                                   func=AF.Copy), [bPB[2]], [B["xT"]])
                    yield
                for h in range(8):
                    k = proj_fm(h * 64, 64)
                    S.op("act", "activation", dict(out=W["qT"][0:64, h, :], in_=PB[k][0:64, 0:TM], func=AF.Copy, scale=0.125), [bPB[k]], [B["qT"]])
                    yield
                for kv in range(2):
                    k = proj_fm(512 + kv * 64, 64)
                    S.op("dve", "tensor_copy", dict(out=W["kT"][0:64, kv, 128:128 + TM], in_=PB[k][0:64, 0:TM]), [bPB[k]], [B["kT"]])
                    yield
                for j in range(SUBM):
                    k = fbank()
                    for c in range(8):
                        S.op("pe", "matmul", dict(out=PB[k][:, 0:128], lhsT=W["xT"][:, c, j * 128:(j + 1) * 128], rhs=W["WinT"][:, c, 640:768],
                                                  start=(c == 0), stop=(c == 7)), [B["WinT"], B["xT"]], [bPB[k]])
                    S.op("dve", "tensor_copy", dict(out=W["Vaug"][:, 1 + j, :, 0:64], in_=PB[k][:, 0:128].rearrange("p (a b) -> p a b", b=64)),
                         [bPB[k]], [B["Vaug"]])
                    yield
                k = proj_fm(RW0 + 1536, 128)
                shift_chunk(k, 12, W["t0"], B["t0"])
                S.op("act", "activation", dict(out=W["twa"][0:64, :], in_=W["t0"][0:64, :], func=AF.Tanh), [B["t0"]], [B["twa"]])
                S.op("dve", "tensor_copy", dict(out=W["twa"][64:128, :], in_=W["t0"][64:128, :]), [B["t0"]], [B["twa"]])
                yield
                k = proj_fm(RW0 + 1664, 128)
                shift_chunk(k, 13, W["t0"], B["t0"])
                S.op("act", "activation", dict(out=sgd[:], in_=W["t0"], func=AF.Sigmoid), [B["t0"]], [bsgd])
                yield
                for cc in range(4):
                    k = proj_fm(RW0 + 1024 + cc * 128, 128)
                    shift_chunk(k, 8 + cc, W["vall"][:, cc, :], B["vall"])
                    yield
                if l == 0:
                    S.dma(vf[:, :, tok0:tok0 + TM], W["vall"][:], reads=[B["vall"]], writes=[b_vf[it]], eng="sp")
                else:
                    vfst = bonus
                    S.dma(vfst[:], vf[:, :, tok0:tok0 + TM], reads=[b_vf[it]], writes=[bbonus])
                    S.op("pool", "tensor_copy", dict(out=W["vbf"][:], in_=W["vall"][:]), [B["vall"]], [B["vbf"]])
                    k = fbank()
                    for cc in range(4):
                        S.op("pe", "matmul", dict(out=PB[k][0:32, 0:TM], lhsT=W["v1b"][:, cc, :], rhs=W["vbf"][:, cc, :], start=(cc == 0), stop=(cc == 3)),
                             [B["v1b"], B["vbf"]], [bPB[k]])
                    S.op("act", "activation", dict(out=W["lvb"][:], in_=PB[k][0:32, 0:TM], func=AF.Copy), [bPB[k]], [B["lvb"]])
                    yield
                    for cc in range(4):
                        k = fbank()
                        S.op("pe", "matmul", dict(out=PB[k][:, 0:TM], lhsT=W["v2b"][0:32, cc * 128:(cc + 1) * 128], rhs=W["lvb"][:], start=True, stop=True),
                             [B["v2b"], B["lvb"]], [bPB[k]])
                        S.op("act", "activation", dict(out=W["t0"], in_=PB[k][:, 0:TM], func=AF.Sigmoid, bias=col(CV_V0, cc)), [bPB[k], B["cv"]], [B["t0"]])
                        S.op("dve", "tensor_tensor", dict(out=W["t1"], in0=vfst[:, cc, :], in1=W["vall"][:, cc, :], op=ALU.subtract),
                             [bbonus, B["vall"]], [B["t1"]])
                        S.op("dve", "tensor_tensor", dict(out=W["t1"], in0=W["t1"], in1=W["t0"], op=ALU.mult), [B["t1"], B["t0"]], [B["t1"]])
                        S.op("dve", "tensor_tensor", dict(out=W["vall"][:, cc, :], in0=W["vall"][:, cc, :], in1=W["t1"], op=ALU.add),
                             [B["vall"], B["t1"]], [B["vall"]])
                        yield
                S.op("pool", "tensor_copy", dict(out=W["vbf"][:], in_=W["vall"][:]), [B["vall"]], [B["vbf"]])
                yield
                for cc in range(4):
                    k = proj_fm(RW0 + cc * 128, 128)
                    shift_chunk(k, cc, W["rk"][:, 0, :], B["rk"])
                    yield
                    k = proj_fm(RW0 + 512 + cc * 128, 128)
                    shift_chunk(k, 4 + cc, W["rk"][:, 1, :], B["rk"])
                    yield
                    r_ = W["rk"][:, 0, :]
                    k_ = W["rk"][:, 1, :]
                    v_ = W["vall"][:, cc, :]
                    ld, cl, g, gi, gm1, a, kk, kkn, kp, ba, tA, tB = [W["t%d" % i] for i in range(12)]
                    bl = [B["t%d" % i] for i in range(12)]
                    kw = fbank()
                    S.op("pe", "matmul", dict(out=PB[kw][:, 0:TM], lhsT=W["w2a2b"][0:64, cc * 128:(cc + 1) * 128], rhs=W["twa"][0:64, :], start=True, stop=True),
                         [B["w2a2b"], B["twa"]], [bPB[kw]])
                    ka_ = fbank()
                    S.op("pe", "matmul", dict(out=PB[ka_][:, 0:TM], lhsT=W["w2a2b"][64:128, cc * 128:(cc + 1) * 128], rhs=W["twa"][64:128, :], start=True, stop=True),
                         [B["w2a2b"], B["twa"]], [bPB[ka_]])
                    S.op("act", "activation", dict(out=ld, in_=PB[kw][:, 0:TM], func=AF.Sigmoid, bias=col(CV_W0, cc)), [bPB[kw], B["cv"]], [bl[0]])
                    S.op("act", "activation", dict(out=a, in_=PB[ka_][:, 0:TM], func=AF.Sigmoid, bias=col(CV_A0, cc)), [bPB[ka_], B["cv"]], [bl[5]])
                    S.op("pool", "tensor_scalar", dict(out=kk, in0=k_, scalar1=col(CV_KK, cc), scalar2=None, op0=ALU.mult), [B["rk"], B["cv"]], [bl[6]])
                    S.op("act", "activation", dict(out=tA, in_=kk, func=AF.Square), [bl[6]], [bl[10]])
                    yield
                    S.op("dve", "tensor_scalar", dict(out=kkn, in0=a, scalar1=col(CV_KA, cc), scalar2=W["omka"][:, cc:cc + 1], op0=ALU.mult, op1=ALU.add),
                         [bl[5], B["cv"], B["omka"]], [bl[7]])
                    S.op("pool", "tensor_tensor", dict(out=kp, in0=k_, in1=kkn, op=ALU.mult), [B["rk"], bl[7]], [bl[8]])
                    S.op("dve", "scalar_tensor_tensor", dict(out=ba, in0=r_, scalar=col(CV_RK, cc), in1=kp, op0=ALU.mult, op1=ALU.mult),
                         [B["rk"], B["cv"], bl[8]], [bl[9]])
                    yield
                    S.op("pool", "tensor_scalar", dict(out=ld, in0=ld, scalar1=-EM05, scalar2=None, op0=ALU.mult), [bl[0]], [bl[0]])
                    S.op("dve", "tensor_tensor_scan", dict(out=cl, data0=C["resetm"][:], data1=ld, initial=0.0, op0=ALU.mult, op1=ALU.add),
                         [bC, bl[0]], [bl[1]])
                    S.op("act", "activation", dict(out=g, in_=cl, func=AF.Exp), [bl[1]], [bl[2]])
                    S.op("act", "activation", dict(out=gi, in_=cl, func=AF.Exp, scale=-1.0), [bl[1]], [bl[3]])
                    S.op("pool", "tensor_tensor", dict(out=gm1, in0=cl, in1=ld, op=ALU.subtract), [bl[1], bl[0]], [bl[4]])
                    S.op("act", "activation", dict(out=gm1, in_=gm1, func=AF.Exp), [bl[4]], [bl[4]])
                    S.op("pool", "tensor_copy", dict(out=gCt[:, cc, :], in_=g.rearrange("p (n t) -> p n t", t=CH)[:, :, CH - 1]), [bl[2]], [bgC])
                    yield
                    k1 = fbank()
                    S.op("pe", "matmul", dict(out=PB[k1][:, 0:TM], lhsT=C["bones"][:], rhs=tA, start=True, stop=True), [bC, bl[10]], [bPB[k1]])
                    k2 = fbank()
                    S.op("pe", "matmul", dict(out=PB[k2][:, 0:TM], lhsT=C["bones"][:], rhs=ba, start=True, stop=True), [bC, bl[9]], [bPB[k2]])
                    S.op("dve", "tensor_scalar", dict(out=tB, in0=PB[k1][:, 0:TM], scalar1=1e-18, scalar2=None, op0=ALU.max), [bPB[k1]], [bl[11]])
                    S.op("act", "activation", dict(out=tB, in_=tB, func=AF.Ln), [bl[11]], [bl[11]])
                    S.op("act", "activation", dict(out=tB, in_=tB, func=AF.Exp, scale=-0.5), [bl[11]], [bl[11]])
                    S.op("dve", "tensor_tensor", dict(out=bonus[:, cc, :], in0=PB[k2][:, 0:TM], in1=v_, op=ALU.mult), [bPB[k2], B["vall"]], [bbonus])
                    yield
                    S.op("dve", "tensor_tensor", dict(out=kkn, in0=kk, in1=tB, op=ALU.mult), [bl[6], bl[11]], [bl[7]])
                    S.op("pool", "tensor_tensor", dict(out=AR[:, cc, :, 1, :], in0=r_.rearrange("p (n t) -> p n t", t=CH),
                                                       in1=g.rearrange("p (n t) -> p n t", t=CH), op=ALU.mult), [B["rk"], bl[2]], [bAR])
                    S.op("dve", "scalar_tensor_tensor", dict(out=AR[:, cc, :, 0, :], in0=kkn.rearrange("p (n t) -> p n t", t=CH), scalar=-1.0,
                                                              in1=gm1.rearrange("p (n t) -> p n t", t=CH), op0=ALU.mult, op1=ALU.mult),
                         [bl[7], bl[4]], [bAR])
                    S.op("dve", "tensor_tensor", dict(out=ba, in0=kkn, in1=a, op=ALU.mult), [bl[7], bl[5]], [bl[9]])
                    S.op("pool", "tensor_tensor", dict(out=BT[:, cc, :], in0=ba, in1=gi, op=ALU.mult), [bl[9], bl[3]], [bBT])
                    S.op("dve", "tensor_tensor", dict(out=KT[:, cc, :], in0=kp, in1=gi, op=ALU.mult), [bl[8], bl[3]], [bKT])
                    yield
                    for n in range(NCH):
                        S.op("dve", "tensor_scalar", dict(out=tB[:, n * CH:(n + 1) * CH], in0=gi[:, n * CH:(n + 1) * CH],
                                                          scalar1=g[:, n * CH + CH - 1:n * CH + CH], scalar2=None, op0=ALU.mult), [bl[3], bl[2]], [bl[11]])
                    S.op("pool", "tensor_tensor", dict(out=W["bp"][:], in0=ba, in1=tB, op=ALU.mult), [bl[9], bl[11]], [B["bp"]])
                    S.op("dve", "tensor_tensor", dict(out=W["kpp"][:], in0=kp, in1=tB, op=ALU.mult), [bl[8], bl[11]], [B["kpp"]])
                    yield
                    trp = PB[2][:].bitcast(BF16)
                    for hh in range(2):
                        h = cc * 2 + hh
                        pb = hh * 64
                        for n in range(NCH):
                            for oi, (srcT, bsrcT) in enumerate(((W["bp"][:], B["bp"]), (W["kpp"][:], B["kpp"]), (W["vbf"][:, cc, :], B["vbf"]))):
                                o0 = (n * 3 + oi) * 64
                                S.op("pe", "transpose", dict(out=trp[0:64, o0:o0 + 64], in_=srcT[pb:pb + 64, n * CH:(n + 1) * CH],
                                                             identity=C["identb"][pb:pb + 64, pb:pb + 64]), [bsrcT, bC], [bPB[2]])
                        S.op("act", "activation", dict(out=TM3[:, h, :, :, :].rearrange("p n o t -> p (n o t)"),
                                                       in_=trp[0:64, 0:NCH * 192], func=AF.Copy), [bPB[2]], [bTM3])
                        yield
                for j in range(SUBM):
                    gsub = it * SUBM + j
                    sbs = [1] if gsub == 0 else [0, 1]
                    for kv in range(2):
                        for sb in sbs:
                            kcol = (j + sb) * 128
                            K_ = 67 if sb == 0 else 66
                            S.op("pe", "matmul", dict(out=PB[sb][:, :], lhsT=W["kT"][0:K_, kv, kcol:kcol + 128],
                                                      rhs=W["qT"][0:K_, kv * 4:(kv + 1) * 4, j * 128:(j + 1) * 128], start=True, stop=False),
                                 [B["kT"], B["qT"]], [bPB[sb]])
                            S.op("pe", "matmul", dict(out=PB[sb][:, :], lhsT=C["identb"][:], rhs=C["amask"][:, sb, :], start=False, stop=True),
                                 [bC], [bPB[sb]])
                            S.op("act", "activation", dict(out=W["PT"][:, sb, :], in_=PB[sb][:, :], func=AF.Exp), [bPB[sb]], [B["PT"]])
                        yield
                        yield
                        for i, sb in enumerate(sbs):
                            S.op("pe", "matmul", dict(out=PB[2][0:65, :], lhsT=W["Vaug"][:, j + sb, kv, :], rhs=W["PT"][:, sb, :],
                                                      start=(i == 0), stop=(i == len(sbs) - 1)), [B["Vaug"], B["PT"]], [bPB[2]])
                        for i, sb in enumerate(sbs):
                            S.op("pe", "matmul", dict(out=PB[0][0:64, :], lhsT=W["onesb"][:, :], rhs=W["PT"][:, sb, :],
                                                      start=(i == 0), stop=(i == len(sbs) - 1)), [B["onesb"], B["PT"]], [bPB[0]])
                        for gg in range(4):
                            S.op("dve", "tensor_scalar", dict(out=W["OTc"][:, gg * 128:(gg + 1) * 128], in0=PB[0][0:64, gg * 128:(gg + 1) * 128],
                                                              scalar1=W["esk"][:, kv * 4 + gg:kv * 4 + gg + 1], scalar2=None, op0=ALU.add),
                                 [bPB[0], B["esk"]], [B["OTc"]])
                        S.op("act", "activation", dict(out=W["OTc"], in_=W["OTc"], func=AF.Ln), [B["OTc"]], [B["OTc"]])
                        S.op("act", "activation", dict(out=W["OTc"], in_=W["OTc"], func=AF.Exp, scale=-1.0), [B["OTc"]], [B["OTc"]])
                        yield
                        S.op("dve", "tensor_tensor", dict(out=attT[:, kv * 4:(kv + 1) * 4, j * 128:(j + 1) * 128],
                                                          in0=PB[2][0:64, :].rearrange("p (g t) -> p g t", t=128),
                                                          in1=W["OTc"].rearrange("p (g t) -> p g t", t=128), op=ALU.mult),
                             [B["OTc"], bPB[2]], [battT])
                        yield
                S.op("pool", "tensor_copy", dict(out=W["kT"][0:64, :, 0:128], in_=W["kT"][0:64, :, TM:TM + 128]), [B["kT"]], [B["kT"]])
                S.op("pool", "tensor_copy", dict(out=W["Vaug"][:, 0, :, :], in_=W["Vaug"][:, SUBM, :, :]), [B["Vaug"]], [B["Vaug"]])
                yield

            def back(it):
                par = it % 2
                AR, BT, KT, TM3, gCt, bonus, sgd = (W["AR%d" % par], W["BT%d" % par], W["KT%d" % par], W["TM3%d" % par],
                                                    W["gC%d" % par], W["bonus%d" % par], W["sgd%d" % par])
                attT, battT = W["attT%d" % par], B["attT%d" % par]
                bAR, bBT, bKT, bTM3, bgC, bbonus, bsgd = (B["AR%d" % par], B["BT%d" % par], B["KT%d" % par], B["TM3%d" % par],
                                                          B["gC%d" % par], B["bonus%d" % par], B["sgd%d" % par])
                tok0 = it * TM
                bsrc = b_src_fn(it)
                for n in range(NCH):
                    for hp in range(4):
                        for hh in range(2):
                            h = hp * 2 + hh
                            pb = hh * 64
                            bank = ((3, 5), (4, 7))[hh][hp % 2]
                            rhsAR = AR[pb:pb + 64, hp, n, :, :]
                            S.op("pe", "matmul", dict(out=PB[bank][0:64, 0:128], lhsT=BT[pb:pb + 64, hp, n * CH:(n + 1) * CH],
                                                      rhs=rhsAR, start=True, stop=True), [bBT, bAR], [bPB[bank]])
                            S.op("pe", "matmul", dict(out=PB[bank][0:64, 128:256], lhsT=KT[pb:pb + 64, hp, n * CH:(n + 1) * CH],
                                                      rhs=rhsAR, start=True, stop=True), [bKT, bAR], [bPB[bank]])
                            S.op("dve", "tensor_tensor", dict(out=W["Amat"][:, h, :], in0=PB[bank][0:64, 0:256],
                                                              in1=C["maskA"][:, 0:256], op=ALU.mult), [bPB[bank], bC], [B["Amat"]])
                        yield
                    trp = PB[5][:].bitcast(BF16)
                    for h in range(8):
                        S.op("pe", "transpose", dict(out=trp[0:64, h * 64:(h + 1) * 64], in_=W["Amat"][:, h, 0:64], identity=C["identb"][0:64, 0:64]),
                             [B["Amat"], bC], [bPB[5]])
                    S.op("act", "activation", dict(out=W["Nm"][:].rearrange("p h c -> p (h c)"), in_=trp[0:64, 0:512], func=AF.Copy), [bPB[5]], [B["Nm"]])
                    S.op("dve", "tensor_tensor", dict(out=W["Pm"][:].rearrange("p h c -> p (h c)"), in0=W["Nm"][:].rearrange("p h c -> p (h c)"),
                                                      in1=C["identI8"][:], op=ALU.add), [B["Nm"], bC], [B["Pm"]])
                    yield
                    M_, MT_, M2_, MT2_ = "Nm", "NTm", "Nm2", "NTm2"
                    P_, P2_ = "Pm", "Pm2"
                    for lev in range(5):
                        last = (lev == 4)
                        if lev == 0:
                            mt_ap = lambda h: W["Amat"][:, h, 0:64]
                            bmt = B["Amat"]
                        else:
                            mt_ap = lambda h, MT_=MT_: W[MT_][:, h, :]
                            bmt = B[MT_]
                        for h in range(8):
                            S.op("pe", "matmul", dict(out=PB[6][0:64, h * 64:(h + 1) * 64], lhsT=W[M_][:, h, :], rhs=mt_ap(h), start=True, stop=True),
                                 [B[M_], bmt], [bPB[6]])
                        S.op("act", "activation", dict(out=W[MT2_][:].rearrange("p h c -> p (h c)"), in_=PB[6][0:64, :], func=AF.Copy), [bPB[6]], [B[MT2_]])
                        yield
                        if not last:
                            for h in range(8):
                                S.op("pe", "matmul", dict(out=PB[7][0:64, h * 64:(h + 1) * 64], lhsT=mt_ap(h), rhs=W[M_][:, h, :], start=True, stop=True),
                                     [B[M_], bmt], [bPB[7]])
                            S.op("dve", "tensor_copy", dict(out=W[M2_][:].rearrange("p h c -> p (h c)"), in_=PB[7][0:64, :]), [bPB[7]], [B[M2_]])
                            yield
                        for h in range(8):
                            S.op("pe", "matmul", dict(out=PB[5][0:64, h * 64:(h + 1) * 64], lhsT=W[MT2_][:, h, :], rhs=W[P_][:, h, :], start=True, stop=True),
                                 [B[MT2_], B[P_]], [bPB[5]])
                        S.op("dve", "tensor_tensor", dict(out=W[P2_][:].rearrange("p h c -> p (h c)"), in0=PB[5][0:64, :],
                                                          in1=W[P_][:].rearrange("p h c -> p (h c)"), op=ALU.add), [bPB[5], B[P_]], [B[P2_]])
                        yield
                        M_, M2_ = M2_, M_
                        MT_, MT2_ = MT2_, MT_
                        P_, P2_ = P2_, P_
                    trp = PB[6][:].bitcast(BF16)
                    for h in range(8):
                        S.op("pe", "transpose", dict(out=trp[0:64, h * 64:(h + 1) * 64], in_=W[P_][:, h, :], identity=C["identb"][0:64, 0:64]),
                             [B[P_], bC], [bPB[6]])
                    S.op("act", "activation", dict(out=W[P2_][:].rearrange("p h c -> p (h c)"), in_=trp[0:64, 0:512], func=AF.Copy), [bPB[6]], [B[P2_]])
                    TinvT = P2_
                    yield
                    for h in range(8):
                        hp, hh = h // 2, h % 2
                        pb = hh * 64
                        S.op("pe", "matmul", dict(out=PB[3 + hh][0:64, hp * 64:(hp + 1) * 64], lhsT=AR[pb:pb + 64, hp, n, 0, :], rhs=W["Hb"][pb:pb + 64, hp, :],
                                                  start=True, stop=True), [bAR, B["Hb"]], [bPB[3 + hh]])
                        S.op("pe", "matmul", dict(out=PB[7][0:64, h * 64:(h + 1) * 64], lhsT=W["Amat"][:, h, 128:192], rhs=TM3[:, h, n, 2, :],
                                                  start=True, stop=True), [B["Amat"], bTM3], [bPB[7]])
                    S.op("act", "activation", dict(out=W["Z2b"], in_=PB[7][0:64, :], func=AF.Copy), [bPB[7]], [B["Z2b"]])
                    z2 = W["Z2b"].rearrange("p (q e c) -> p q e c", e=2, c=64)
                    zb4 = W["Zb"][:].rearrange("p (q e) c -> p q e c", e=2)
                    for hh in range(2):
                        S.op("dve", "tensor_tensor", dict(out=zb4[:, :, hh, :], in0=PB[3 + hh][0:64, 0:256].rearrange("p (q c) -> p q c", c=64),
                                                          in1=z2[:, :, hh, :], op=ALU.add), [bPB[3 + hh], B["Z2b"]], [B["Zb"]])
                    yield
                    for h in range(8):
                        S.op("pe", "matmul", dict(out=PB[6][0:64, h * 64:(h + 1) * 64], lhsT=W[TinvT][:, h, :], rhs=W["Zb"][:, h, :], start=True, stop=True),
                             [B[TinvT], B["Zb"]], [bPB[6]])
                    S.op("act", "activation", dict(out=W["Ub"][:].rearrange("p h c -> p (h c)"), in_=PB[6][0:64, :], func=AF.Copy), [bPB[6]], [B["Ub"]])
                    yield
                    for h in range(8):
                        hp, hh = h // 2, h % 2
                        pb = hh * 64
                        S.op("pe", "matmul", dict(out=PB[3 + hh][pb:pb + 64, 256 + hp * 64:256 + (hp + 1) * 64], lhsT=W["Hb"][pb:pb + 64, hp, :],
                                                  rhs=AR[pb:pb + 64, hp, n, 1, :], start=True, stop=True), [B["Hb"], bAR], [bPB[3 + hh]])
                        yo = PB[5][pb:pb + 64, hp * 64:(hp + 1) * 64]
                        S.op("pe", "matmul", dict(out=yo, lhsT=W["Ub"][:, h, :], rhs=W["Amat"][:, h, 64:128], start=True, stop=False),
                             [B["Ub"], B["Amat"]], [bPB[5]])
                        S.op("pe", "matmul", dict(out=yo, lhsT=TM3[:, h, n, 2, :], rhs=W["Amat"][:, h, 192:256], start=False, stop=True),
                             [bTM3, B["Amat"]], [bPB[5]])
                        ho = PB[5][pb:pb + 64, 256 + hp * 64:256 + (hp + 1) * 64]
                        S.op("pe", "matmul", dict(out=ho, lhsT=TM3[:, h, n, 0, :], rhs=W["Ub"][:, h, :], start=True, stop=False),
                             [bTM3, B["Ub"]], [bPB[5]])
                        S.op("pe", "matmul", dict(out=ho, lhsT=TM3[:, h, n, 1, :], rhs=TM3[:, h, n, 2, :], start=False, stop=True),
                             [bTM3], [bPB[5]])
                        if h % 2 == 1:
                            yield
                    S.op("act", "activation", dict(out=W["yT"][:, :, n * CH:(n + 1) * CH], in_=PB[5][:, 0:256].rearrange("p (c t) -> p c t", t=64), func=AF.Copy),
                         [bPB[5]], byT)
                    for hh in range(2):
                        pb = hh * 64
                        S.op("dve", "tensor_tensor", dict(out=W["yT"][pb:pb + 64, :, n * CH:(n + 1) * CH],
                                                          in0=PB[3 + hh][pb:pb + 64, 256:512].rearrange("p (c t) -> p c t", t=64),
                                                          in1=W["yT"][pb:pb + 64, :, n * CH:(n + 1) * CH], op=ALU.add), [bPB[3 + hh]] + byT, byT)
                    for hp in range(4):
                        S.op("dve", "scalar_tensor_tensor", dict(out=W["H"][:, hp, :], in0=W["H"][:, hp, :], scalar=gCt[:, hp, n:n + 1],
                                                                  in1=PB[5][:, 256 + hp * 64:256 + (hp + 1) * 64], op0=ALU.mult, op1=ALU.add),
                             [B["H"], bgC, bPB[5]], [B["H"]])
                    S.op("act", "activation", dict(out=W["Hb"][:], in_=W["H"][:], func=AF.Copy), [B["H"]], [B["Hb"]])
                    yield
                def gate_mm(cc):
                    S.op("pe", "matmul", dict(out=PB[5][:, (cc % 2) * 256:(cc % 2) * 256 + TM], lhsT=W["g2b"][:, cc * 128:(cc + 1) * 128], rhs=sgd[:],
                                              start=True, stop=True), [B["g2b"], bsgd], [bPB[5]])
                yield
                for c0 in (0, 2):
                    ccs = (c0, c0 + 1)
                    m1 = {c0: 6, c0 + 1: 3}
                    m2 = {c0: 7, c0 + 1: 4}
                    tmp = {c0: W["pp0"][:], c0 + 1: W["pp1"][:]}
                    btmp = {c0: B["pp0"], c0 + 1: B["pp1"]}
                    for cc in ccs:
                        S.op("pe", "matmul", dict(out=PB[m1[cc]][:, 0:TM], lhsT=C["bones"][:], rhs=W["yT"][:, cc, :], start=True, stop=True),
                             [bC, byT[cc]], [bPB[m1[cc]]])
                    if c0 == 0:
                        gate_mm(0)
                        gate_mm(1)
                    yield
                    for cc in ccs:
                        S.op("dve", "scalar_tensor_tensor", dict(out=W["yT"][:, cc, :], in0=PB[m1[cc]][:, 0:TM], scalar=-1.0 / 64, in1=W["yT"][:, cc, :],
                                                                  op0=ALU.mult, op1=ALU.add), [bPB[m1[cc]], byT[cc]], [byT[cc]])
                        S.op("act", "activation", dict(out=tmp[cc], in_=W["yT"][:, cc, :], func=AF.Square), [byT[cc]], [btmp[cc]])
                    yield
                    yield
                    for cc in ccs:
                        S.op("pe", "matmul", dict(out=PB[m2[cc]][:, 0:TM], lhsT=C["bones"][:], rhs=tmp[cc], start=True, stop=True), [bC, btmp[cc]], [bPB[m2[cc]]])
                    yield
                    for cc in ccs:
                        S.op("dve", "tensor_scalar", dict(out=tmp[cc], in0=PB[m2[cc]][:, 0:TM], scalar1=1.0 / 64, scalar2=LNX_EPS, op0=ALU.mult, op1=ALU.add),
                             [bPB[m2[cc]]], [btmp[cc]])
                        S.op("act", "activation", dict(out=tmp[cc], in_=tmp[cc], func=AF.Ln), [btmp[cc]], [btmp[cc]])
                        S.op("act", "activation", dict(out=tmp[cc], in_=tmp[cc], func=AF.Exp, scale=-0.5), [btmp[cc]], [btmp[cc]])
                    yield
                    for cc in ccs:
                        yn = W["yT"][:, cc, :]
                        S.op("dve", "tensor_tensor", dict(out=yn, in0=yn, in1=tmp[cc], op=ALU.mult), [byT[cc], btmp[cc]], [byT[cc]])
                        S.op("dve", "tensor_scalar", dict(out=yn, in0=yn, scalar1=col(CV_LG, cc), scalar2=col(CV_LB, cc), op0=ALU.mult, op1=ALU.add),
                             [byT[cc], B["cv"]], [byT[cc]])
                        S.op("pool", "tensor_tensor", dict(out=yn, in0=yn, in1=bonus[:, cc, :], op=ALU.add), [byT[cc], bbonus], [byT[cc]])
                    yield
                    for cc in ccs:
                        S.op("dve", "tensor_tensor", dict(out=W["rwT"][:, cc, :], in0=PB[5][:, (cc % 2) * 256:(cc % 2) * 256 + TM], in1=W["yT"][:, cc, :], op=ALU.mult),
                             [bPB[5], byT[cc]], [B["rwT"]])
                    if c0 == 0:
                        gate_mm(2)
                        gate_mm(3)
                    yield
                if dbg and "rw" in dbg_out and it == dbg.get("_dbgtile", 0):
                    S.op("pool", "tensor_copy", dict(out=W["yT"][:], in_=W["rwT"][:]), [B["rwT"]], byT)
                    S.dma(dbg_out["rw"], W["yT"][:], reads=byT, writes=[b_out])
                    S.op("pool", "tensor_copy", dict(out=W["yT"][0:64, :, :].rearrange("p a b -> p (a b)"), in_=attT[:, 0:4, :].rearrange("p a b -> p (a b)")),
                         [battT], byT)
                    S.dma(dbg_out["att"], W["yT"][0:64, :, :], reads=byT, writes=[b_out])
                kk_ = 0
                for j in range(SUBM):
                    for nh in range(2):
                        xo = W["xos%d" % (kk_ % 2)]
                        bxo = B["xos%d" % (kk_ % 2)]
                        kk_ += 1
                        r0 = tok0 + j * 128
                        S.dma(xo[:], src[r0:r0 + 128, nh * 512:(nh + 1) * 512], reads=[bsrc], writes=[bxo])
                        k = bbank()
                        for h in range(8):
                            S.op("pe", "matmul", dict(out=PB[k][:, :], lhsT=attT[:, h, j * 128:(j + 1) * 128], rhs=W["WoA"][:, h, nh * 512:(nh + 1) * 512],
                                                      start=(h == 0), stop=False), [battT, B["WoA"]], [bPB[k]])
                        for cc in range(4):
                            S.op("pe", "matmul", dict(out=PB[k][:, :], lhsT=W["rwT"][:, cc, j * 128:(j + 1) * 128], rhs=W["WoR"][:, cc, nh * 512:(nh + 1) * 512],
                                                      start=False, stop=(cc == 3)), [B["rwT"], B["WoR"]], [bPB[k]])
                        S.op("dve", "tensor_tensor", dict(out=xo[:], in0=PB[k][:, :], in1=xo[:], op=ALU.add), [bPB[k], bxo], [bxo])
                        yield
                        S.dma(xs[r0:r0 + 128, nh * 512:(nh + 1) * 512], xo[:], reads=[bxo], writes=[b_xs[it]], eng="sp")

            ntm = dbg.get("_ntm", NTM) if dbg else NTM
            drive([front(0)])
            for it in range(ntm):
                drive([back(it), front(it + 1) if it + 1 < ntm else None], weights=[int(_os.environ.get("MK_RB", "2")), int(_os.environ.get("MK_RF", "1"))])
        S.barrier()

    def ffn(l, last):
        with ExitStack() as st:
            W = tiles(st, [
                ("xt0", [128, SUBF, D], F32), ("xt1", [128, SUBF, D], F32), ("ss", [128, SUBF], F32), ("rstd", [128, SUBF], F32),
                ("ss2", [128, SUBF], F32), ("rstd2", [128, SUBF], F32),
                ("xnb", [128, D], BF16), ("xnb1", [128, D], BF16), ("xT0", [128, 8, TF], BF16), ("xT1", [128, 8, TF], BF16),
                ("wgu0", [128, 8, 256], BF16), ("wgu1", [128, 8, 256], BF16), ("wgu2", [128, 8, 256], BF16), ("Wd", [128, NFC, D], BF16),
                ("gc", [128, 2 + TF], F32), ("gcar", [128, NFC, 2], F32),
                ("c1", [128, TF], F32), ("c2", [128, TF], F32), ("c3", [128, TF], F32), ("sl", [128, TF], F32),
                ("hT", [128, NFC, TF], BF16), ("xo", [128, SUBF, D], F32), ("cv", [128, NCV], F32), ("fingb", [128, D], F32),
            ])
            B = {k: Buf(k) for k in W}
            bw = b_wscr[l]
            cv = W["cv"]
            S.dma(cv[:], cvL[l], writes=[B["cv"]])
            S.dma(W["fingb"][:], fing.broadcast_to([128, D]), writes=[B["fingb"]])
            S.op("pool", "memset", dict(ap=W["gcar"][:], constant=0.0), [], [B["gcar"]])
            bWd = [Buf("Wd%d" % i) for i in range(NFC)]

            def head_dma(it):
                par = it % 2
                tok0 = it * TF
                xt, bxt = W["xt%d" % par], B["xt%d" % par]
                S.dma(xt[:], xs[tok0:tok0 + TF, :].rearrange("(j p) d -> p j d", p=128), reads=[b_xs[2 * it], b_xs[2 * it + 1]], writes=[bxt])

            def head_load(it):
                par = it % 2
                xt, bxt = W["xt%d" % par], B["xt%d" % par]
                for j in range(SUBF):
                    S.op("act", "activation", dict(out=W["xnb"][:], in_=xt[:, j, :], func=AF.Square, accum_out=W["ss"][:, j:j + 1]),
                         [bxt], [B["xnb"], B["ss"]])
                S.op("dve", "tensor_scalar", dict(out=W["rstd"][:], in0=W["ss"][:], scalar1=1.0 / D, scalar2=RMS_EPS, op0=ALU.mult, op1=ALU.add),
                     [B["ss"]], [B["rstd"]])
                S.op("act", "activation", dict(out=W["rstd"][:], in_=W["rstd"][:], func=AF.Sqrt), [B["rstd"]], [B["rstd"]])
                S.op("dve", "reciprocal", dict(out=W["rstd"][:], in_=W["rstd"][:]), [B["rstd"]], [B["rstd"]])

            def head_sub_a(it, j):
                par = it % 2
                xt, bxt = W["xt%d" % par], B["xt%d" % par]
                xn = "xnb" if j % 2 == 0 else "xnb1"
                S.op("dve", "tensor_scalar", dict(out=W[xn][:], in0=xt[:, j, :], scalar1=W["rstd"][:, j:j + 1], scalar2=None, op0=ALU.mult),
                     [bxt, B["rstd"]], [B[xn]])

            def head_sub_b(it, j):
                par = it % 2
                xT, bxT = W["xT%d" % par], B["xT%d" % par]
                pT = PB[4][:].bitcast(BF16)
                xn = "xnb" if j % 2 == 0 else "xnb1"
                for c in range(8):
                    S.op("pe", "transpose", dict(out=pT[:, c * 128:(c + 1) * 128], in_=W[xn][:, c * 128:(c + 1) * 128], identity=C["identb"][:]),
                         [B[xn], bC], [bPB[4]])
                S.op("act", "activation", dict(out=xT[:, :, j * 128:(j + 1) * 128], in_=pT[:, 0:1024].rearrange("p (c t) -> p c t", t=128),
                                               func=AF.Copy), [bPB[4]], [bxT])

            def ffn_gen():
                head_dma(0)
                head_load(0)
                for j in range(SUBF):
                    head_sub_a(0, j)
                    head_sub_b(0, j)
                    yield
                for it in range(NTF):
                    tok0 = it * TF
                    par = it % 2
                    xt, bxt = W["xt%d" % par], B["xt%d" % par]
                    xT, bxT = W["xT%d" % par], B["xT%d" % par]
                    for fc in range(NFC):
                        wk = "wgu%d" % (fc % 3)
                        S.dma(W[wk][:], wguB[l][fc], reads=bw["wgu"], writes=[B[wk]])
                        if fc < 11:
                            for f2 in (2 * fc, 2 * fc + 1):
                                S.dma(W["Wd"][:, f2, :], wdB[l][:, f2 * D:(f2 + 1) * D], reads=bw["wd"], writes=[bWd[f2]])
                        if it + 1 < NTF:
                            if fc == 1:
                                head_dma(it + 1)
                            if fc == 9:
                                head_load(it + 1)
                            if fc in (10, 11):
                                head_sub_a(it + 1, fc - 10)
                            if fc in (14, 16):
                                head_sub_a(it + 1, 2 + (fc - 14) // 2)
                            if fc in (13, 15, 17, 19):
                                head_sub_b(it + 1, (fc - 13) // 2)
                        pg = (fc % 2) * 2
                        pu = pg + 1
                        for c in range(8):
                            S.op("pe", "matmul", dict(out=PB[pg][:, :], lhsT=W[wk][:, c, 0:128], rhs=xT[:, c, :], start=(c == 0), stop=(c == 7)),
                                 [B[wk], bxT], [bPB[pg]])
                        for c in range(8):
                            S.op("pe", "matmul", dict(out=PB[pu][:, :], lhsT=W[wk][:, c, 128:256], rhs=xT[:, c, :], start=(c == 0), stop=(c == 7)),
                                 [B[wk], bxT], [bPB[pu]])
                        gc = W["gc"]
                        S.op("act", "activation", dict(out=gc[:, 2:2 + TF], in_=PB[pg][:, :], func=AF.Copy), [bPB[pg]], [B["gc"]])
                        S.op("pool", "tensor_copy", dict(out=gc[:, 0:2], in_=W["gcar"][:, fc, :]), [B["gcar"]], [B["gc"]])
                        S.op("pool", "tensor_scalar", dict(out=W["c1"][:], in0=gc[:, 0:TF], scalar1=cv[:, CV_CW + fc:CV_CW + fc + 1],
                                                           scalar2=cv[:, CV_CB + fc:CV_CB + fc + 1], op0=ALU.mult, op1=ALU.add), [B["gc"], B["cv"]], [B["c1"]])
                        S.op("dve", "scalar_tensor_tensor", dict(out=W["c2"][:], in0=gc[:, 1:1 + TF], scalar=cv[:, CV_CW + NFC + fc:CV_CW + NFC + fc + 1],
                                                                  in1=W["c1"][:], op0=ALU.mult, op1=ALU.add), [B["gc"], B["cv"], B["c1"]], [B["c2"]])
                        S.op("dve", "scalar_tensor_tensor", dict(out=W["c3"][:], in0=gc[:, 2:2 + TF], scalar=cv[:, CV_CW + 2 * NFC + fc:CV_CW + 2 * NFC + fc + 1],
                                                                  in1=W["c2"][:], op0=ALU.mult, op1=ALU.add), [B["gc"], B["cv"], B["c2"]], [B["c3"]])
                        S.op("pool", "tensor_copy", dict(out=W["gcar"][:, fc, :], in_=gc[:, TF:TF + 2]), [B["gc"]], [B["gcar"]])
                        S.op("act", "activation", dict(out=W["sl"][:], in_=W["c3"][:], func=AF.Silu), [B["c3"]], [B["sl"]])
                        S.op("dve", "tensor_tensor", dict(out=W["hT"][:, fc, :], in0=PB[pu][:, :], in1=W["sl"][:], op=ALU.mult), [bPB[pu], B["sl"]], [B["hT"]])
                        yield
                    kk = 0
                    for j in range(SUBF):
                        for nh in range(2):
                            k = 5 + (kk % 3)
                            kk += 1
                            for fc in range(NFC):
                                S.op("pe", "matmul", dict(out=PB[k][:, :], lhsT=W["hT"][:, fc, j * 128:(j + 1) * 128], rhs=W["Wd"][:, fc, nh * 512:(nh + 1) * 512],
                                                          start=(fc == 0), stop=(fc == NFC - 1)), [B["hT"], bWd[fc]], [bPB[k]])
                            S.op("dve", "tensor_tensor", dict(out=W["xo"][:, j, nh * 512:(nh + 1) * 512], in0=PB[k][:, :], in1=xt[:, j, nh * 512:(nh + 1) * 512],
                                                              op=ALU.add), [bPB[k], bxt], [B["xo"]])
                            yield
                    if not last:
                        S.dma(xs[tok0:tok0 + TF, :].rearrange("(j p) d -> p j d", p=128), W["xo"][:], reads=[B["xo"]], writes=[b_xs[2 * it], b_xs[2 * it + 1]], eng="sp")
                    else:
                        for j in range(SUBF):
                            S.op("act", "activation", dict(out=W["xnb"][:], in_=W["xo"][:, j, :], func=AF.Square, accum_out=W["ss2"][:, j:j + 1]),
                                 [B["xo"]], [B["xnb"], B["ss2"]])
                        S.op("dve", "tensor_scalar", dict(out=W["rstd2"][:], in0=W["ss2"][:], scalar1=1.0 / D, scalar2=RMS_EPS, op0=ALU.mult, op1=ALU.add),
                             [B["ss2"]], [B["rstd2"]])
                        S.op("act", "activation", dict(out=W["rstd2"][:], in_=W["rstd2"][:], func=AF.Sqrt), [B["rstd2"]], [B["rstd2"]])
                        S.op("dve", "reciprocal", dict(out=W["rstd2"][:], in_=W["rstd2"][:]), [B["rstd2"]], [B["rstd2"]])
                        for j in range(SUBF):
                            S.op("dve", "scalar_tensor_tensor", dict(out=W["xo"][:, j, :], in0=W["xo"][:, j, :], scalar=W["rstd2"][:, j:j + 1], in1=W["fingb"][:],
                                                                      op0=ALU.mult, op1=ALU.mult), [B["xo"], B["rstd2"], B["fingb"]], [B["xo"]])
                        S.dma(out_d[tok0:tok0 + TF, :].rearrange("(j p) d -> p j d", p=128), W["xo"][:], reads=[B["xo"]], writes=[b_out], eng="sp")

            if l + 1 < L:
                Wp = tiles(st, pre_tiles(2))
                drive([ffn_gen(), prepass_gen(l + 1, Wp, "sp", True, load_eng="act", engs=("dve",), NSTG=2)], weights=[6, 1])
            else:
                drive([ffn_gen()])
        S.barrier()

    nlayers = dbg.get("_layers", L) if dbg else L
    stop_after = dbg.get("_stop", None) if dbg else None
    for l in range(nlayers):
        if l == 0:
            prepass(l)
        if stop_after == ("prepass", l):
            break
        if l == 0:
            mixer(l, x_in, lambda it: b_x)
        else:
            mixer(l, xs, lambda it: b_xs[it])
        if stop_after == ("mixer", l):
            break
        ffn(l, last=(l == L - 1))
    S.finish()
    cst.close()
    return nc, S


def _colmajor(v):
    n = v.shape[0] // 128
    return np.ascontiguousarray(v.reshape(n, 128).T)


def host_layout(inp):
    f = np.float32
    d = {}
    w_in = inp["w_in"]
    d["winL"] = np.ascontiguousarray(w_in.reshape(L, 8, 128, NIN).transpose(0, 2, 1, 3))
    w_out = inp["w_out"]
    d["woAL"] = np.ascontiguousarray(w_out[:, :512].reshape(L, 8, 64, D).transpose(0, 2, 1, 3))
    d["woRL"] = np.ascontiguousarray(w_out[:, 512:].reshape(L, 4, 128, D).transpose(0, 2, 1, 3))
    wg = inp["ffn_w_gate"].reshape(L, 8, 128, NFC, 128)
    wu = inp["ffn_w_up"].reshape(L, 8, 128, NFC, 128)
    wgu = np.stack([wg, wu], axis=4)
    d["wguL"] = np.ascontiguousarray(wgu.reshape(L, 8, 128, 2, 11 * 256))
    d["wdL"] = np.ascontiguousarray(inp["ffn_w_down"].reshape(L, NFC, 128, D).transpose(0, 2, 1, 3).reshape(L, 128, NFC * D))
    d["w2a2L"] = np.ascontiguousarray(np.concatenate([inp["w2"], inp["a2"]], axis=1))
    d["g2L"] = np.ascontiguousarray(inp["g2"])
    d["v1L"] = np.ascontiguousarray(inp["v1"][0].reshape(4, 128, 32).transpose(1, 0, 2))
    d["v2L"] = np.ascontiguousarray(inp["v2"][0])
    cv = np.zeros((L, 128, NCV), f)
    for l in range(L):
        cv[l, :, CV_G1:CV_G1 + 8] = _colmajor(inp["norm1_g"][l])
        cv[l, :, CV_G2:CV_G2 + 8] = _colmajor(inp["norm2_g"][l])
        cv[l, :, CV_MU:CV_MU + 14] = _colmajor(inp["shift_mu"][l])
        for nm, o in (("w0", CV_W0), ("a0", CV_A0), ("k_k", CV_KK), ("k_a", CV_KA), ("r_k", CV_RK), ("lnx_g", CV_LG), ("lnx_b", CV_LB)):
            cv[l, :, o:o + 4] = _colmajor(inp[nm][l])
        if l > 0:
            cv[l, :, CV_V0:CV_V0 + 4] = _colmajor(inp["v0"][l - 1])
        for j in range(3):
            cv[l, :, CV_CW + j * NFC:CV_CW + (j + 1) * NFC] = _colmajor(inp["conv_w"][l, j])
        cv[l, :, CV_CB:CV_CB + NFC] = _colmajor(inp["conv_b"][l])
    d["cvL"] = cv
    d["sinkL"] = np.ascontiguousarray(inp["attn_sinks"].reshape(L, 1, 8))
    d["fing"] = np.ascontiguousarray(inp["final_g"].reshape(1, D))
    d["c_ident"] = np.eye(128, dtype=f)
    s = np.arange(64)[:, None]
    t = np.arange(64)[None, :]
    strict = (t > s).astype(f)
    incl = (t >= s).astype(f)
    m256 = np.concatenate([strict, incl, strict, incl], axis=1)
    d["c_maskA"] = np.ascontiguousarray(np.concatenate([m256, m256], axis=1))
    rm = np.ones((128, TM), f)
    rm[:, ::CH] = 0.0
    d["c_reset"] = rm
    d["c_identI8"] = np.ascontiguousarray(np.tile(np.eye(64, dtype=f), (1, 8)))
    bo = np.zeros((128, 128), f)
    bo[:64, :64] = 1.0
    bo[64:, 64:] = 1.0
    d["c_bones"] = bo
    s = np.arange(128)[:, None]
    t = np.arange(128)[None, :]
    NEG = -30000.0
    m_prev = np.where(s > t, 0.0, NEG).astype(f)
    m_cur = np.where(s <= t, 0.0, NEG).astype(f)
    am = np.stack([np.tile(m_prev, (1, 4)), np.tile(m_cur, (1, 4))], axis=1)
    d["c_amask"] = np.ascontiguousarray(am)
    slopes = (2.0 ** (-8.0 * np.arange(1, 9) / 8)).astype(f)
    tt = (np.arange(TM) % 128).astype(f)
    qa = np.zeros((3, 8, TM), f)
    qa[0] = slopes[:, None]
    qa[1] = -slopes[:, None] * tt[None, :]
    qa[2] = -128.0 * slopes[:, None]
    d["c_qaug"] = qa
    ka = np.zeros((3, 2, 128 + TM), f)
    ka[0] = (np.arange(128 + TM) % 128).astype(f)[None, :]
    ka[1] = 1.0
    ka[2] = 1.0
    d["c_kaug"] = ka
    return d


_CACHE = {}


def kernel(**inputs):
    inp = {k: np.asarray(v, dtype=np.float32) for k, v in inputs.items()}
    shared = host_layout(inp)
    if "nc" not in _CACHE:
        _CACHE["nc"] = build_program()[0]
    nc = _CACHE["nc"]
    x = inp["x"]
    in_maps = []
    for b in range(8):
        m = dict(shared)
        m["x"] = np.ascontiguousarray(x[b])
        in_maps.append(m)
    res = run_bass_kernel_spmd(nc, in_maps, core_ids=list(range(8)))
    out = np.stack([np.asarray(res.results[b]["out"], dtype=np.float32) for b in range(8)], axis=0)
    return out
```

```python
import os as _os
import numpy as np
from contextlib import ExitStack
import concourse.bass as bass
import concourse.mybir as mybir
from concourse.bass_utils import run_bass_kernel_spmd
from concourse.alu_op_type import AluOpType as ALU

AF = mybir.ActivationFunctionType
F32 = mybir.dt.float32
BF16 = mybir.dt.bfloat16

T = 4096
D = 1024
L = 2
NIN = 2560
FF = 2816
NFC = 22
TM = 256
NTM = T // TM
SUBM = TM // 128
CH = 64
NCH = TM // CH
TF = 512
NTF = T // TF
SUBF = TF // 128
NCV = 150
CV_G1, CV_G2, CV_MU, CV_W0, CV_A0, CV_KK, CV_KA, CV_RK, CV_LG, CV_LB, CV_V0, CV_CW, CV_CB = \
    0, 8, 16, 30, 34, 38, 42, 46, 50, 54, 58, 62, 128
RMS_EPS = 1e-5
LNX_EPS = 64e-5
EM05 = float(np.exp(-0.5))


class Buf:
    __slots__ = ("name", "w", "r")

    def __init__(self, name):
        self.name = name
        self.w = None
        self.r = {}


class Sched:
    NDMA = 32

    def __init__(self, nc):
        self.nc = nc
        self.names = ["pe", "dve", "act", "pool", "sp"]
        self.engobj = {"pe": nc.tensor, "dve": nc.vector, "act": nc.scalar, "pool": nc.gpsimd, "sp": nc.sync}
        self.count = {e: 0 for e in self.names}
        self.waited = {e: {} for e in self.names}
        self.sem_by_id = {}
        for e in self.names:
            self.sem_by_id[("e", e)] = nc.alloc_semaphore("es_" + e)
        for i in range(self.NDMA):
            self.sem_by_id[("d", i)] = nc.alloc_semaphore("ds_%d" % i)
        self.duse = [0] * self.NDMA
        self.drr = 0
        self.ninst = 0

    def _deps(self, eng, reads, writes):
        deps = []
        for b in reads:
            if b.w is not None:
                deps.append(b.w)
        for b in writes:
            if b.w is not None:
                deps.append(b.w)
            deps.extend(b.r.values())
        waits = []
        for (sid, val, src) in deps:
            if src == "pe" and eng == "pe":
                continue
            if self.waited[eng].get(sid, 0) >= val:
                continue
            self.waited[eng][sid] = val
            waits.append((sid, val))
        return waits

    def _commit(self, eng, ev, reads, writes):
        key = eng if ev[2] != "dma" else ev[0]
        for b in reads:
            b.r[key] = ev
        for b in writes:
            b.w = ev
            b.r = {}

    def _emit(self, name, waits, meth, kw, inc):
        eng = self.engobj[name]
        for (sid, val) in waits:
            eng.wait_ge(self.sem_by_id[sid], val)
            self.ninst += 1
        if meth is not None:
            ins = getattr(eng, meth)(**kw)
            ins.then_inc(self.sem_by_id[inc[0]], inc[1])
            self.ninst += 1

    def op(self, eng, meth, kw, reads=(), writes=()):
        waits = self._deps(eng, reads, writes)
        self.count[eng] += 1
        ev = (("e", eng), self.count[eng], eng)
        self._emit(eng, waits, meth, kw, (("e", eng), 1))
        self._commit(eng, ev, reads, writes)

    def dma(self, out, in_, reads=(), writes=(), eng="sp"):
        waits = self._deps(eng, reads, writes)
        j = self.drr
        self.drr = (j + 1) % self.NDMA
        sid = ("d", j)
        if self.duse[j] > 0:
            val = 16 * self.duse[j]
            if self.waited[eng].get(sid, 0) < val:
                self.waited[eng][sid] = val
                waits.append((sid, val))
        self.duse[j] += 1
        ev = (sid, 16 * self.duse[j], "dma")
        self._emit(eng, waits, "dma_start", dict(out=out, in_=in_), (sid, 16))
        self._commit(eng, ev, reads, writes)

    def barrier(self):
        evs = []
        for j in range(self.NDMA):
            if self.duse[j] > 0:
                evs.append((("d", j), 16 * self.duse[j]))
        for e in self.names:
            if self.count[e] > 0:
                evs.append((("e", e), self.count[e]))
        for e in self.names:
            w = []
            for (sid, val) in evs:
                if self.waited[e].get(sid, 0) < val:
                    self.waited[e][sid] = val
                    w.append((sid, val))
            self._emit(e, w, None, None, None)

    def finish(self):
        self.barrier()


def build_program(dbg=None):
    nc = bass.Bass("TRN2", target_bir_lowering=False)
    S = Sched(nc)

    def din(name, shape, dt=F32):
        return nc.dram_tensor(name, list(shape), dt, kind="ExternalInput").ap()

    x_in = din("x", [T, D])
    winL = din("winL", [L, 128, 8, NIN])
    woAL = din("woAL", [L, 64, 8, D])
    woRL = din("woRL", [L, 128, 4, D])
    wguL = din("wguL", [L, 8, 128, 2, 11 * 256])
    wdL = din("wdL", [L, 128, NFC * D])
    w2a2L = din("w2a2L", [L, 128, 512])
    g2L = din("g2L", [L, 128, 512])
    v1L = din("v1L", [128, 4, 32])
    v2L = din("v2L", [32, 512])
    cvL = din("cvL", [L, 128, NCV])
    sinkL = din("sinkL", [L, 1, 8])
    fing = din("fing", [1, D])
    c_ident = din("c_ident", [128, 128])
    c_maskA = din("c_maskA", [64, 512])
    c_reset = din("c_reset", [128, TM])
    c_identI8 = din("c_identI8", [64, 512])
    c_bones = din("c_bones", [128, 128])
    c_amask = din("c_amask", [128, 2, 512])
    c_qaug = din("c_qaug", [3, 8, TM])
    c_kaug = din("c_kaug", [3, 2, 128 + TM])
    out_d = nc.dram_tensor("out", [T, D], F32, kind="ExternalOutput").ap()
    xs = nc.dram_tensor("xs", [T, D], F32, kind="Internal").ap()
    vf = nc.dram_tensor("vf", [128, 4, T], F32, kind="Internal").ap()
    winB = [nc.dram_tensor("winB%d" % l, [128, 8, NIN], BF16, kind="Internal").ap() for l in range(L)]
    woAB = [nc.dram_tensor("woAB%d" % l, [64, 8 * D], BF16, kind="Internal").ap() for l in range(L)]
    woRB = [nc.dram_tensor("woRB%d" % l, [128, 4 * D], BF16, kind="Internal").ap() for l in range(L)]
    wguB = [nc.dram_tensor("wguB%d" % l, [NFC, 128, 8, 256], BF16, kind="Internal").ap() for l in range(L)]
    wdB = [nc.dram_tensor("wdB%d" % l, [128, NFC * D], BF16, kind="Internal").ap() for l in range(L)]
    dbg_out = {}
    if dbg:
        for nm, shp in dbg.items():
            if nm.startswith("_"):
                continue
            dbg_out[nm] = nc.dram_tensor("dbg_" + nm, list(shp), F32, kind="ExternalOutput").ap()

    b_x = Buf("x")
    b_xs = [Buf("xs%d" % i) for i in range(NTM)]
    b_out = Buf("out")
    b_vf = [Buf("vf%d" % i) for i in range(NTM)]
    b_wscr = {}

    PB = [nc.alloc_psum_tensor("pb%d" % i, [128, 512], F32) for i in range(8)]
    bPB = [Buf("pb%d" % i) for i in range(8)]

    uid = {"i": 0}

    def tiles(st, specs):
        res = {}
        uid["i"] += 1
        for nm, shp, dt in specs:
            res[nm] = st.enter_context(nc.sbuf_tensor("%s_%d" % (nm, uid["i"]), list(shp), dt))
        if dbg and dbg.get("_mem"):
            print("SBUF after tiles group", uid["i"], nc.bytes_allocated(res[nm].space), "of", nc.space_capacity(res[nm].space))
        return res

    rr = {"i": 0}

    def ew():
        rr["i"] += 1
        return ("dve", "pool")[rr["i"] % 2]

    cst = ExitStack()
    C = tiles(cst, [
        ("identf", [128, 128], F32), ("identb", [128, 128], BF16), ("bones", [128, 128], F32),
        ("maskA", [64, 256], F32), ("resetm", [128, TM], F32), ("identI8", [64, 512], BF16),
        ("amask", [128, 2, 512], BF16),
    ])
    bC = Buf("consts")
    S.dma(C["identf"][:], c_ident, writes=[bC])
    S.dma(C["bones"][:], c_bones, writes=[bC])
    S.dma(C["maskA"][:], c_maskA[:, 0:256], writes=[bC])
    S.dma(C["resetm"][:], c_reset, writes=[bC])
    with ExitStack() as st0:
        Cs = tiles(st0, [("i8f", [64, 512], F32), ("amf", [128, 2, 512], F32)])
        bCs = Buf("cstage")
        S.dma(Cs["i8f"][:], c_identI8, writes=[bCs])
        S.dma(Cs["amf"][:], c_amask, writes=[bCs])
        S.op("dve", "tensor_copy", dict(out=C["identI8"][:], in_=Cs["i8f"][:]), [bCs], [bC])
        S.op("dve", "tensor_copy", dict(out=C["amask"][:], in_=Cs["amf"][:]), [bCs], [bC])
        S.barrier()
    S.op("dve", "tensor_copy", dict(out=C["identb"][:], in_=C["identf"][:]), [bC], [bC])

    def drive(gens, weights=None):
        items = [[g, (weights[i] if weights else 1)] for i, g in enumerate(gens) if g is not None]
        while items:
            for itm in list(items):
                for _ in range(itm[1]):
                    try:
                        next(itm[0])
                    except StopIteration:
                        items.remove(itm)
                        break

    def pre_tiles(nstg):
        return ([("stg%d" % i, [128, FF], F32) for i in range(nstg)] + [("ob%d" % i, [128, FF], BF16) for i in range(nstg)]
                + [("cvp", [128, NCV], F32)])

    def prepass_gen(l, W, store_eng, lazy_store, load_eng="sp", engs=("dve", "pool"), NSTG=3):
        bs = [Buf("stg%d" % i) for i in range(NSTG)]
        bo = [Buf("ob%d" % i) for i in range(NSTG)]
        bcv = Buf("cvp")
        S.dma(W["cvp"][:], cvL[l], writes=[bcv])
        bw = {k: [] for k in ("win", "woA", "woR", "wgu", "wd")}
        b_wscr[l] = bw
        pieces = []
        for c in range(8):
            pieces.append((winL[l, :, c, :], winB[l][:, c, :], 128, NIN, W["cvp"][:, CV_G1 + c:CV_G1 + c + 1], bw["win"], None))
        woA_src = woAL[l].rearrange("p h n -> p (h n)")
        for i in range(4):
            pieces.append((woA_src[:, i * 2048:(i + 1) * 2048], woAB[l][:, i * 2048:(i + 1) * 2048], 64, 2048, None, bw["woA"], None))
        woR_src = woRL[l].rearrange("p h n -> p (h n)")
        for i in range(2):
            pieces.append((woR_src[:, i * 2048:(i + 1) * 2048], woRB[l][:, i * 2048:(i + 1) * 2048], 128, 2048, None, bw["woR"], None))
        for c in range(8):
            for hf in range(2):
                dst = wguB[l][hf * 11:(hf + 1) * 11, :, c, :].rearrange("f p j -> p f j")
                pieces.append((wguL[l, c, :, hf, :], dst, 128, FF, W["cvp"][:, CV_G2 + c:CV_G2 + c + 1], bw["wgu"], 256))
        for i in range(8):
            pieces.append((wdL[l][:, i * FF:(i + 1) * FF], wdB[l][:, i * FF:(i + 1) * FF], 128, FF, None, bw["wd"], None))
        pending = None

        def store(pd):
            (dst, ob_ap, bok, wb, dv3) = pd
            if dv3 is None:
                S.dma(dst, ob_ap, reads=[bok], writes=[wb], eng=store_eng)
            else:
                S.dma(dst, ob_ap.rearrange("p (a b) -> p a b", b=dv3), reads=[bok], writes=[wb], eng=store_eng)

        for i, (src, dst, np_, n, scale, wbl, dv3) in enumerate(pieces):
            wb = Buf("wpiece")
            wbl.append(wb)
            k = i % NSTG
            stg = W["stg%d" % k]
            ob = W["ob%d" % k]
            if pending is not None:
                store(pending)
                pending = None
            S.dma(stg[0:np_, 0:n], src, writes=[bs[k]], eng=load_eng)
            e = engs[i % len(engs)]
            if scale is None:
                S.op(e, "tensor_copy", dict(out=ob[0:np_, 0:n], in_=stg[0:np_, 0:n]), [bs[k]], [bo[k]])
            else:
                S.op(e, "tensor_scalar", dict(out=ob[0:np_, 0:n], in0=stg[0:np_, 0:n], scalar1=scale, scalar2=None, op0=ALU.mult),
                     [bs[k], bcv], [bo[k]])
            pd = (dst, ob[0:np_, 0:n], bo[k], wb, dv3)
            if lazy_store:
                pending = pd
            else:
                store(pd)
            yield
        if pending is not None:
            store(pending)

    def prepass(l):
        with ExitStack() as st:
            W = tiles(st, pre_tiles(4))
            drive([prepass_gen(l, W, "act", False, engs=("dve",), NSTG=4)])
        S.barrier()

    def mixer(l, src, b_src_fn):
        with ExitStack() as st:
            W = tiles(st, [
                ("WinT", [128, 8, NIN], BF16), ("WoA", [64, 8, D], BF16), ("WoR", [128, 4, D], BF16),
                ("w2a2b", [128, 512], BF16), ("g2b", [128, 512], BF16),
                ("v1b", [128, 4, 32], BF16), ("v2b", [32, 512], BF16),
                ("cv", [128, NCV], F32), ("omka", [128, 4], F32), ("esk", [64, 8], F32), ("onesb", [128, 64], BF16),
                ("ulast", [128, 14], F32), ("H", [128, 4, 64], F32), ("Hb", [128, 4, 64], BF16),
                ("kT", [67, 2, 128 + TM], BF16), ("Vaug", [128, 1 + SUBM, 2, 65], BF16), ("qT", [67, 8, TM], BF16),
                ("xt", [128, SUBM, D], F32),
                ("ss", [128, SUBM], F32), ("rstd", [128, SUBM], F32),
                ("xnb", [128, D], BF16), ("xT", [128, 8, TM], BF16),
                ("u", [128, 1 + TM], F32), ("dsh", [128, TM], F32),
                ("twa", [128, TM], BF16), ("sgd0", [128, TM], BF16), ("sgd1", [128, TM], BF16),
                ("vall", [128, 4, TM], F32), ("vbf", [128, 4, TM], BF16), ("lvb", [32, TM], BF16),
                ("rk", [128, 2, TM], F32),
                ("tbig", [128, 12, TM], F32),
                ("gC0", [128, 4, NCH], F32), ("gC1", [128, 4, NCH], F32),
                ("AR0", [128, 4, NCH, 2, CH], BF16), ("BT0", [128, 4, TM], BF16), ("KT0", [128, 4, TM], BF16),
                ("AR1", [128, 4, NCH, 2, CH], BF16), ("BT1", [128, 4, TM], BF16), ("KT1", [128, 4, TM], BF16),
                ("bp", [128, TM], BF16), ("kpp", [128, TM], BF16),
                ("TM30", [64, 8, NCH, 3, 64], BF16), ("TM31", [64, 8, NCH, 3, 64], BF16),
                ("bonus0", [128, 4, TM], F32), ("bonus1", [128, 4, TM], F32), ("yT", [128, 4, TM], F32),
                ("Amat", [64, 8, 256], BF16), ("Nm", [64, 8, 64], BF16), ("NTm", [64, 8, 64], BF16),
                ("Nm2", [64, 8, 64], BF16), ("NTm2", [64, 8, 64], BF16),
                ("Pm", [64, 8, 64], BF16), ("Pm2", [64, 8, 64], BF16), ("Zb", [64, 8, 64], BF16), ("Ub", [64, 8, 64], BF16),
                ("sT", [128, 1, 512], F32), ("PT", [128, 2, 512], BF16),
                ("den", [65, 512], F32), ("attT0", [64, 8, TM], BF16), ("attT1", [64, 8, TM], BF16), ("rwT", [128, 4, TM], BF16),
                ("xos0", [128, 512], F32), ("xos1", [128, 512], F32),
                ("pp0", [128, TM], F32), ("pp1", [128, TM], F32),
            ])
            B = {k: Buf(k) for k in W}
            for i in range(12):
                W["t%d" % i] = W["tbig"][:, i, :]
                B["t%d" % i] = Buf("t%d" % i)
            xtf = W["xt"][:].rearrange("p j d -> p (j d)")
            W["w2a2f"] = xtf[:, 0:512]
            W["g2f"] = xtf[:, 512:1024]
            W["v1f"] = xtf[:, 1024:1152].rearrange("p (a b) -> p a b", b=32)
            W["v2f"] = xtf[0:32, 1152:1664]
            W["augf"] = xtf[0:67, 0:8 * TM].rearrange("p (a b) -> p a b", b=TM)
            tbf = W["tbig"][:].rearrange("p a b -> p (a b)")
            W["kaugf"] = tbf[0:67, 0:2 * (128 + TM)].rearrange("p (a b) -> p a b", b=128 + TM)
            for nm in ("w2a2f", "g2f", "v1f", "v2f", "augf"):
                B[nm] = B["xt"]
            B["kaugf"] = B["t0"]
            byT = [Buf("yT%d" % i) for i in range(4)]
            W["OTc"] = W["den"][0:64, :]
            B["OTc"] = Buf("OTc")
            W["Z2b"] = W["Ub"][:].rearrange("p h c -> p (h c)")
            B["Z2b"] = B["Ub"]
            bw = b_wscr[l]
            S.dma(W["WinT"][:], winB[l], reads=bw["win"], writes=[B["WinT"]])
            S.dma(W["WoA"][:].rearrange("p h n -> p (h n)"), woAB[l], reads=bw["woA"], writes=[B["WoA"]])
            S.dma(W["WoR"][:].rearrange("p h n -> p (h n)"), woRB[l], reads=bw["woR"], writes=[B["WoR"]])
            S.dma(W["w2a2f"][:], w2a2L[l], writes=[B["w2a2f"]])
            S.dma(W["g2f"][:], g2L[l], writes=[B["g2f"]])
            S.dma(W["cv"][:], cvL[l], writes=[B["cv"]])
            S.op("dve", "tensor_copy", dict(out=W["w2a2b"][:], in_=W["w2a2f"][:]), [B["w2a2f"]], [B["w2a2b"]])
            S.op("dve", "tensor_copy", dict(out=W["g2b"][:], in_=W["g2f"][:]), [B["g2f"]], [B["g2b"]])
            if l > 0:
                S.dma(W["v1f"][:], v1L, writes=[B["v1f"]])
                S.dma(W["v2f"][:], v2L, writes=[B["v2f"]])
                S.op("dve", "tensor_copy", dict(out=W["v1b"][:], in_=W["v1f"][:]), [B["v1f"]], [B["v1b"]])
                S.op("dve", "tensor_copy", dict(out=W["v2b"][:], in_=W["v2f"][:]), [B["v2f"]], [B["v2b"]])
            cv = W["cv"]

            def col(base, i):
                return cv[:, base + i:base + i + 1]
            S.op("dve", "tensor_scalar", dict(out=W["omka"][:], in0=cv[:, CV_KA:CV_KA + 4], scalar1=-1.0, scalar2=1.0,
                                              op0=ALU.mult, op1=ALU.add), [B["cv"]], [B["omka"]])
            S.dma(W["esk"][:], sinkL[l].broadcast_to([64, 8]), writes=[B["esk"]])
            S.op("act", "activation", dict(out=W["esk"][:], in_=W["esk"][:], func=AF.Exp), [B["esk"]], [B["esk"]])
            S.op("pool", "memset", dict(ap=W["onesb"][:], constant=1.0), [], [B["onesb"]])
            S.dma(W["augf"][64:67, :, :], c_qaug, writes=[B["augf"]])
            S.dma(W["kaugf"][64:67, :, :], c_kaug, writes=[B["kaugf"], B["t1"], B["t2"], B["t3"]])
            S.op("act", "activation", dict(out=W["qT"][64:67, :, :], in_=W["augf"][64:67, :, :], func=AF.Copy), [B["augf"]], [B["qT"]])
            S.op("act", "activation", dict(out=W["kT"][64:67, :, :], in_=W["kaugf"][64:67, :, :], func=AF.Copy), [B["kaugf"], B["t1"], B["t2"], B["t3"]], [B["kT"]])
            S.op("pool", "memset", dict(ap=W["ulast"][:], constant=0.0), [], [B["ulast"]])
            S.op("pool", "memset", dict(ap=W["H"][:], constant=0.0), [], [B["H"]])
            S.op("pool", "memset", dict(ap=W["Hb"][:], constant=0.0), [], [B["Hb"]])
            S.op("pool", "memset", dict(ap=W["Vaug"][:], constant=1.0), [], [B["Vaug"]])
            S.op("pool", "memset", dict(ap=W["kT"][0:64, :, :], constant=0.0), [], [B["kT"]])

            pj = {"i": 0, "b": 0}

            def fbank():
                pj["i"] += 1
                return pj["i"] % 2

            def bbank():
                pj["b"] += 1
                return 3 + pj["b"] % 2

            def proj_fm(col0, ncols):
                k = fbank()
                for c in range(8):
                    S.op("pe", "matmul", dict(out=PB[k][0:ncols, 0:TM], lhsT=W["WinT"][:, c, col0:col0 + ncols], rhs=W["xT"][:, c, :],
                                              start=(c == 0), stop=(c == 7)), [B["WinT"], B["xT"]], [bPB[k]])
                return k

            def shift_chunk(k, ci, dst, bdst):
                u = W["u"]
                S.op("act", "activation", dict(out=u[:, 1:1 + TM], in_=PB[k][:, 0:TM], func=AF.Copy), [bPB[k]], [B["u"]])
                S.op("pool", "tensor_copy", dict(out=u[:, 0:1], in_=W["ulast"][:, ci:ci + 1]), [B["ulast"]], [B["u"]])
                S.op("dve", "tensor_tensor", dict(out=W["dsh"][:], in0=u[:, 0:TM], in1=u[:, 1:1 + TM], op=ALU.subtract), [B["u"]], [B["dsh"]])
                S.op("dve", "scalar_tensor_tensor", dict(out=dst, in0=W["dsh"][:], scalar=col(CV_MU, ci), in1=u[:, 1:1 + TM],
                                                          op0=ALU.mult, op1=ALU.add), [B["dsh"], B["u"], B["cv"]], [bdst])
                S.op("pool", "tensor_copy", dict(out=W["ulast"][:, ci:ci + 1], in_=u[:, TM:TM + 1]), [B["u"]], [B["ulast"]])

            RW0 = 768

            def front(it):
                par = it % 2
                AR, BT, KT, TM3, gCt, bonus, sgd = (W["AR%d" % par], W["BT%d" % par], W["KT%d" % par], W["TM3%d" % par],
                                                    W["gC%d" % par], W["bonus%d" % par], W["sgd%d" % par])
                attT, battT = W["attT%d" % par], B["attT%d" % par]
                bAR, bBT, bKT, bTM3, bgC, bbonus, bsgd = (B["AR%d" % par], B["BT%d" % par], B["KT%d" % par], B["TM3%d" % par],
                                                          B["gC%d" % par], B["bonus%d" % par], B["sgd%d" % par])
                tok0 = it * TM
                bsrc = b_src_fn(it)
                S.dma(W["xt"][:], src[tok0:tok0 + TM, :].rearrange("(j p) d -> p j d", p=128), reads=[bsrc], writes=[B["xt"]])
                for j in range(SUBM):
                    S.op("act", "activation", dict(out=W["xnb"][:], in_=W["xt"][:, j, :], func=AF.Square, accum_out=W["ss"][:, j:j + 1]),
                         [B["xt"]], [B["xnb"], B["ss"]])
                S.op("dve", "tensor_scalar", dict(out=W["rstd"][:], in0=W["ss"][:], scalar1=1.0 / D, scalar2=RMS_EPS, op0=ALU.mult, op1=ALU.add),
                     [B["ss"]], [B["rstd"]])
                S.op("act", "activation", dict(out=W["rstd"][:], in_=W["rstd"][:], func=AF.Sqrt), [B["rstd"]], [B["rstd"]])
                S.op("dve", "reciprocal", dict(out=W["rstd"][:], in_=W["rstd"][:]), [B["rstd"]], [B["rstd"]])
                yield
                for j in range(SUBM):
                    S.op("dve", "tensor_scalar", dict(out=W["xnb"][:], in0=W["xt"][:, j, :], scalar1=W["rstd"][:, j:j + 1], scalar2=None, op0=ALU.mult),
                         [B["xt"], B["rstd"]], [B["xnb"]])
                    pT = PB[2][:].bitcast(BF16)
                    for c in range(8):
                        S.op("pe", "transpose", dict(out=pT[:, c * 128:(c + 1) * 128], in_=W["xnb"][:, c * 128:(c + 1) * 128], identity=C["identb"][:]),
                             [B["xnb"], bC], [bPB[2]])
                    S.op("act", "activation", dict(out=W["xT"][:, :, j * 128:(j + 1) * 128], in_=pT[:, 0:1024].rearrange("p (c t) -> p c t", t=128),
                                                   func=AF.Copy), [bPB[2]], [B["xT"]])
                    yield
                for h in range(8):
                    k = proj_fm(h * 64, 64)
                    S.op("act", "activation", dict(out=W["qT"][0:64, h, :], in_=PB[k][0:64, 0:TM], func=AF.Copy, scale=0.125), [bPB[k]], [B["qT"]])
                    yield
                for kv in range(2):
                    k = proj_fm(512 + kv * 64, 64)
                    S.op("dve", "tensor_copy", dict(out=W["kT"][0:64, kv, 128:128 + TM], in_=PB[k][0:64, 0:TM]), [bPB[k]], [B["kT"]])
                    yield
                for j in range(SUBM):
                    k = fbank()
                    for c in range(8):
                        S.op("pe", "matmul", dict(out=PB[k][:, 0:128], lhsT=W["xT"][:, c, j * 128:(j + 1) * 128], rhs=W["WinT"][:, c, 640:768],
                                                  start=(c == 0), stop=(c == 7)), [B["WinT"], B["xT"]], [bPB[k]])
                    S.op("dve", "tensor_copy", dict(out=W["Vaug"][:, 1 + j, :, 0:64], in_=PB[k][:, 0:128].rearrange("p (a b) -> p a b", b=64)),
                         [bPB[k]], [B["Vaug"]])
                    yield
                k = proj_fm(RW0 + 1536, 128)
                shift_chunk(k, 12, W["t0"], B["t0"])
                S.op("act", "activation", dict(out=W["twa"][0:64, :], in_=W["t0"][0:64, :], func=AF.Tanh), [B["t0"]], [B["twa"]])
                S.op("dve", "tensor_copy", dict(out=W["twa"][64:128, :], in_=W["t0"][64:128, :]), [B["t0"]], [B["twa"]])
                yield
                k = proj_fm(RW0 + 1664, 128)
                shift_chunk(k, 13, W["t0"], B["t0"])
                S.op("act", "activation", dict(out=sgd[:], in_=W["t0"], func=AF.Sigmoid), [B["t0"]], [bsgd])
                yield
                for cc in range(4):
                    k = proj_fm(RW0 + 1024 + cc * 128, 128)
                    shift_chunk(k, 8 + cc, W["vall"][:, cc, :], B["vall"])
                    yield
                if l == 0:
                    S.dma(vf[:, :, tok0:tok0 + TM], W["vall"][:], reads=[B["vall"]], writes=[b_vf[it]], eng="sp")
                else:
                    vfst = bonus
                    S.dma(vfst[:], vf[:, :, tok0:tok0 + TM], reads=[b_vf[it]], writes=[bbonus])
                    S.op("pool", "tensor_copy", dict(out=W["vbf"][:], in_=W["vall"][:]), [B["vall"]], [B["vbf"]])
                    k = fbank()
                    for cc in range(4):
                        S.op("pe", "matmul", dict(out=PB[k][0:32, 0:TM], lhsT=W["v1b"][:, cc, :], rhs=W["vbf"][:, cc, :], start=(cc == 0), stop=(cc == 3)),
                             [B["v1b"], B["vbf"]], [bPB[k]])
                    S.op("act", "activation", dict(out=W["lvb"][:], in_=PB[k][0:32, 0:TM], func=AF.Copy), [bPB[k]], [B["lvb"]])
                    yield
                    for cc in range(4):
                        k = fbank()
                        S.op("pe", "matmul", dict(out=PB[k][:, 0:TM], lhsT=W["v2b"][0:32, cc * 128:(cc + 1) * 128], rhs=W["lvb"][:], start=True, stop=True),
                             [B["v2b"], B["lvb"]], [bPB[k]])
                        S.op("act", "activation", dict(out=W["t0"], in_=PB[k][:, 0:TM], func=AF.Sigmoid, bias=col(CV_V0, cc)), [bPB[k], B["cv"]], [B["t0"]])
                        S.op("dve", "tensor_tensor", dict(out=W["t1"], in0=vfst[:, cc, :], in1=W["vall"][:, cc, :], op=ALU.subtract),
                             [bbonus, B["vall"]], [B["t1"]])
                        S.op("dve", "tensor_tensor", dict(out=W["t1"], in0=W["t1"], in1=W["t0"], op=ALU.mult), [B["t1"], B["t0"]], [B["t1"]])
                        S.op("dve", "tensor_tensor", dict(out=W["vall"][:, cc, :], in0=W["vall"][:, cc, :], in1=W["t1"], op=ALU.add),
                             [B["vall"], B["t1"]], [B["vall"]])
                        yield
                S.op("pool", "tensor_copy", dict(out=W["vbf"][:], in_=W["vall"][:]), [B["vall"]], [B["vbf"]])
                yield
                for cc in range(4):
                    k = proj_fm(RW0 + cc * 128, 128)
                    shift_chunk(k, cc, W["rk"][:, 0, :], B["rk"])
                    yield
                    k = proj_fm(RW0 + 512 + cc * 128, 128)
                    shift_chunk(k, 4 + cc, W["rk"][:, 1, :], B["rk"])
                    yield
                    r_ = W["rk"][:, 0, :]
                    k_ = W["rk"][:, 1, :]
                    v_ = W["vall"][:, cc, :]
                    ld, cl, g, gi, gm1, a, kk, kkn, kp, ba, tA, tB = [W["t%d" % i] for i in range(12)]
                    bl = [B["t%d" % i] for i in range(12)]
                    kw = fbank()
                    S.op("pe", "matmul", dict(out=PB[kw][:, 0:TM], lhsT=W["w2a2b"][0:64, cc * 128:(cc + 1) * 128], rhs=W["twa"][0:64, :], start=True, stop=True),
                         [B["w2a2b"], B["twa"]], [bPB[kw]])
                    ka_ = fbank()
                    S.op("pe", "matmul", dict(out=PB[ka_][:, 0:TM], lhsT=W["w2a2b"][64:128, cc * 128:(cc + 1) * 128], rhs=W["twa"][64:128, :], start=True, stop=True),
                         [B["w2a2b"], B["twa"]], [bPB[ka_]])
                    S.op("act", "activation", dict(out=ld, in_=PB[kw][:, 0:TM], func=AF.Sigmoid, bias=col(CV_W0, cc)), [bPB[kw], B["cv"]], [bl[0]])
                    S.op("act", "activation", dict(out=a, in_=PB[ka_][:, 0:TM], func=AF.Sigmoid, bias=col(CV_A0, cc)), [bPB[ka_], B["cv"]], [bl[5]])
                    S.op("pool", "tensor_scalar", dict(out=kk, in0=k_, scalar1=col(CV_KK, cc), scalar2=None, op0=ALU.mult), [B["rk"], B["cv"]], [bl[6]])
                    S.op("act", "activation", dict(out=tA, in_=kk, func=AF.Square), [bl[6]], [bl[10]])
                    yield
                    S.op("dve", "tensor_scalar", dict(out=kkn, in0=a, scalar1=col(CV_KA, cc), scalar2=W["omka"][:, cc:cc + 1], op0=ALU.mult, op1=ALU.add),
                         [bl[5], B["cv"], B["omka"]], [bl[7]])
                    S.op("pool", "tensor_tensor", dict(out=kp, in0=k_, in1=kkn, op=ALU.mult), [B["rk"], bl[7]], [bl[8]])
                    S.op("dve", "scalar_tensor_tensor", dict(out=ba, in0=r_, scalar=col(CV_RK, cc), in1=kp, op0=ALU.mult, op1=ALU.mult),
                         [B["rk"], B["cv"], bl[8]], [bl[9]])
                    yield
                    S.op("dve", "tensor_tensor_scan", dict(out=cl, data0=C["resetm"][:], data1=ld, initial=0.0, op0=ALU.mult, op1=ALU.add),
                         [bC, bl[0]], [bl[1]])
                    S.op("act", "activation", dict(out=g, in_=cl, func=AF.Exp, scale=-EM05), [bl[1]], [bl[2]])
                    S.op("act", "activation", dict(out=gi, in_=cl, func=AF.Exp, scale=EM05), [bl[1]], [bl[3]])
                    S.op("pool", "tensor_tensor", dict(out=gm1, in0=cl, in1=ld, op=ALU.subtract), [bl[1], bl[0]], [bl[4]])
                    S.op("act", "activation", dict(out=gm1, in_=gm1, func=AF.Exp, scale=-EM05), [bl[4]], [bl[4]])
                    S.op("pool", "tensor_copy", dict(out=gCt[:, cc, :], in_=g.rearrange("p (n t) -> p n t", t=CH)[:, :, CH - 1]), [bl[2]], [bgC])
                    yield
                    k1 = fbank()
                    S.op("pe", "matmul", dict(out=PB[k1][:, 0:TM], lhsT=C["bones"][:], rhs=tA, start=True, stop=True), [bC, bl[10]], [bPB[k1]])
                    k2 = fbank()
                    S.op("pe", "matmul", dict(out=PB[k2][:, 0:TM], lhsT=C["bones"][:], rhs=ba, start=True, stop=True), [bC, bl[9]], [bPB[k2]])
                    S.op("dve", "tensor_scalar", dict(out=tB, in0=PB[k1][:, 0:TM], scalar1=1e-18, scalar2=None, op0=ALU.max), [bPB[k1]], [bl[11]])
                    S.op("act", "activation", dict(out=tB, in_=tB, func=AF.Ln), [bl[11]], [bl[11]])
                    S.op("act", "activation", dict(out=tB, in_=tB, func=AF.Exp, scale=-0.5), [bl[11]], [bl[11]])
                    S.op("dve", "tensor_tensor", dict(out=bonus[:, cc, :], in0=PB[k2][:, 0:TM], in1=v_, op=ALU.mult), [bPB[k2], B["vall"]], [bbonus])
                    yield
                    S.op("dve", "tensor_tensor", dict(out=kkn, in0=kk, in1=tB, op=ALU.mult), [bl[6], bl[11]], [bl[7]])
                    S.op("pool", "tensor_tensor", dict(out=AR[:, cc, :, 1, :], in0=r_.rearrange("p (n t) -> p n t", t=CH),
                                                       in1=g.rearrange("p (n t) -> p n t", t=CH), op=ALU.mult), [B["rk"], bl[2]], [bAR])
                    S.op("dve", "scalar_tensor_tensor", dict(out=AR[:, cc, :, 0, :], in0=kkn.rearrange("p (n t) -> p n t", t=CH), scalar=-1.0,
                                                              in1=gm1.rearrange("p (n t) -> p n t", t=CH), op0=ALU.mult, op1=ALU.mult),
                         [bl[7], bl[4]], [bAR])
                    S.op("dve", "tensor_tensor", dict(out=ba, in0=kkn, in1=a, op=ALU.mult), [bl[7], bl[5]], [bl[9]])
                    S.op("pool", "tensor_tensor", dict(out=BT[:, cc, :], in0=ba, in1=gi, op=ALU.mult), [bl[9], bl[3]], [bBT])
                    S.op("dve", "tensor_tensor", dict(out=KT[:, cc, :], in0=kp, in1=gi, op=ALU.mult), [bl[8], bl[3]], [bKT])
                    yield
                    for n in range(NCH):
                        S.op("dve", "tensor_scalar", dict(out=tB[:, n * CH:(n + 1) * CH], in0=gi[:, n * CH:(n + 1) * CH],
                                                          scalar1=g[:, n * CH + CH - 1:n * CH + CH], scalar2=None, op0=ALU.mult), [bl[3], bl[2]], [bl[11]])
                    S.op("pool", "tensor_tensor", dict(out=W["bp"][:], in0=ba, in1=tB, op=ALU.mult), [bl[9], bl[11]], [B["bp"]])
                    S.op("dve", "tensor_tensor", dict(out=W["kpp"][:], in0=kp, in1=tB, op=ALU.mult), [bl[8], bl[11]], [B["kpp"]])
                    yield
                    trp = PB[2][:].bitcast(BF16)
                    for hh in range(2):
                        h = cc * 2 + hh
                        pb = hh * 64
                        for n in range(NCH):
                            for oi, (srcT, bsrcT) in enumerate(((W["bp"][:], B["bp"]), (W["kpp"][:], B["kpp"]), (W["vbf"][:, cc, :], B["vbf"]))):
                                o0 = (n * 3 + oi) * 64
                                S.op("pe", "transpose", dict(out=trp[0:64, o0:o0 + 64], in_=srcT[pb:pb + 64, n * CH:(n + 1) * CH],
                                                             identity=C["identb"][pb:pb + 64, pb:pb + 64]), [bsrcT, bC], [bPB[2]])
                        S.op("act", "activation", dict(out=TM3[:, h, :, :, :].rearrange("p n o t -> p (n o t)"),
                                                       in_=trp[0:64, 0:NCH * 192], func=AF.Copy), [bPB[2]], [bTM3])
                        yield
                for j in range(SUBM):
                    gsub = it * SUBM + j
                    sbs = [1] if gsub == 0 else [0, 1]
                    for kv in range(2):
                        for sb in sbs:
                            kcol = (j + sb) * 128
                            K_ = 67 if sb == 0 else 66
                            S.op("pe", "matmul", dict(out=PB[sb][:, :], lhsT=W["kT"][0:K_, kv, kcol:kcol + 128],
                                                      rhs=W["qT"][0:K_, kv * 4:(kv + 1) * 4, j * 128:(j + 1) * 128], start=True, stop=False),
                                 [B["kT"], B["qT"]], [bPB[sb]])
                            S.op("pe", "matmul", dict(out=PB[sb][:, :], lhsT=C["identb"][:], rhs=C["amask"][:, sb, :], start=False, stop=True),
                                 [bC], [bPB[sb]])
                            S.op("act", "activation", dict(out=W["PT"][:, sb, :], in_=PB[sb][:, :], func=AF.Exp), [bPB[sb]], [B["PT"]])
                        yield
                        yield
                        for i, sb in enumerate(sbs):
                            S.op("pe", "matmul", dict(out=PB[2][0:65, :], lhsT=W["Vaug"][:, j + sb, kv, :], rhs=W["PT"][:, sb, :],
                                                      start=(i == 0), stop=(i == len(sbs) - 1)), [B["Vaug"], B["PT"]], [bPB[2]])
                        for i, sb in enumerate(sbs):
                            S.op("pe", "matmul", dict(out=PB[0][0:64, :], lhsT=W["onesb"][:, :], rhs=W["PT"][:, sb, :],
                                                      start=(i == 0), stop=(i == len(sbs) - 1)), [B["onesb"], B["PT"]], [bPB[0]])
                        for gg in range(4):
                            S.op("dve", "tensor_scalar", dict(out=W["OTc"][:, gg * 128:(gg + 1) * 128], in0=PB[0][0:64, gg * 128:(gg + 1) * 128],
                                                              scalar1=W["esk"][:, kv * 4 + gg:kv * 4 + gg + 1], scalar2=None, op0=ALU.add),
                                 [bPB[0], B["esk"]], [B["OTc"]])
                        S.op("act", "activation", dict(out=W["OTc"], in_=W["OTc"], func=AF.Ln), [B["OTc"]], [B["OTc"]])
                        S.op("act", "activation", dict(out=W["OTc"], in_=W["OTc"], func=AF.Exp, scale=-1.0), [B["OTc"]], [B["OTc"]])
                        yield
                        S.op("dve", "tensor_tensor", dict(out=attT[:, kv * 4:(kv + 1) * 4, j * 128:(j + 1) * 128],
                                                          in0=PB[2][0:64, :].rearrange("p (g t) -> p g t", t=128),
                                                          in1=W["OTc"].rearrange("p (g t) -> p g t", t=128), op=ALU.mult),
                             [B["OTc"], bPB[2]], [battT])
                        yield
                S.op("pool", "tensor_copy", dict(out=W["kT"][0:64, :, 0:128], in_=W["kT"][0:64, :, TM:TM + 128]), [B["kT"]], [B["kT"]])
                S.op("pool", "tensor_copy", dict(out=W["Vaug"][:, 0, :, :], in_=W["Vaug"][:, SUBM, :, :]), [B["Vaug"]], [B["Vaug"]])
                yield

            def back(it):
                par = it % 2
                AR, BT, KT, TM3, gCt, bonus, sgd = (W["AR%d" % par], W["BT%d" % par], W["KT%d" % par], W["TM3%d" % par],
                                                    W["gC%d" % par], W["bonus%d" % par], W["sgd%d" % par])
                attT, battT = W["attT%d" % par], B["attT%d" % par]
                bAR, bBT, bKT, bTM3, bgC, bbonus, bsgd = (B["AR%d" % par], B["BT%d" % par], B["KT%d" % par], B["TM3%d" % par],
                                                          B["gC%d" % par], B["bonus%d" % par], B["sgd%d" % par])
                tok0 = it * TM
                bsrc = b_src_fn(it)
                for n in range(NCH):
                    for hp in range(4):
                        for hh in range(2):
                            h = hp * 2 + hh
                            pb = hh * 64
                            bank = ((3, 5), (4, 7))[hh][hp % 2]
                            rhsAR = AR[pb:pb + 64, hp, n, :, :]
                            S.op("pe", "matmul", dict(out=PB[bank][0:64, 0:128], lhsT=BT[pb:pb + 64, hp, n * CH:(n + 1) * CH],
                                                      rhs=rhsAR, start=True, stop=True), [bBT, bAR], [bPB[bank]])
                            S.op("pe", "matmul", dict(out=PB[bank][0:64, 128:256], lhsT=KT[pb:pb + 64, hp, n * CH:(n + 1) * CH],
                                                      rhs=rhsAR, start=True, stop=True), [bKT, bAR], [bPB[bank]])
                            S.op("dve", "tensor_tensor", dict(out=W["Amat"][:, h, :], in0=PB[bank][0:64, 0:256],
                                                              in1=C["maskA"][:, 0:256], op=ALU.mult), [bPB[bank], bC], [B["Amat"]])
                        yield
                    trp = PB[5][:].bitcast(BF16)
                    for h in range(8):
                        S.op("pe", "transpose", dict(out=trp[0:64, h * 64:(h + 1) * 64], in_=W["Amat"][:, h, 0:64], identity=C["identb"][0:64, 0:64]),
                             [B["Amat"], bC], [bPB[5]])
                    S.op("act", "activation", dict(out=W["Nm"][:].rearrange("p h c -> p (h c)"), in_=trp[0:64, 0:512], func=AF.Copy), [bPB[5]], [B["Nm"]])
                    S.op("dve", "tensor_tensor", dict(out=W["Pm"][:].rearrange("p h c -> p (h c)"), in0=W["Nm"][:].rearrange("p h c -> p (h c)"),
                                                      in1=C["identI8"][:], op=ALU.add), [B["Nm"], bC], [B["Pm"]])
                    yield
                    M_, MT_, M2_, MT2_ = "Nm", "NTm", "Nm2", "NTm2"
                    P_, P2_ = "Pm", "Pm2"
                    for lev in range(5):
                        last = (lev == 4)
                        if lev == 0:
                            mt_ap = lambda h: W["Amat"][:, h, 0:64]
                            bmt = B["Amat"]
                        else:
                            mt_ap = lambda h, MT_=MT_: W[MT_][:, h, :]
                            bmt = B[MT_]
                        for h in range(8):
                            S.op("pe", "matmul", dict(out=PB[6][0:64, h * 64:(h + 1) * 64], lhsT=W[M_][:, h, :], rhs=mt_ap(h), start=True, stop=True),
                                 [B[M_], bmt], [bPB[6]])
                        S.op("act", "activation", dict(out=W[MT2_][:].rearrange("p h c -> p (h c)"), in_=PB[6][0:64, :], func=AF.Copy), [bPB[6]], [B[MT2_]])
                        yield
                        if not last:
                            for h in range(8):
                                S.op("pe", "matmul", dict(out=PB[7][0:64, h * 64:(h + 1) * 64], lhsT=mt_ap(h), rhs=W[M_][:, h, :], start=True, stop=True),
                                     [B[M_], bmt], [bPB[7]])
                            S.op("dve", "tensor_copy", dict(out=W[M2_][:].rearrange("p h c -> p (h c)"), in_=PB[7][0:64, :]), [bPB[7]], [B[M2_]])
                            yield
                        for h in range(8):
                            S.op("pe", "matmul", dict(out=PB[5][0:64, h * 64:(h + 1) * 64], lhsT=W[MT2_][:, h, :], rhs=W[P_][:, h, :], start=True, stop=True),
                                 [B[MT2_], B[P_]], [bPB[5]])
                        S.op("dve", "tensor_tensor", dict(out=W[P2_][:].rearrange("p h c -> p (h c)"), in0=PB[5][0:64, :],
                                                          in1=W[P_][:].rearrange("p h c -> p (h c)"), op=ALU.add), [bPB[5], B[P_]], [B[P2_]])
                        yield
                        M_, M2_ = M2_, M_
                        MT_, MT2_ = MT2_, MT_
                        P_, P2_ = P2_, P_
                    trp = PB[6][:].bitcast(BF16)
                    for h in range(8):
                        S.op("pe", "transpose", dict(out=trp[0:64, h * 64:(h + 1) * 64], in_=W[P_][:, h, :], identity=C["identb"][0:64, 0:64]),
                             [B[P_], bC], [bPB[6]])
                    S.op("act", "activation", dict(out=W[P2_][:].rearrange("p h c -> p (h c)"), in_=trp[0:64, 0:512], func=AF.Copy), [bPB[6]], [B[P2_]])
                    TinvT = P2_
                    yield
                    for h in range(8):
                        hp, hh = h // 2, h % 2
                        pb = hh * 64
                        S.op("pe", "matmul", dict(out=PB[3 + hh][0:64, hp * 64:(hp + 1) * 64], lhsT=AR[pb:pb + 64, hp, n, 0, :], rhs=W["Hb"][pb:pb + 64, hp, :],
                                                  start=True, stop=True), [bAR, B["Hb"]], [bPB[3 + hh]])
                        S.op("pe", "matmul", dict(out=PB[7][0:64, h * 64:(h + 1) * 64], lhsT=W["Amat"][:, h, 128:192], rhs=TM3[:, h, n, 2, :],
                                                  start=True, stop=True), [B["Amat"], bTM3], [bPB[7]])
                    S.op("act", "activation", dict(out=W["Z2b"], in_=PB[7][0:64, :], func=AF.Copy), [bPB[7]], [B["Z2b"]])
                    z2 = W["Z2b"].rearrange("p (q e c) -> p q e c", e=2, c=64)
                    zb4 = W["Zb"][:].rearrange("p (q e) c -> p q e c", e=2)
                    for hh in range(2):
                        S.op("dve", "tensor_tensor", dict(out=zb4[:, :, hh, :], in0=PB[3 + hh][0:64, 0:256].rearrange("p (q c) -> p q c", c=64),
                                                          in1=z2[:, :, hh, :], op=ALU.add), [bPB[3 + hh], B["Z2b"]], [B["Zb"]])
                    yield
                    for h in range(8):
                        S.op("pe", "matmul", dict(out=PB[6][0:64, h * 64:(h + 1) * 64], lhsT=W[TinvT][:, h, :], rhs=W["Zb"][:, h, :], start=True, stop=True),
                             [B[TinvT], B["Zb"]], [bPB[6]])
                    S.op("act", "activation", dict(out=W["Ub"][:].rearrange("p h c -> p (h c)"), in_=PB[6][0:64, :], func=AF.Copy), [bPB[6]], [B["Ub"]])
                    yield
                    for h in range(8):
                        hp, hh = h // 2, h % 2
                        pb = hh * 64
                        S.op("pe", "matmul", dict(out=PB[3 + hh][pb:pb + 64, 256 + hp * 64:256 + (hp + 1) * 64], lhsT=W["Hb"][pb:pb + 64, hp, :],
                                                  rhs=AR[pb:pb + 64, hp, n, 1, :], start=True, stop=True), [B["Hb"], bAR], [bPB[3 + hh]])
                        yo = PB[5][pb:pb + 64, hp * 64:(hp + 1) * 64]
                        S.op("pe", "matmul", dict(out=yo, lhsT=W["Ub"][:, h, :], rhs=W["Amat"][:, h, 64:128], start=True, stop=False),
                             [B["Ub"], B["Amat"]], [bPB[5]])
                        S.op("pe", "matmul", dict(out=yo, lhsT=TM3[:, h, n, 2, :], rhs=W["Amat"][:, h, 192:256], start=False, stop=True),
                             [bTM3, B["Amat"]], [bPB[5]])
                        ho = PB[5][pb:pb + 64, 256 + hp * 64:256 + (hp + 1) * 64]
                        S.op("pe", "matmul", dict(out=ho, lhsT=TM3[:, h, n, 0, :], rhs=W["Ub"][:, h, :], start=True, stop=False),
                             [bTM3, B["Ub"]], [bPB[5]])
                        S.op("pe", "matmul", dict(out=ho, lhsT=TM3[:, h, n, 1, :], rhs=TM3[:, h, n, 2, :], start=False, stop=True),
                             [bTM3], [bPB[5]])
                        if h % 2 == 1:
                            yield
                    S.op("act", "activation", dict(out=W["yT"][:, :, n * CH:(n + 1) * CH], in_=PB[5][:, 0:256].rearrange("p (c t) -> p c t", t=64), func=AF.Copy),
                         [bPB[5]], byT)
                    for hh in range(2):
                        pb = hh * 64
                        S.op("dve", "tensor_tensor", dict(out=W["yT"][pb:pb + 64, :, n * CH:(n + 1) * CH],
                                                          in0=PB[3 + hh][pb:pb + 64, 256:512].rearrange("p (c t) -> p c t", t=64),
                                                          in1=W["yT"][pb:pb + 64, :, n * CH:(n + 1) * CH], op=ALU.add), [bPB[3 + hh]] + byT, byT)
                    for hp in range(4):
                        S.op("dve", "scalar_tensor_tensor", dict(out=W["H"][:, hp, :], in0=W["H"][:, hp, :], scalar=gCt[:, hp, n:n + 1],
                                                                  in1=PB[5][:, 256 + hp * 64:256 + (hp + 1) * 64], op0=ALU.mult, op1=ALU.add),
                             [B["H"], bgC, bPB[5]], [B["H"]])
                    S.op("act", "activation", dict(out=W["Hb"][:], in_=W["H"][:], func=AF.Copy), [B["H"]], [B["Hb"]])
                    yield
                def gate_mm(cc):
                    S.op("pe", "matmul", dict(out=PB[5][:, (cc % 2) * 256:(cc % 2) * 256 + TM], lhsT=W["g2b"][:, cc * 128:(cc + 1) * 128], rhs=sgd[:],
                                              start=True, stop=True), [B["g2b"], bsgd], [bPB[5]])
                yield
                for c0 in (0, 2):
                    ccs = (c0, c0 + 1)
                    m1 = {c0: 6, c0 + 1: 3}
                    m2 = {c0: 7, c0 + 1: 4}
                    tmp = {c0: W["pp0"][:], c0 + 1: W["pp1"][:]}
                    btmp = {c0: B["pp0"], c0 + 1: B["pp1"]}
                    for cc in ccs:
                        S.op("pe", "matmul", dict(out=PB[m1[cc]][:, 0:TM], lhsT=C["bones"][:], rhs=W["yT"][:, cc, :], start=True, stop=True),
                             [bC, byT[cc]], [bPB[m1[cc]]])
                    if c0 == 0:
                        gate_mm(0)
                        gate_mm(1)
                    yield
                    for cc in ccs:
                        S.op("dve", "scalar_tensor_tensor", dict(out=W["yT"][:, cc, :], in0=PB[m1[cc]][:, 0:TM], scalar=-1.0 / 64, in1=W["yT"][:, cc, :],
                                                                  op0=ALU.mult, op1=ALU.add), [bPB[m1[cc]], byT[cc]], [byT[cc]])
                        S.op("act", "activation", dict(out=tmp[cc], in_=W["yT"][:, cc, :], func=AF.Square), [byT[cc]], [btmp[cc]])
                    yield
                    yield
                    for cc in ccs:
                        S.op("pe", "matmul", dict(out=PB[m2[cc]][:, 0:TM], lhsT=C["bones"][:], rhs=tmp[cc], start=True, stop=True), [bC, btmp[cc]], [bPB[m2[cc]]])
                    yield
                    for cc in ccs:
                        S.op("dve", "tensor_scalar", dict(out=tmp[cc], in0=PB[m2[cc]][:, 0:TM], scalar1=1.0 / 64, scalar2=LNX_EPS, op0=ALU.mult, op1=ALU.add),
                             [bPB[m2[cc]]], [btmp[cc]])
                        S.op("act", "activation", dict(out=tmp[cc], in_=tmp[cc], func=AF.Ln), [btmp[cc]], [btmp[cc]])
                        S.op("act", "activation", dict(out=tmp[cc], in_=tmp[cc], func=AF.Exp, scale=-0.5), [btmp[cc]], [btmp[cc]])
                    yield
                    for cc in ccs:
                        yn = W["yT"][:, cc, :]
                        S.op("dve", "tensor_tensor", dict(out=yn, in0=yn, in1=tmp[cc], op=ALU.mult), [byT[cc], btmp[cc]], [byT[cc]])
                        S.op("dve", "tensor_scalar", dict(out=yn, in0=yn, scalar1=col(CV_LG, cc), scalar2=col(CV_LB, cc), op0=ALU.mult, op1=ALU.add),
                             [byT[cc], B["cv"]], [byT[cc]])
                        S.op("pool", "tensor_tensor", dict(out=yn, in0=yn, in1=bonus[:, cc, :], op=ALU.add), [byT[cc], bbonus], [byT[cc]])
                    yield
                    for cc in ccs:
                        S.op("dve", "tensor_tensor", dict(out=W["rwT"][:, cc, :], in0=PB[5][:, (cc % 2) * 256:(cc % 2) * 256 + TM], in1=W["yT"][:, cc, :], op=ALU.mult),
                             [bPB[5], byT[cc]], [B["rwT"]])
                    if c0 == 0:
                        gate_mm(2)
                        gate_mm(3)
                    yield
                if dbg and "rw" in dbg_out and it == dbg.get("_dbgtile", 0):
                    S.op("pool", "tensor_copy", dict(out=W["yT"][:], in_=W["rwT"][:]), [B["rwT"]], byT)
                    S.dma(dbg_out["rw"], W["yT"][:], reads=byT, writes=[b_out])
                    S.op("pool", "tensor_copy", dict(out=W["yT"][0:64, :, :].rearrange("p a b -> p (a b)"), in_=attT[:, 0:4, :].rearrange("p a b -> p (a b)")),
                         [battT], byT)
                    S.dma(dbg_out["att"], W["yT"][0:64, :, :], reads=byT, writes=[b_out])
                kk_ = 0
                for j in range(SUBM):
                    for nh in range(2):
                        xo = W["xos%d" % (kk_ % 2)]
                        bxo = B["xos%d" % (kk_ % 2)]
                        kk_ += 1
                        r0 = tok0 + j * 128
                        S.dma(xo[:], src[r0:r0 + 128, nh * 512:(nh + 1) * 512], reads=[bsrc], writes=[bxo])
                        k = bbank()
                        for h in range(8):
                            S.op("pe", "matmul", dict(out=PB[k][:, :], lhsT=attT[:, h, j * 128:(j + 1) * 128], rhs=W["WoA"][:, h, nh * 512:(nh + 1) * 512],
                                                      start=(h == 0), stop=False), [battT, B["WoA"]], [bPB[k]])
                        for cc in range(4):
                            S.op("pe", "matmul", dict(out=PB[k][:, :], lhsT=W["rwT"][:, cc, j * 128:(j + 1) * 128], rhs=W["WoR"][:, cc, nh * 512:(nh + 1) * 512],
                                                      start=False, stop=(cc == 3)), [B["rwT"], B["WoR"]], [bPB[k]])
                        S.op("dve", "tensor_tensor", dict(out=xo[:], in0=PB[k][:, :], in1=xo[:], op=ALU.add), [bPB[k], bxo], [bxo])
                        yield
                        S.dma(xs[r0:r0 + 128, nh * 512:(nh + 1) * 512], xo[:], reads=[bxo], writes=[b_xs[it]], eng="sp")

            ntm = dbg.get("_ntm", NTM) if dbg else NTM
            drive([front(0)])
            for it in range(ntm):
                drive([back(it), front(it + 1) if it + 1 < ntm else None], weights=[int(_os.environ.get("MK_RB", "2")), int(_os.environ.get("MK_RF", "1"))])
        S.barrier()

    def ffn(l, last):
        with ExitStack() as st:
            W = tiles(st, [
                ("xt0", [128, SUBF, D], F32), ("xt1", [128, SUBF, D], F32), ("ss", [128, SUBF], F32), ("rstd", [128, SUBF], F32),
                ("ss2", [128, SUBF], F32), ("rstd2", [128, SUBF], F32),
                ("xnb", [128, D], BF16), ("xnb1", [128, D], BF16), ("xT0", [128, 8, TF], BF16), ("xT1", [128, 8, TF], BF16),
                ("wgu0", [128, 8, 256], BF16), ("wgu1", [128, 8, 256], BF16), ("wgu2", [128, 8, 256], BF16), ("Wd", [128, NFC, D], BF16),
                ("gc", [128, 2 + TF], F32), ("gcar", [128, NFC, 2], F32),
                ("c1", [128, TF], F32), ("c2", [128, TF], F32), ("c3", [128, TF], F32), ("sl", [128, TF], F32),
                ("hT", [128, NFC, TF], BF16), ("xo", [128, SUBF, D], F32), ("cv", [128, NCV], F32), ("fingb", [128, D], F32),
            ])
            B = {k: Buf(k) for k in W}
            bw = b_wscr[l]
            cv = W["cv"]
            S.dma(cv[:], cvL[l], writes=[B["cv"]])
            S.dma(W["fingb"][:], fing.broadcast_to([128, D]), writes=[B["fingb"]])
            S.op("pool", "memset", dict(ap=W["gcar"][:], constant=0.0), [], [B["gcar"]])
            bWd = [Buf("Wd%d" % i) for i in range(NFC)]

            def head_dma(it):
                par = it % 2
                tok0 = it * TF
                xt, bxt = W["xt%d" % par], B["xt%d" % par]
                S.dma(xt[:], xs[tok0:tok0 + TF, :].rearrange("(j p) d -> p j d", p=128), reads=[b_xs[2 * it], b_xs[2 * it + 1]], writes=[bxt])

            def head_load(it):
                par = it % 2
                xt, bxt = W["xt%d" % par], B["xt%d" % par]
                for j in range(SUBF):
                    S.op("act", "activation", dict(out=W["xnb"][:], in_=xt[:, j, :], func=AF.Square, accum_out=W["ss"][:, j:j + 1]),
                         [bxt], [B["xnb"], B["ss"]])
                S.op("dve", "tensor_scalar", dict(out=W["rstd"][:], in0=W["ss"][:], scalar1=1.0 / D, scalar2=RMS_EPS, op0=ALU.mult, op1=ALU.add),
                     [B["ss"]], [B["rstd"]])
                S.op("act", "activation", dict(out=W["rstd"][:], in_=W["rstd"][:], func=AF.Sqrt), [B["rstd"]], [B["rstd"]])
                S.op("dve", "reciprocal", dict(out=W["rstd"][:], in_=W["rstd"][:]), [B["rstd"]], [B["rstd"]])

            def head_sub_a(it, j):
                par = it % 2
                xt, bxt = W["xt%d" % par], B["xt%d" % par]
                xn = "xnb" if j % 2 == 0 else "xnb1"
                S.op("dve", "tensor_scalar", dict(out=W[xn][:], in0=xt[:, j, :], scalar1=W["rstd"][:, j:j + 1], scalar2=None, op0=ALU.mult),
                     [bxt, B["rstd"]], [B[xn]])

            def head_sub_b(it, j):
                par = it % 2
                xT, bxT = W["xT%d" % par], B["xT%d" % par]
                pT = PB[4][:].bitcast(BF16)
                xn = "xnb" if j % 2 == 0 else "xnb1"
                for c in range(8):
                    S.op("pe", "transpose", dict(out=pT[:, c * 128:(c + 1) * 128], in_=W[xn][:, c * 128:(c + 1) * 128], identity=C["identb"][:]),
                         [B[xn], bC], [bPB[4]])
                S.op("act", "activation", dict(out=xT[:, :, j * 128:(j + 1) * 128], in_=pT[:, 0:1024].rearrange("p (c t) -> p c t", t=128),
                                               func=AF.Copy), [bPB[4]], [bxT])

            def ffn_gen():
                head_dma(0)
                head_load(0)
                for j in range(SUBF):
                    head_sub_a(0, j)
                    head_sub_b(0, j)
                    yield
                for it in range(NTF):
                    tok0 = it * TF
                    par = it % 2
                    xt, bxt = W["xt%d" % par], B["xt%d" % par]
                    xT, bxT = W["xT%d" % par], B["xT%d" % par]
                    for fc in range(NFC):
                        wk = "wgu%d" % (fc % 3)
                        S.dma(W[wk][:], wguB[l][fc], reads=bw["wgu"], writes=[B[wk]])
                        if fc < 11:
                            for f2 in (2 * fc, 2 * fc + 1):
                                S.dma(W["Wd"][:, f2, :], wdB[l][:, f2 * D:(f2 + 1) * D], reads=bw["wd"], writes=[bWd[f2]])
                        if it + 1 < NTF:
                            if fc == 1:
                                head_dma(it + 1)
                            if fc == 9:
                                head_load(it + 1)
                            if fc in (10, 11):
                                head_sub_a(it + 1, fc - 10)
                            if fc in (14, 16):
                                head_sub_a(it + 1, 2 + (fc - 14) // 2)
                            if fc in (13, 15, 17, 19):
                                head_sub_b(it + 1, (fc - 13) // 2)
                        pg = (fc % 2) * 2
                        pu = pg + 1
                        for c in range(8):
                            S.op("pe", "matmul", dict(out=PB[pg][:, :], lhsT=W[wk][:, c, 0:128], rhs=xT[:, c, :], start=(c == 0), stop=(c == 7)),
                                 [B[wk], bxT], [bPB[pg]])
                        for c in range(8):
                            S.op("pe", "matmul", dict(out=PB[pu][:, :], lhsT=W[wk][:, c, 128:256], rhs=xT[:, c, :], start=(c == 0), stop=(c == 7)),
                                 [B[wk], bxT], [bPB[pu]])
                        gc = W["gc"]
                        S.op("act", "activation", dict(out=gc[:, 2:2 + TF], in_=PB[pg][:, :], func=AF.Copy), [bPB[pg]], [B["gc"]])
                        S.op("pool", "tensor_copy", dict(out=gc[:, 0:2], in_=W["gcar"][:, fc, :]), [B["gcar"]], [B["gc"]])
                        S.op("pool", "tensor_scalar", dict(out=W["c1"][:], in0=gc[:, 0:TF], scalar1=cv[:, CV_CW + fc:CV_CW + fc + 1],
                                                           scalar2=cv[:, CV_CB + fc:CV_CB + fc + 1], op0=ALU.mult, op1=ALU.add), [B["gc"], B["cv"]], [B["c1"]])
                        S.op("dve", "scalar_tensor_tensor", dict(out=W["c2"][:], in0=gc[:, 1:1 + TF], scalar=cv[:, CV_CW + NFC + fc:CV_CW + NFC + fc + 1],
                                                                  in1=W["c1"][:], op0=ALU.mult, op1=ALU.add), [B["gc"], B["cv"], B["c1"]], [B["c2"]])
                        S.op("dve", "scalar_tensor_tensor", dict(out=W["c3"][:], in0=gc[:, 2:2 + TF], scalar=cv[:, CV_CW + 2 * NFC + fc:CV_CW + 2 * NFC + fc + 1],
                                                                  in1=W["c2"][:], op0=ALU.mult, op1=ALU.add), [B["gc"], B["cv"], B["c2"]], [B["c3"]])
                        S.op("pool", "tensor_copy", dict(out=W["gcar"][:, fc, :], in_=gc[:, TF:TF + 2]), [B["gc"]], [B["gcar"]])
                        S.op("act", "activation", dict(out=W["sl"][:], in_=W["c3"][:], func=AF.Silu), [B["c3"]], [B["sl"]])
                        S.op("dve", "tensor_tensor", dict(out=W["hT"][:, fc, :], in0=PB[pu][:, :], in1=W["sl"][:], op=ALU.mult), [bPB[pu], B["sl"]], [B["hT"]])
                        yield
                    kk = 0
                    for j in range(SUBF):
                        for nh in range(2):
                            k = 5 + (kk % 3)
                            kk += 1
                            for fc in range(NFC):
                                S.op("pe", "matmul", dict(out=PB[k][:, :], lhsT=W["hT"][:, fc, j * 128:(j + 1) * 128], rhs=W["Wd"][:, fc, nh * 512:(nh + 1) * 512],
                                                          start=(fc == 0), stop=(fc == NFC - 1)), [B["hT"], bWd[fc]], [bPB[k]])
                            S.op("dve", "tensor_tensor", dict(out=W["xo"][:, j, nh * 512:(nh + 1) * 512], in0=PB[k][:, :], in1=xt[:, j, nh * 512:(nh + 1) * 512],
                                                              op=ALU.add), [bPB[k], bxt], [B["xo"]])
                            yield
                    if not last:
                        S.dma(xs[tok0:tok0 + TF, :].rearrange("(j p) d -> p j d", p=128), W["xo"][:], reads=[B["xo"]], writes=[b_xs[2 * it], b_xs[2 * it + 1]], eng="sp")
                    else:
                        for j in range(SUBF):
                            S.op("act", "activation", dict(out=W["xnb"][:], in_=W["xo"][:, j, :], func=AF.Square, accum_out=W["ss2"][:, j:j + 1]),
                                 [B["xo"]], [B["xnb"], B["ss2"]])
                        S.op("dve", "tensor_scalar", dict(out=W["rstd2"][:], in0=W["ss2"][:], scalar1=1.0 / D, scalar2=RMS_EPS, op0=ALU.mult, op1=ALU.add),
                             [B["ss2"]], [B["rstd2"]])
                        S.op("act", "activation", dict(out=W["rstd2"][:], in_=W["rstd2"][:], func=AF.Sqrt), [B["rstd2"]], [B["rstd2"]])
                        S.op("dve", "reciprocal", dict(out=W["rstd2"][:], in_=W["rstd2"][:]), [B["rstd2"]], [B["rstd2"]])
                        for j in range(SUBF):
                            S.op("dve", "scalar_tensor_tensor", dict(out=W["xo"][:, j, :], in0=W["xo"][:, j, :], scalar=W["rstd2"][:, j:j + 1], in1=W["fingb"][:],
                                                                      op0=ALU.mult, op1=ALU.mult), [B["xo"], B["rstd2"], B["fingb"]], [B["xo"]])
                        S.dma(out_d[tok0:tok0 + TF, :].rearrange("(j p) d -> p j d", p=128), W["xo"][:], reads=[B["xo"]], writes=[b_out], eng="sp")

            if l + 1 < L:
                Wp = tiles(st, pre_tiles(2))
                drive([ffn_gen(), prepass_gen(l + 1, Wp, "sp", True, load_eng="act", engs=("dve",), NSTG=2)], weights=[6, 1])
            else:
                drive([ffn_gen()])
        S.barrier()

    nlayers = dbg.get("_layers", L) if dbg else L
    stop_after = dbg.get("_stop", None) if dbg else None
    for l in range(nlayers):
        if l == 0:
            prepass(l)
        if stop_after == ("prepass", l):
            break
        if l == 0:
            mixer(l, x_in, lambda it: b_x)
        else:
            mixer(l, xs, lambda it: b_xs[it])
        if stop_after == ("mixer", l):
            break
        ffn(l, last=(l == L - 1))
    S.finish()
    cst.close()
    return nc, S


def _colmajor(v):
    n = v.shape[0] // 128
    return np.ascontiguousarray(v.reshape(n, 128).T)


def host_layout(inp):
    f = np.float32
    d = {}
    w_in = inp["w_in"]
    d["winL"] = np.ascontiguousarray(w_in.reshape(L, 8, 128, NIN).transpose(0, 2, 1, 3))
    w_out = inp["w_out"]
    d["woAL"] = np.ascontiguousarray(w_out[:, :512].reshape(L, 8, 64, D).transpose(0, 2, 1, 3))
    d["woRL"] = np.ascontiguousarray(w_out[:, 512:].reshape(L, 4, 128, D).transpose(0, 2, 1, 3))
    wg = inp["ffn_w_gate"].reshape(L, 8, 128, NFC, 128)
    wu = inp["ffn_w_up"].reshape(L, 8, 128, NFC, 128)
    wgu = np.stack([wg, wu], axis=4)
    d["wguL"] = np.ascontiguousarray(wgu.reshape(L, 8, 128, 2, 11 * 256))
    d["wdL"] = np.ascontiguousarray(inp["ffn_w_down"].reshape(L, NFC, 128, D).transpose(0, 2, 1, 3).reshape(L, 128, NFC * D))
    d["w2a2L"] = np.ascontiguousarray(np.concatenate([inp["w2"], inp["a2"]], axis=1))
    d["g2L"] = np.ascontiguousarray(inp["g2"])
    d["v1L"] = np.ascontiguousarray(inp["v1"][0].reshape(4, 128, 32).transpose(1, 0, 2))
    d["v2L"] = np.ascontiguousarray(inp["v2"][0])
    cv = np.zeros((L, 128, NCV), f)
    for l in range(L):
        cv[l, :, CV_G1:CV_G1 + 8] = _colmajor(inp["norm1_g"][l])
        cv[l, :, CV_G2:CV_G2 + 8] = _colmajor(inp["norm2_g"][l])
        cv[l, :, CV_MU:CV_MU + 14] = _colmajor(inp["shift_mu"][l])
        for nm, o in (("w0", CV_W0), ("a0", CV_A0), ("k_k", CV_KK), ("k_a", CV_KA), ("r_k", CV_RK), ("lnx_g", CV_LG), ("lnx_b", CV_LB)):
            cv[l, :, o:o + 4] = _colmajor(inp[nm][l])
        if l > 0:
            cv[l, :, CV_V0:CV_V0 + 4] = _colmajor(inp["v0"][l - 1])
        for j in range(3):
            cv[l, :, CV_CW + j * NFC:CV_CW + (j + 1) * NFC] = _colmajor(inp["conv_w"][l, j])
        cv[l, :, CV_CB:CV_CB + NFC] = _colmajor(inp["conv_b"][l])
    d["cvL"] = cv
    d["sinkL"] = np.ascontiguousarray(inp["attn_sinks"].reshape(L, 1, 8))
    d["fing"] = np.ascontiguousarray(inp["final_g"].reshape(1, D))
    d["c_ident"] = np.eye(128, dtype=f)
    s = np.arange(64)[:, None]
    t = np.arange(64)[None, :]
    strict = (t > s).astype(f)
    incl = (t >= s).astype(f)
    m256 = np.concatenate([strict, incl, strict, incl], axis=1)
    d["c_maskA"] = np.ascontiguousarray(np.concatenate([m256, m256], axis=1))
    rm = np.ones((128, TM), f)
    rm[:, ::CH] = 0.0
    d["c_reset"] = rm
    d["c_identI8"] = np.ascontiguousarray(np.tile(np.eye(64, dtype=f), (1, 8)))
    bo = np.zeros((128, 128), f)
    bo[:64, :64] = 1.0
    bo[64:, 64:] = 1.0
    d["c_bones"] = bo
    s = np.arange(128)[:, None]
    t = np.arange(128)[None, :]
    NEG = -30000.0
    m_prev = np.where(s > t, 0.0, NEG).astype(f)
    m_cur = np.where(s <= t, 0.0, NEG).astype(f)
    am = np.stack([np.tile(m_prev, (1, 4)), np.tile(m_cur, (1, 4))], axis=1)
    d["c_amask"] = np.ascontiguousarray(am)
    slopes = (2.0 ** (-8.0 * np.arange(1, 9) / 8)).astype(f)
    tt = (np.arange(TM) % 128).astype(f)
    qa = np.zeros((3, 8, TM), f)
    qa[0] = slopes[:, None]
    qa[1] = -slopes[:, None] * tt[None, :]
    qa[2] = -128.0 * slopes[:, None]
    d["c_qaug"] = qa
    ka = np.zeros((3, 2, 128 + TM), f)
    ka[0] = (np.arange(128 + TM) % 128).astype(f)[None, :]
    ka[1] = 1.0
    ka[2] = 1.0
    d["c_kaug"] = ka
    return d


_CACHE = {}


def kernel(**inputs):
    inp = {k: np.asarray(v, dtype=np.float32) for k, v in inputs.items()}
    shared = host_layout(inp)
    if "nc" not in _CACHE:
        _CACHE["nc"] = build_program()[0]
    nc = _CACHE["nc"]
    x = inp["x"]
    in_maps = []
    for b in range(8):
        m = dict(shared)
        m["x"] = np.ascontiguousarray(x[b])
        in_maps.append(m)
    res = run_bass_kernel_spmd(nc, in_maps, core_ids=list(range(8)))
    out = np.stack([np.asarray(res.results[b]["out"], dtype=np.float32) for b in range(8)], axis=0)
    return out
```

```python
import os as _os
import numpy as np
from contextlib import ExitStack
import concourse.bass as bass
import concourse.mybir as mybir
from concourse.bass_utils import run_bass_kernel_spmd
from concourse.alu_op_type import AluOpType as ALU

AF = mybir.ActivationFunctionType
F32 = mybir.dt.float32
BF16 = mybir.dt.bfloat16

T = 4096
D = 1024
L = 2
NIN = 2560
FF = 2816
NFC = 22
TM = 256
NTM = T // TM
SUBM = TM // 128
CH = 64
NCH = TM // CH
TF = 512
NTF = T // TF
SUBF = TF // 128
NCV = 150
CV_G1, CV_G2, CV_MU, CV_W0, CV_A0, CV_KK, CV_KA, CV_RK, CV_LG, CV_LB, CV_V0, CV_CW, CV_CB = \
    0, 8, 16, 30, 34, 38, 42, 46, 50, 54, 58, 62, 128
RMS_EPS = 1e-5
LNX_EPS = 64e-5
EM05 = float(np.exp(-0.5))


class Buf:
    __slots__ = ("name", "w", "r")

    def __init__(self, name):
        self.name = name
        self.w = None
        self.r = {}


class Sched:
    NDMA = 32

    def __init__(self, nc):
        self.nc = nc
        self.names = ["pe", "dve", "act", "pool", "sp"]
        self.engobj = {"pe": nc.tensor, "dve": nc.vector, "act": nc.scalar, "pool": nc.gpsimd, "sp": nc.sync}
        self.count = {e: 0 for e in self.names}
        self.waited = {e: {} for e in self.names}
        self.sem_by_id = {}
        for e in self.names:
            self.sem_by_id[("e", e)] = nc.alloc_semaphore("es_" + e)
        for i in range(self.NDMA):
            self.sem_by_id[("d", i)] = nc.alloc_semaphore("ds_%d" % i)
        self.duse = [0] * self.NDMA
        self.drr = 0
        self.ninst = 0

    def _deps(self, eng, reads, writes):
        deps = []
        for b in reads:
            if b.w is not None:
                deps.append(b.w)
        for b in writes:
            if b.w is not None:
                deps.append(b.w)
            deps.extend(b.r.values())
        waits = []
        for (sid, val, src) in deps:
            if src == "pe" and eng == "pe":
                continue
            if self.waited[eng].get(sid, 0) >= val:
                continue
            self.waited[eng][sid] = val
            waits.append((sid, val))
        return waits

    def _commit(self, eng, ev, reads, writes):
        key = eng if ev[2] != "dma" else ev[0]
        for b in reads:
            b.r[key] = ev
        for b in writes:
            b.w = ev
            b.r = {}

    def _emit(self, name, waits, meth, kw, inc):
        eng = self.engobj[name]
        for (sid, val) in waits:
            eng.wait_ge(self.sem_by_id[sid], val)
            self.ninst += 1
        if meth is not None:
            ins = getattr(eng, meth)(**kw)
            ins.then_inc(self.sem_by_id[inc[0]], inc[1])
            self.ninst += 1

    def op(self, eng, meth, kw, reads=(), writes=()):
        waits = self._deps(eng, reads, writes)
        self.count[eng] += 1
        ev = (("e", eng), self.count[eng], eng)
        self._emit(eng, waits, meth, kw, (("e", eng), 1))
        self._commit(eng, ev, reads, writes)

    def dma(self, out, in_, reads=(), writes=(), eng="sp"):
        waits = self._deps(eng, reads, writes)
        j = self.drr
        self.drr = (j + 1) % self.NDMA
        sid = ("d", j)
        if self.duse[j] > 0:
            val = 16 * self.duse[j]
            if self.waited[eng].get(sid, 0) < val:
                self.waited[eng][sid] = val
                waits.append((sid, val))
        self.duse[j] += 1
        ev = (sid, 16 * self.duse[j], "dma")
        self._emit(eng, waits, "dma_start", dict(out=out, in_=in_), (sid, 16))
        self._commit(eng, ev, reads, writes)

    def barrier(self):
        evs = []
        for j in range(self.NDMA):
            if self.duse[j] > 0:
                evs.append((("d", j), 16 * self.duse[j]))
        for e in self.names:
            if self.count[e] > 0:
                evs.append((("e", e), self.count[e]))
        for e in self.names:
            w = []
            for (sid, val) in evs:
                if self.waited[e].get(sid, 0) < val:
                    self.waited[e][sid] = val
                    w.append((sid, val))
            self._emit(e, w, None, None, None)

    def finish(self):
        self.barrier()


def build_program(dbg=None):
    nc = bass.Bass("TRN2", target_bir_lowering=False)
    S = Sched(nc)

    def din(name, shape, dt=F32):
        return nc.dram_tensor(name, list(shape), dt, kind="ExternalInput").ap()

    x_in = din("x", [T, D])
    winL = din("winL", [L, 128, 8, NIN])
    woAL = din("woAL", [L, 64, 8, D])
    woRL = din("woRL", [L, 128, 4, D])
    wguL = din("wguL", [L, 8, 128, 2, 11 * 256])
    wdL = din("wdL", [L, 128, NFC * D])
    w2a2L = din("w2a2L", [L, 128, 512])
    g2L = din("g2L", [L, 128, 512])
    v1L = din("v1L", [128, 4, 32])
    v2L = din("v2L", [32, 512])
    cvL = din("cvL", [L, 128, NCV])
    sinkL = din("sinkL", [L, 1, 8])
    fing = din("fing", [1, D])
    c_ident = din("c_ident", [128, 128])
    c_maskA = din("c_maskA", [64, 512])
    c_reset = din("c_reset", [128, TM])
    c_identI8 = din("c_identI8", [64, 512])
    c_bones = din("c_bones", [128, 128])
    c_amask = din("c_amask", [128, 2, 512])
    c_qaug = din("c_qaug", [3, 8, TM])
    c_kaug = din("c_kaug", [3, 2, 128 + TM])
    out_d = nc.dram_tensor("out", [T, D], F32, kind="ExternalOutput").ap()
    xs = nc.dram_tensor("xs", [T, D], F32, kind="Internal").ap()
    vf = nc.dram_tensor("vf", [128, 4, T], F32, kind="Internal").ap()
    winB = [nc.dram_tensor("winB%d" % l, [128, 8, NIN], BF16, kind="Internal").ap() for l in range(L)]
    woAB = [nc.dram_tensor("woAB%d" % l, [64, 8 * D], BF16, kind="Internal").ap() for l in range(L)]
    woRB = [nc.dram_tensor("woRB%d" % l, [128, 4 * D], BF16, kind="Internal").ap() for l in range(L)]
    wguB = [nc.dram_tensor("wguB%d" % l, [NFC, 128, 8, 256], BF16, kind="Internal").ap() for l in range(L)]
    wdB = [nc.dram_tensor("wdB%d" % l, [128, NFC * D], BF16, kind="Internal").ap() for l in range(L)]
    dbg_out = {}
    if dbg:
        for nm, shp in dbg.items():
            if nm.startswith("_"):
                continue
            dbg_out[nm] = nc.dram_tensor("dbg_" + nm, list(shp), F32, kind="ExternalOutput").ap()

    b_x = Buf("x")
    b_xs = [Buf("xs%d" % i) for i in range(NTM)]
    b_out = Buf("out")
    b_vf = [Buf("vf%d" % i) for i in range(NTM)]
    b_wscr = {}

    PB = [nc.alloc_psum_tensor("pb%d" % i, [128, 512], F32) for i in range(8)]
    bPB = [Buf("pb%d" % i) for i in range(8)]

    uid = {"i": 0}

    def tiles(st, specs):
        res = {}
        uid["i"] += 1
        for nm, shp, dt in specs:
            res[nm] = st.enter_context(nc.sbuf_tensor("%s_%d" % (nm, uid["i"]), list(shp), dt))
        if dbg and dbg.get("_mem"):
            print("SBUF after tiles group", uid["i"], nc.bytes_allocated(res[nm].space), "of", nc.space_capacity(res[nm].space))
        return res

    rr = {"i": 0}

    def ew():
        rr["i"] += 1
        return ("dve", "pool")[rr["i"] % 2]

    cst = ExitStack()
    C = tiles(cst, [
        ("identf", [128, 128], F32), ("identb", [128, 128], BF16), ("bones", [128, 128], F32),
        ("maskA", [64, 256], F32), ("resetm", [128, TM], F32), ("identI8", [64, 512], BF16),
        ("amask", [128, 2, 512], BF16),
    ])
    bC = Buf("consts")
    S.dma(C["identf"][:], c_ident, writes=[bC])
    S.dma(C["bones"][:], c_bones, writes=[bC])
    S.dma(C["maskA"][:], c_maskA[:, 0:256], writes=[bC])
    S.dma(C["resetm"][:], c_reset, writes=[bC])
    with ExitStack() as st0:
        Cs = tiles(st0, [("i8f", [64, 512], F32), ("amf", [128, 2, 512], F32)])
        bCs = Buf("cstage")
        S.dma(Cs["i8f"][:], c_identI8, writes=[bCs])
        S.dma(Cs["amf"][:], c_amask, writes=[bCs])
        S.op("dve", "tensor_copy", dict(out=C["identI8"][:], in_=Cs["i8f"][:]), [bCs], [bC])
        S.op("dve", "tensor_copy", dict(out=C["amask"][:], in_=Cs["amf"][:]), [bCs], [bC])
        S.barrier()
    S.op("dve", "tensor_copy", dict(out=C["identb"][:], in_=C["identf"][:]), [bC], [bC])

    def drive(gens, weights=None):
        items = [[g, (weights[i] if weights else 1)] for i, g in enumerate(gens) if g is not None]
        while items:
            for itm in list(items):
                for _ in range(itm[1]):
                    try:
                        next(itm[0])
                    except StopIteration:
                        items.remove(itm)
                        break

    def pre_tiles(nstg):
        return ([("stg%d" % i, [128, FF], F32) for i in range(nstg)] + [("ob%d" % i, [128, FF], BF16) for i in range(nstg)]
                + [("cvp", [128, NCV], F32)])

    def prepass_gen(l, W, store_eng, lazy_store, load_eng="sp", engs=("dve", "pool"), NSTG=3):
        bs = [Buf("stg%d" % i) for i in range(NSTG)]
        bo = [Buf("ob%d" % i) for i in range(NSTG)]
        bcv = Buf("cvp")
        S.dma(W["cvp"][:], cvL[l], writes=[bcv])
        bw = {k: [] for k in ("win", "woA", "woR", "wgu", "wd")}
        b_wscr[l] = bw
        pieces = []
        for c in range(8):
            pieces.append((winL[l, :, c, :], winB[l][:, c, :], 128, NIN, W["cvp"][:, CV_G1 + c:CV_G1 + c + 1], bw["win"], None))
        woA_src = woAL[l].rearrange("p h n -> p (h n)")
        for i in range(4):
            pieces.append((woA_src[:, i * 2048:(i + 1) * 2048], woAB[l][:, i * 2048:(i + 1) * 2048], 64, 2048, None, bw["woA"], None))
        woR_src = woRL[l].rearrange("p h n -> p (h n)")
        for i in range(2):
            pieces.append((woR_src[:, i * 2048:(i + 1) * 2048], woRB[l][:, i * 2048:(i + 1) * 2048], 128, 2048, None, bw["woR"], None))
        for c in range(8):
            for hf in range(2):
                dst = wguB[l][hf * 11:(hf + 1) * 11, :, c, :].rearrange("f p j -> p f j")
                pieces.append((wguL[l, c, :, hf, :], dst, 128, FF, W["cvp"][:, CV_G2 + c:CV_G2 + c + 1], bw["wgu"], 256))
        for i in range(8):
            pieces.append((wdL[l][:, i * FF:(i + 1) * FF], wdB[l][:, i * FF:(i + 1) * FF], 128, FF, None, bw["wd"], None))
        pending = None

        def store(pd):
            (dst, ob_ap, bok, wb, dv3) = pd
            if dv3 is None:
                S.dma(dst, ob_ap, reads=[bok], writes=[wb], eng=store_eng)
            else:
                S.dma(dst, ob_ap.rearrange("p (a b) -> p a b", b=dv3), reads=[bok], writes=[wb], eng=store_eng)

        for i, (src, dst, np_, n, scale, wbl, dv3) in enumerate(pieces):
            wb = Buf("wpiece")
            wbl.append(wb)
            k = i % NSTG
            stg = W["stg%d" % k]
            ob = W["ob%d" % k]
            if pending is not None:
                store(pending)
                pending = None
            S.dma(stg[0:np_, 0:n], src, writes=[bs[k]], eng=load_eng)
            e = engs[i % len(engs)]
            if scale is None:
                S.op(e, "tensor_copy", dict(out=ob[0:np_, 0:n], in_=stg[0:np_, 0:n]), [bs[k]], [bo[k]])
            else:
                S.op(e, "tensor_scalar", dict(out=ob[0:np_, 0:n], in0=stg[0:np_, 0:n], scalar1=scale, scalar2=None, op0=ALU.mult),
                     [bs[k], bcv], [bo[k]])
            pd = (dst, ob[0:np_, 0:n], bo[k], wb, dv3)
            if lazy_store:
                pending = pd
            else:
                store(pd)
            yield
        if pending is not None:
            store(pending)

    def prepass(l):
        with ExitStack() as st:
            W = tiles(st, pre_tiles(4))
            drive([prepass_gen(l, W, "act", False, engs=("dve",), NSTG=4)])
        S.barrier()

    def mixer(l, src, b_src_fn):
        with ExitStack() as st:
            W = tiles(st, [
                ("WinT", [128, 8, NIN], BF16), ("WoA", [64, 8, D], BF16), ("WoR", [128, 4, D], BF16),
                ("w2a2b", [128, 512], BF16), ("g2b", [128, 512], BF16),
                ("v1b", [128, 4, 32], BF16), ("v2b", [32, 512], BF16),
                ("cv", [128, NCV], F32), ("omka", [128, 4], F32), ("esk", [64, 8], F32), ("onesb", [128, 64], BF16),
                ("ulast", [128, 14], F32), ("H", [128, 4, 64], F32), ("Hb", [128, 4, 64], BF16),
                ("kT", [67, 2, 128 + TM], BF16), ("Vaug", [128, 1 + SUBM, 2, 65], BF16), ("qT", [67, 8, TM], BF16),
                ("xt", [128, SUBM, D], F32),
                ("ss", [128, SUBM], F32), ("rstd", [128, SUBM], F32),
                ("xnb", [128, D], BF16), ("xT", [128, 8, TM], BF16),
                ("u", [128, 1 + TM], F32), ("dsh", [128, TM], F32),
                ("twa", [128, TM], BF16), ("sgd0", [128, TM], BF16), ("sgd1", [128, TM], BF16),
                ("vall", [128, 4, TM], F32), ("vbf", [128, 4, TM], BF16), ("lvb", [32, TM], BF16),
                ("rk", [128, 2, TM], F32),
                ("tbig", [128, 12, TM], F32),
                ("gC0", [128, 4, NCH], F32), ("gC1", [128, 4, NCH], F32),
                ("AR0", [128, 4, NCH, 2, CH], BF16), ("BT0", [128, 4, TM], BF16), ("KT0", [128, 4, TM], BF16),
                ("AR1", [128, 4, NCH, 2, CH], BF16), ("BT1", [128, 4, TM], BF16), ("KT1", [128, 4, TM], BF16),
                ("bp", [128, TM], BF16), ("kpp", [128, TM], BF16),
                ("TM30", [64, 8, NCH, 3, 64], BF16), ("TM31", [64, 8, NCH, 3, 64], BF16),
                ("bonus0", [128, 4, TM], F32), ("bonus1", [128, 4, TM], F32), ("yT", [128, 4, TM], F32),
                ("Amat", [64, 8, 256], BF16), ("Nm", [64, 8, 64], BF16), ("NTm", [64, 8, 64], BF16),
                ("Nm2", [64, 8, 64], BF16), ("NTm2", [64, 8, 64], BF16),
                ("Pm", [64, 8, 64], BF16), ("Pm2", [64, 8, 64], BF16), ("Zb", [64, 8, 64], BF16), ("Ub", [64, 8, 64], BF16),
                ("sT", [128, 1, 512], F32), ("PT", [128, 2, 512], BF16),
                ("den", [65, 512], F32), ("attT0", [64, 8, TM], BF16), ("attT1", [64, 8, TM], BF16), ("rwT", [128, 4, TM], BF16),
                ("xos0", [128, 512], F32), ("xos1", [128, 512], F32),
                ("pp0", [128, TM], F32), ("pp1", [128, TM], F32),
            ])
            B = {k: Buf(k) for k in W}
            for i in range(12):
                W["t%d" % i] = W["tbig"][:, i, :]
                B["t%d" % i] = Buf("t%d" % i)
            xtf = W["xt"][:].rearrange("p j d -> p (j d)")
            W["w2a2f"] = xtf[:, 0:512]
            W["g2f"] = xtf[:, 512:1024]
            W["v1f"] = xtf[:, 1024:1152].rearrange("p (a b) -> p a b", b=32)
            W["v2f"] = xtf[0:32, 1152:1664]
            W["augf"] = xtf[0:67, 0:8 * TM].rearrange("p (a b) -> p a b", b=TM)
            tbf = W["tbig"][:].rearrange("p a b -> p (a b)")
            W["kaugf"] = tbf[0:67, 0:2 * (128 + TM)].rearrange("p (a b) -> p a b", b=128 + TM)
            for nm in ("w2a2f", "g2f", "v1f", "v2f", "augf"):
                B[nm] = B["xt"]
            B["kaugf"] = B["t0"]
            byT = [Buf("yT%d" % i) for i in range(4)]
            W["OTc"] = W["den"][0:64, :]
            B["OTc"] = Buf("OTc")
            W["Z2b"] = W["Ub"][:].rearrange("p h c -> p (h c)")
            B["Z2b"] = B["Ub"]
            bw = b_wscr[l]
            S.dma(W["WinT"][:], winB[l], reads=bw["win"], writes=[B["WinT"]])
            S.dma(W["WoA"][:].rearrange("p h n -> p (h n)"), woAB[l], reads=bw["woA"], writes=[B["WoA"]])
            S.dma(W["WoR"][:].rearrange("p h n -> p (h n)"), woRB[l], reads=bw["woR"], writes=[B["WoR"]])
            S.dma(W["w2a2f"][:], w2a2L[l], writes=[B["w2a2f"]])
            S.dma(W["g2f"][:], g2L[l], writes=[B["g2f"]])
            S.dma(W["cv"][:], cvL[l], writes=[B["cv"]])
            S.op("dve", "tensor_copy", dict(out=W["w2a2b"][:], in_=W["w2a2f"][:]), [B["w2a2f"]], [B["w2a2b"]])
            S.op("dve", "tensor_copy", dict(out=W["g2b"][:], in_=W["g2f"][:]), [B["g2f"]], [B["g2b"]])
            if l > 0:
                S.dma(W["v1f"][:], v1L, writes=[B["v1f"]])
                S.dma(W["v2f"][:], v2L, writes=[B["v2f"]])
                S.op("dve", "tensor_copy", dict(out=W["v1b"][:], in_=W["v1f"][:]), [B["v1f"]], [B["v1b"]])
                S.op("dve", "tensor_copy", dict(out=W["v2b"][:], in_=W["v2f"][:]), [B["v2f"]], [B["v2b"]])
            cv = W["cv"]

            def col(base, i):
                return cv[:, base + i:base + i + 1]
            S.op("dve", "tensor_scalar", dict(out=W["omka"][:], in0=cv[:, CV_KA:CV_KA + 4], scalar1=-1.0, scalar2=1.0,
                                              op0=ALU.mult, op1=ALU.add), [B["cv"]], [B["omka"]])
            S.dma(W["esk"][:], sinkL[l].broadcast_to([64, 8]), writes=[B["esk"]])
            S.op("act", "activation", dict(out=W["esk"][:], in_=W["esk"][:], func=AF.Exp), [B["esk"]], [B["esk"]])
            S.op("pool", "memset", dict(ap=W["onesb"][:], constant=1.0), [], [B["onesb"]])
            S.dma(W["augf"][64:67, :, :], c_qaug, writes=[B["augf"]])
            S.dma(W["kaugf"][64:67, :, :], c_kaug, writes=[B["kaugf"], B["t1"], B["t2"], B["t3"]])
            S.op("act", "activation", dict(out=W["qT"][64:67, :, :], in_=W["augf"][64:67, :, :], func=AF.Copy), [B["augf"]], [B["qT"]])
            S.op("act", "activation", dict(out=W["kT"][64:67, :, :], in_=W["kaugf"][64:67, :, :], func=AF.Copy), [B["kaugf"], B["t1"], B["t2"], B["t3"]], [B["kT"]])
            S.op("pool", "memset", dict(ap=W["ulast"][:], constant=0.0), [], [B["ulast"]])
            S.op("pool", "memset", dict(ap=W["H"][:], constant=0.0), [], [B["H"]])
            S.op("pool", "memset", dict(ap=W["Hb"][:], constant=0.0), [], [B["Hb"]])
            S.op("pool", "memset", dict(ap=W["Vaug"][:], constant=1.0), [], [B["Vaug"]])
            S.op("pool", "memset", dict(ap=W["kT"][0:64, :, :], constant=0.0), [], [B["kT"]])

            pj = {"i": 0, "b": 0}

            def fbank():
                pj["i"] += 1
                return pj["i"] % 2

            def bbank():
                pj["b"] += 1
                return 3 + pj["b"] % 2

            def proj_fm(col0, ncols):
                k = fbank()
                for c in range(8):
                    S.op("pe", "matmul", dict(out=PB[k][0:ncols, 0:TM], lhsT=W["WinT"][:, c, col0:col0 + ncols], rhs=W["xT"][:, c, :],
                                              start=(c == 0), stop=(c == 7)), [B["WinT"], B["xT"]], [bPB[k]])
                return k

            def shift_chunk(k, ci, dst, bdst):
                u = W["u"]
                S.op("act", "activation", dict(out=u[:, 1:1 + TM], in_=PB[k][:, 0:TM], func=AF.Copy), [bPB[k]], [B["u"]])
                S.op("pool", "tensor_copy", dict(out=u[:, 0:1], in_=W["ulast"][:, ci:ci + 1]), [B["ulast"]], [B["u"]])
                S.op("dve", "tensor_tensor", dict(out=W["dsh"][:], in0=u[:, 0:TM], in1=u[:, 1:1 + TM], op=ALU.subtract), [B["u"]], [B["dsh"]])
                S.op("dve", "scalar_tensor_tensor", dict(out=dst, in0=W["dsh"][:], scalar=col(CV_MU, ci), in1=u[:, 1:1 + TM],
                                                          op0=ALU.mult, op1=ALU.add), [B["dsh"], B["u"], B["cv"]], [bdst])
                S.op("pool", "tensor_copy", dict(out=W["ulast"][:, ci:ci + 1], in_=u[:, TM:TM + 1]), [B["u"]], [B["ulast"]])

            RW0 = 768

            def front(it):
                par = it % 2
                AR, BT, KT, TM3, gCt, bonus, sgd = (W["AR%d" % par], W["BT%d" % par], W["KT%d" % par], W["TM3%d" % par],
                                                    W["gC%d" % par], W["bonus%d" % par], W["sgd%d" % par])
                attT, battT = W["attT%d" % par], B["attT%d" % par]
                bAR, bBT, bKT, bTM3, bgC, bbonus, bsgd = (B["AR%d" % par], B["BT%d" % par], B["KT%d" % par], B["TM3%d" % par],
                                                          B["gC%d" % par], B["bonus%d" % par], B["sgd%d" % par])
                tok0 = it * TM
                bsrc = b_src_fn(it)
                S.dma(W["xt"][:], src[tok0:tok0 + TM, :].rearrange("(j p) d -> p j d", p=128), reads=[bsrc], writes=[B["xt"]])
                for j in range(SUBM):
                    S.op("act", "activation", dict(out=W["xnb"][:], in_=W["xt"][:, j, :], func=AF.Square, accum_out=W["ss"][:, j:j + 1]),
                         [B["xt"]], [B["xnb"], B["ss"]])
                S.op("dve", "tensor_scalar", dict(out=W["rstd"][:], in0=W["ss"][:], scalar1=1.0 / D, scalar2=RMS_EPS, op0=ALU.mult, op1=ALU.add),
                     [B["ss"]], [B["rstd"]])
                S.op("act", "activation", dict(out=W["rstd"][:], in_=W["rstd"][:], func=AF.Sqrt), [B["rstd"]], [B["rstd"]])
                S.op("dve", "reciprocal", dict(out=W["rstd"][:], in_=W["rstd"][:]), [B["rstd"]], [B["rstd"]])
                yield
                for j in range(SUBM):
                    S.op("dve", "tensor_scalar", dict(out=W["xnb"][:], in0=W["xt"][:, j, :], scalar1=W["rstd"][:, j:j + 1], scalar2=None, op0=ALU.mult),
                         [B["xt"], B["rstd"]], [B["xnb"]])
                    pT = PB[2][:].bitcast(BF16)
                    for c in range(8):
                        S.op("pe", "transpose", dict(out=pT[:, c * 128:(c + 1) * 128], in_=W["xnb"][:, c * 128:(c + 1) * 128], identity=C["identb"][:]),
                             [B["xnb"], bC], [bPB[2]])
                    S.op("act", "activation", dict(out=W["xT"][:, :, j * 128:(j + 1) * 128], in_=pT[:, 0:1024].rearrange("p (c t) -> p c t", t=128),
                                                   func=AF.Copy), [bPB[2]], [B["xT"]])
                    yield
                for h in range(8):
                    k = proj_fm(h * 64, 64)
                    S.op("act", "activation", dict(out=W["qT"][0:64, h, :], in_=PB[k][0:64, 0:TM], func=AF.Copy, scale=0.125), [bPB[k]], [B["qT"]])
                    yield
                for kv in range(2):
                    k = proj_fm(512 + kv * 64, 64)
                    S.op("dve", "tensor_copy", dict(out=W["kT"][0:64, kv, 128:128 + TM], in_=PB[k][0:64, 0:TM]), [bPB[k]], [B["kT"]])
                    yield
                for j in range(SUBM):
                    k = fbank()
                    for c in range(8):
                        S.op("pe", "matmul", dict(out=PB[k][:, 0:128], lhsT=W["xT"][:, c, j * 128:(j + 1) * 128], rhs=W["WinT"][:, c, 640:768],
                                                  start=(c == 0), stop=(c == 7)), [B["WinT"], B["xT"]], [bPB[k]])
                    S.op("dve", "tensor_copy", dict(out=W["Vaug"][:, 1 + j, :, 0:64], in_=PB[k][:, 0:128].rearrange("p (a b) -> p a b", b=64)),
                         [bPB[k]], [B["Vaug"]])
                    yield
                k = proj_fm(RW0 + 1536, 128)
                shift_chunk(k, 12, W["t0"], B["t0"])
                S.op("act", "activation", dict(out=W["twa"][0:64, :], in_=W["t0"][0:64, :], func=AF.Tanh), [B["t0"]], [B["twa"]])
                S.op("dve", "tensor_copy", dict(out=W["twa"][64:128, :], in_=W["t0"][64:128, :]), [B["t0"]], [B["twa"]])
                yield
                k = proj_fm(RW0 + 1664, 128)
                shift_chunk(k, 13, W["t0"], B["t0"])
                S.op("act", "activation", dict(out=sgd[:], in_=W["t0"], func=AF.Sigmoid), [B["t0"]], [bsgd])
                yield
                for cc in range(4):
                    k = proj_fm(RW0 + 1024 + cc * 128, 128)
                    shift_chunk(k, 8 + cc, W["vall"][:, cc, :], B["vall"])
                    yield
                if l == 0:
                    S.dma(vf[:, :, tok0:tok0 + TM], W["vall"][:], reads=[B["vall"]], writes=[b_vf[it]], eng="sp")
                else:
                    vfst = bonus
                    S.dma(vfst[:], vf[:, :, tok0:tok0 + TM], reads=[b_vf[it]], writes=[bbonus])
                    S.op("pool", "tensor_copy", dict(out=W["vbf"][:], in_=W["vall"][:]), [B["vall"]], [B["vbf"]])
                    k = fbank()
                    for cc in range(4):
                        S.op("pe", "matmul", dict(out=PB[k][0:32, 0:TM], lhsT=W["v1b"][:, cc, :], rhs=W["vbf"][:, cc, :], start=(cc == 0), stop=(cc == 3)),
                             [B["v1b"], B["vbf"]], [bPB[k]])
                    S.op("act", "activation", dict(out=W["lvb"][:], in_=PB[k][0:32, 0:TM], func=AF.Copy), [bPB[k]], [B["lvb"]])
                    yield
                    for cc in range(4):
                        k = fbank()
                        S.op("pe", "matmul", dict(out=PB[k][:, 0:TM], lhsT=W["v2b"][0:32, cc * 128:(cc + 1) * 128], rhs=W["lvb"][:], start=True, stop=True),
                             [B["v2b"], B["lvb"]], [bPB[k]])
                        S.op("act", "activation", dict(out=W["t0"], in_=PB[k][:, 0:TM], func=AF.Sigmoid, bias=col(CV_V0, cc)), [bPB[k], B["cv"]], [B["t0"]])
                        S.op("dve", "tensor_tensor", dict(out=W["t1"], in0=vfst[:, cc, :], in1=W["vall"][:, cc, :], op=ALU.subtract),
                             [bbonus, B["vall"]], [B["t1"]])
                        S.op("dve", "tensor_tensor", dict(out=W["t1"], in0=W["t1"], in1=W["t0"], op=ALU.mult), [B["t1"], B["t0"]], [B["t1"]])
                        S.op("dve", "tensor_tensor", dict(out=W["vall"][:, cc, :], in0=W["vall"][:, cc, :], in1=W["t1"], op=ALU.add),
                             [B["vall"], B["t1"]], [B["vall"]])
                        yield
                S.op("pool", "tensor_copy", dict(out=W["vbf"][:], in_=W["vall"][:]), [B["vall"]], [B["vbf"]])
                yield
                for cc in range(4):
                    k = proj_fm(RW0 + cc * 128, 128)
                    shift_chunk(k, cc, W["rk"][:, 0, :], B["rk"])
                    yield
                    k = proj_fm(RW0 + 512 + cc * 128, 128)
                    shift_chunk(k, 4 + cc, W["rk"][:, 1, :], B["rk"])
                    yield
                    r_ = W["rk"][:, 0, :]
                    k_ = W["rk"][:, 1, :]
                    v_ = W["vall"][:, cc, :]
                    ld, cl, g, gi, gm1, a, kk, kkn, kp, ba, tA, tB = [W["t%d" % i] for i in range(12)]
                    bl = [B["t%d" % i] for i in range(12)]
                    kw = fbank()
                    S.op("pe", "matmul", dict(out=PB[kw][:, 0:TM], lhsT=W["w2a2b"][0:64, cc * 128:(cc + 1) * 128], rhs=W["twa"][0:64, :], start=True, stop=True),
                         [B["w2a2b"], B["twa"]], [bPB[kw]])
                    ka_ = fbank()
                    S.op("pe", "matmul", dict(out=PB[ka_][:, 0:TM], lhsT=W["w2a2b"][64:128, cc * 128:(cc + 1) * 128], rhs=W["twa"][64:128, :], start=True, stop=True),
                         [B["w2a2b"], B["twa"]], [bPB[ka_]])
                    S.op("act", "activation", dict(out=ld, in_=PB[kw][:, 0:TM], func=AF.Sigmoid, bias=col(CV_W0, cc)), [bPB[kw], B["cv"]], [bl[0]])
                    S.op("act", "activation", dict(out=a, in_=PB[ka_][:, 0:TM], func=AF.Sigmoid, bias=col(CV_A0, cc)), [bPB[ka_], B["cv"]], [bl[5]])
                    S.op("act", "activation", dict(out=tA, in_=k_, func=AF.Square, scale=col(CV_KK, cc)), [B["rk"], B["cv"]], [bl[10]])
                    yield
                    S.op("dve", "tensor_scalar", dict(out=kkn, in0=a, scalar1=col(CV_KA, cc), scalar2=W["omka"][:, cc:cc + 1], op0=ALU.mult, op1=ALU.add),
                         [bl[5], B["cv"], B["omka"]], [bl[7]])
                    S.op("pool", "tensor_tensor", dict(out=kp, in0=k_, in1=kkn, op=ALU.mult), [B["rk"], bl[7]], [bl[8]])
                    S.op("dve", "scalar_tensor_tensor", dict(out=ba, in0=r_, scalar=col(CV_RK, cc), in1=kp, op0=ALU.mult, op1=ALU.mult),
                         [B["rk"], B["cv"], bl[8]], [bl[9]])
                    yield
                    S.op("dve", "tensor_tensor_scan", dict(out=cl, data0=C["resetm"][:], data1=ld, initial=0.0, op0=ALU.mult, op1=ALU.add),
                         [bC, bl[0]], [bl[1]])
                    S.op("act", "activation", dict(out=g, in_=cl, func=AF.Exp, scale=-EM05), [bl[1]], [bl[2]])
                    S.op("act", "activation", dict(out=gi, in_=cl, func=AF.Exp, scale=EM05), [bl[1]], [bl[3]])
                    S.op("pool", "tensor_tensor", dict(out=gm1, in0=cl, in1=ld, op=ALU.subtract), [bl[1], bl[0]], [bl[4]])
                    S.op("act", "activation", dict(out=gm1, in_=gm1, func=AF.Exp, scale=-EM05), [bl[4]], [bl[4]])
                    S.op("pool", "tensor_copy", dict(out=gCt[:, cc, :], in_=g.rearrange("p (n t) -> p n t", t=CH)[:, :, CH - 1]), [bl[2]], [bgC])
                    yield
                    k1 = fbank()
                    S.op("pe", "matmul", dict(out=PB[k1][:, 0:TM], lhsT=C["bones"][:], rhs=tA, start=True, stop=True), [bC, bl[10]], [bPB[k1]])
                    k2 = fbank()
                    S.op("pe", "matmul", dict(out=PB[k2][:, 0:TM], lhsT=C["bones"][:], rhs=ba, start=True, stop=True), [bC, bl[9]], [bPB[k2]])
                    S.op("dve", "tensor_scalar", dict(out=tB, in0=PB[k1][:, 0:TM], scalar1=1e-18, scalar2=None, op0=ALU.max), [bPB[k1]], [bl[11]])
                    S.op("act", "activation", dict(out=tB, in_=tB, func=AF.Ln), [bl[11]], [bl[11]])
                    S.op("act", "activation", dict(out=tB, in_=tB, func=AF.Exp, scale=-0.5), [bl[11]], [bl[11]])
                    S.op("dve", "tensor_tensor", dict(out=bonus[:, cc, :], in0=PB[k2][:, 0:TM], in1=v_, op=ALU.mult), [bPB[k2], B["vall"]], [bbonus])
                    yield
                    S.op("dve", "scalar_tensor_tensor", dict(out=kkn, in0=k_, scalar=col(CV_KK, cc), in1=tB, op0=ALU.mult, op1=ALU.mult),
                         [B["rk"], B["cv"], bl[11]], [bl[7]])
                    S.op("pool", "tensor_tensor", dict(out=AR[:, cc, :, 1, :], in0=r_.rearrange("p (n t) -> p n t", t=CH),
                                                       in1=g.rearrange("p (n t) -> p n t", t=CH), op=ALU.mult), [B["rk"], bl[2]], [bAR])
                    S.op("dve", "scalar_tensor_tensor", dict(out=AR[:, cc, :, 0, :], in0=kkn.rearrange("p (n t) -> p n t", t=CH), scalar=-1.0,
                                                              in1=gm1.rearrange("p (n t) -> p n t", t=CH), op0=ALU.mult, op1=ALU.mult),
                         [bl[7], bl[4]], [bAR])
                    S.op("dve", "tensor_tensor", dict(out=ba, in0=kkn, in1=a, op=ALU.mult), [bl[7], bl[5]], [bl[9]])
                    S.op("pool", "tensor_tensor", dict(out=BT[:, cc, :], in0=ba, in1=gi, op=ALU.mult), [bl[9], bl[3]], [bBT])
                    S.op("dve", "tensor_tensor", dict(out=KT[:, cc, :], in0=kp, in1=gi, op=ALU.mult), [bl[8], bl[3]], [bKT])
                    yield
                    for n in range(NCH):
                        S.op("dve", "tensor_scalar", dict(out=tB[:, n * CH:(n + 1) * CH], in0=gi[:, n * CH:(n + 1) * CH],
                                                          scalar1=g[:, n * CH + CH - 1:n * CH + CH], scalar2=None, op0=ALU.mult), [bl[3], bl[2]], [bl[11]])
                    S.op("pool", "tensor_tensor", dict(out=W["bp"][:], in0=ba, in1=tB, op=ALU.mult), [bl[9], bl[11]], [B["bp"]])
                    S.op("dve", "tensor_tensor", dict(out=W["kpp"][:], in0=kp, in1=tB, op=ALU.mult), [bl[8], bl[11]], [B["kpp"]])
                    yield
                    trp = PB[2][:].bitcast(BF16)
                    for hh in range(2):
                        h = cc * 2 + hh
                        pb = hh * 64
                        for n in range(NCH):
                            for oi, (srcT, bsrcT) in enumerate(((W["bp"][:], B["bp"]), (W["kpp"][:], B["kpp"]), (W["vbf"][:, cc, :], B["vbf"]))):
                                o0 = (n * 3 + oi) * 64
                                S.op("pe", "transpose", dict(out=trp[0:64, o0:o0 + 64], in_=srcT[pb:pb + 64, n * CH:(n + 1) * CH],
                                                             identity=C["identb"][pb:pb + 64, pb:pb + 64]), [bsrcT, bC], [bPB[2]])
                        S.op("act", "activation", dict(out=TM3[:, h, :, :, :].rearrange("p n o t -> p (n o t)"),
                                                       in_=trp[0:64, 0:NCH * 192], func=AF.Copy), [bPB[2]], [bTM3])
                        yield
                for j in range(SUBM):
                    gsub = it * SUBM + j
                    sbs = [1] if gsub == 0 else [0, 1]
                    for kv in range(2):
                        for sb in sbs:
                            kcol = (j + sb) * 128
                            K_ = 67 if sb == 0 else 66
                            S.op("pe", "matmul", dict(out=PB[sb][:, :], lhsT=W["kT"][0:K_, kv, kcol:kcol + 128],
                                                      rhs=W["qT"][0:K_, kv * 4:(kv + 1) * 4, j * 128:(j + 1) * 128], start=True, stop=False),
                                 [B["kT"], B["qT"]], [bPB[sb]])
                            S.op("pe", "matmul", dict(out=PB[sb][:, :], lhsT=C["identb"][:], rhs=C["amask"][:, sb, :], start=False, stop=True),
                                 [bC], [bPB[sb]])
                            S.op("act", "activation", dict(out=W["PT"][:, sb, :], in_=PB[sb][:, :], func=AF.Exp), [bPB[sb]], [B["PT"]])
                        yield
                        yield
                        for i, sb in enumerate(sbs):
                            S.op("pe", "matmul", dict(out=PB[2][0:65, :], lhsT=W["Vaug"][:, j + sb, kv, :], rhs=W["PT"][:, sb, :],
                                                      start=(i == 0), stop=(i == len(sbs) - 1)), [B["Vaug"], B["PT"]], [bPB[2]])
                        for i, sb in enumerate(sbs):
                            S.op("pe", "matmul", dict(out=PB[0][0:64, :], lhsT=W["onesb"][:, :], rhs=W["PT"][:, sb, :],
                                                      start=(i == 0), stop=(i == len(sbs) - 1)), [B["onesb"], B["PT"]], [bPB[0]])
                        for gg in range(4):
                            S.op("dve", "tensor_scalar", dict(out=W["OTc"][:, gg * 128:(gg + 1) * 128], in0=PB[0][0:64, gg * 128:(gg + 1) * 128],
                                                              scalar1=W["esk"][:, kv * 4 + gg:kv * 4 + gg + 1], scalar2=None, op0=ALU.add),
                                 [bPB[0], B["esk"]], [B["OTc"]])
                        S.op("act", "activation", dict(out=W["OTc"], in_=W["OTc"], func=AF.Ln), [B["OTc"]], [B["OTc"]])
                        S.op("act", "activation", dict(out=W["OTc"], in_=W["OTc"], func=AF.Exp, scale=-1.0), [B["OTc"]], [B["OTc"]])
                        yield
                        S.op("dve", "tensor_tensor", dict(out=attT[:, kv * 4:(kv + 1) * 4, j * 128:(j + 1) * 128],
                                                          in0=PB[2][0:64, :].rearrange("p (g t) -> p g t", t=128),
                                                          in1=W["OTc"].rearrange("p (g t) -> p g t", t=128), op=ALU.mult),
                             [B["OTc"], bPB[2]], [battT])
                        yield
                S.op("pool", "tensor_copy", dict(out=W["kT"][0:64, :, 0:128], in_=W["kT"][0:64, :, TM:TM + 128]), [B["kT"]], [B["kT"]])
                S.op("pool", "tensor_copy", dict(out=W["Vaug"][:, 0, :, :], in_=W["Vaug"][:, SUBM, :, :]), [B["Vaug"]], [B["Vaug"]])
                yield

            def back(it):
                par = it % 2
                AR, BT, KT, TM3, gCt, bonus, sgd = (W["AR%d" % par], W["BT%d" % par], W["KT%d" % par], W["TM3%d" % par],
                                                    W["gC%d" % par], W["bonus%d" % par], W["sgd%d" % par])
                attT, battT = W["attT%d" % par], B["attT%d" % par]
                bAR, bBT, bKT, bTM3, bgC, bbonus, bsgd = (B["AR%d" % par], B["BT%d" % par], B["KT%d" % par], B["TM3%d" % par],
                                                          B["gC%d" % par], B["bonus%d" % par], B["sgd%d" % par])
                tok0 = it * TM
                bsrc = b_src_fn(it)
                for n in range(NCH):
                    for hp in range(4):
                        for hh in range(2):
                            h = hp * 2 + hh
                            pb = hh * 64
                            bank = ((3, 5), (4, 7))[hh][hp % 2]
                            rhsAR = AR[pb:pb + 64, hp, n, :, :]
                            S.op("pe", "matmul", dict(out=PB[bank][0:64, 0:128], lhsT=BT[pb:pb + 64, hp, n * CH:(n + 1) * CH],
                                                      rhs=rhsAR, start=True, stop=True), [bBT, bAR], [bPB[bank]])
                            S.op("pe", "matmul", dict(out=PB[bank][0:64, 128:256], lhsT=KT[pb:pb + 64, hp, n * CH:(n + 1) * CH],
                                                      rhs=rhsAR, start=True, stop=True), [bKT, bAR], [bPB[bank]])
                            S.op("dve", "tensor_tensor", dict(out=W["Amat"][:, h, :], in0=PB[bank][0:64, 0:256],
                                                              in1=C["maskA"][:, 0:256], op=ALU.mult), [bPB[bank], bC], [B["Amat"]])
                        yield
                    trp = PB[5][:].bitcast(BF16)
                    for h in range(8):
                        S.op("pe", "transpose", dict(out=trp[0:64, h * 64:(h + 1) * 64], in_=W["Amat"][:, h, 0:64], identity=C["identb"][0:64, 0:64]),
                             [B["Amat"], bC], [bPB[5]])
                    S.op("act", "activation", dict(out=W["Nm"][:].rearrange("p h c -> p (h c)"), in_=trp[0:64, 0:512], func=AF.Copy), [bPB[5]], [B["Nm"]])
                    S.op("dve", "tensor_tensor", dict(out=W["Pm"][:].rearrange("p h c -> p (h c)"), in0=W["Nm"][:].rearrange("p h c -> p (h c)"),
                                                      in1=C["identI8"][:], op=ALU.add), [B["Nm"], bC], [B["Pm"]])
                    yield
                    M_, MT_, M2_, MT2_ = "Nm", "NTm", "Nm2", "NTm2"
                    P_, P2_ = "Pm", "Pm2"
                    for lev in range(5):
                        last = (lev == 4)
                        if lev == 0:
                            mt_ap = lambda h: W["Amat"][:, h, 0:64]
                            bmt = B["Amat"]
                        else:
                            mt_ap = lambda h, MT_=MT_: W[MT_][:, h, :]
                            bmt = B[MT_]
                        for h in range(8):
                            S.op("pe", "matmul", dict(out=PB[6][0:64, h * 64:(h + 1) * 64], lhsT=W[M_][:, h, :], rhs=mt_ap(h), start=True, stop=True),
                                 [B[M_], bmt], [bPB[6]])
                        S.op("act", "activation", dict(out=W[MT2_][:].rearrange("p h c -> p (h c)"), in_=PB[6][0:64, :], func=AF.Copy), [bPB[6]], [B[MT2_]])
                        yield
                        if not last:
                            for h in range(8):
                                S.op("pe", "matmul", dict(out=PB[7][0:64, h * 64:(h + 1) * 64], lhsT=mt_ap(h), rhs=W[M_][:, h, :], start=True, stop=True),
                                     [B[M_], bmt], [bPB[7]])
                            S.op("dve", "tensor_copy", dict(out=W[M2_][:].rearrange("p h c -> p (h c)"), in_=PB[7][0:64, :]), [bPB[7]], [B[M2_]])
                            yield
                        for h in range(8):
                            S.op("pe", "matmul", dict(out=PB[5][0:64, h * 64:(h + 1) * 64], lhsT=W[MT2_][:, h, :], rhs=W[P_][:, h, :], start=True, stop=True),
                                 [B[MT2_], B[P_]], [bPB[5]])
                        S.op("dve", "tensor_tensor", dict(out=W[P2_][:].rearrange("p h c -> p (h c)"), in0=PB[5][0:64, :],
                                                          in1=W[P_][:].rearrange("p h c -> p (h c)"), op=ALU.add), [bPB[5], B[P_]], [B[P2_]])
                        yield
                        M_, M2_ = M2_, M_
                        MT_, MT2_ = MT2_, MT_
                        P_, P2_ = P2_, P_
                    trp = PB[6][:].bitcast(BF16)
                    for h in range(8):
                        S.op("pe", "transpose", dict(out=trp[0:64, h * 64:(h + 1) * 64], in_=W[P_][:, h, :], identity=C["identb"][0:64, 0:64]),
                             [B[P_], bC], [bPB[6]])
                    S.op("act", "activation", dict(out=W[P2_][:].rearrange("p h c -> p (h c)"), in_=trp[0:64, 0:512], func=AF.Copy), [bPB[6]], [B[P2_]])
                    TinvT = P2_
                    yield
                    for h in range(8):
                        hp, hh = h // 2, h % 2
                        pb = hh * 64
                        S.op("pe", "matmul", dict(out=PB[3 + hh][0:64, hp * 64:(hp + 1) * 64], lhsT=AR[pb:pb + 64, hp, n, 0, :], rhs=W["Hb"][pb:pb + 64, hp, :],
                                                  start=True, stop=True), [bAR, B["Hb"]], [bPB[3 + hh]])
                        S.op("pe", "matmul", dict(out=PB[7][0:64, h * 64:(h + 1) * 64], lhsT=W["Amat"][:, h, 128:192], rhs=TM3[:, h, n, 2, :],
                                                  start=True, stop=True), [B["Amat"], bTM3], [bPB[7]])
                    S.op("act", "activation", dict(out=W["Z2b"], in_=PB[7][0:64, :], func=AF.Copy), [bPB[7]], [B["Z2b"]])
                    z2 = W["Z2b"].rearrange("p (q e c) -> p q e c", e=2, c=64)
                    zb4 = W["Zb"][:].rearrange("p (q e) c -> p q e c", e=2)
                    for hh in range(2):
                        S.op("dve", "tensor_tensor", dict(out=zb4[:, :, hh, :], in0=PB[3 + hh][0:64, 0:256].rearrange("p (q c) -> p q c", c=64),
                                                          in1=z2[:, :, hh, :], op=ALU.add), [bPB[3 + hh], B["Z2b"]], [B["Zb"]])
                    yield
                    for h in range(8):
                        S.op("pe", "matmul", dict(out=PB[6][0:64, h * 64:(h + 1) * 64], lhsT=W[TinvT][:, h, :], rhs=W["Zb"][:, h, :], start=True, stop=True),
                             [B[TinvT], B["Zb"]], [bPB[6]])
                    S.op("act", "activation", dict(out=W["Ub"][:].rearrange("p h c -> p (h c)"), in_=PB[6][0:64, :], func=AF.Copy), [bPB[6]], [B["Ub"]])
                    yield
                    for h in range(8):
                        hp, hh = h // 2, h % 2
                        pb = hh * 64
                        S.op("pe", "matmul", dict(out=PB[3 + hh][pb:pb + 64, 256 + hp * 64:256 + (hp + 1) * 64], lhsT=W["Hb"][pb:pb + 64, hp, :],
                                                  rhs=AR[pb:pb + 64, hp, n, 1, :], start=True, stop=True), [B["Hb"], bAR], [bPB[3 + hh]])
                        yo = PB[5][pb:pb + 64, hp * 64:(hp + 1) * 64]
                        S.op("pe", "matmul", dict(out=yo, lhsT=W["Ub"][:, h, :], rhs=W["Amat"][:, h, 64:128], start=True, stop=False),
                             [B["Ub"], B["Amat"]], [bPB[5]])
                        S.op("pe", "matmul", dict(out=yo, lhsT=TM3[:, h, n, 2, :], rhs=W["Amat"][:, h, 192:256], start=False, stop=True),
                             [bTM3, B["Amat"]], [bPB[5]])
                        ho = PB[5][pb:pb + 64, 256 + hp * 64:256 + (hp + 1) * 64]
                        S.op("pe", "matmul", dict(out=ho, lhsT=TM3[:, h, n, 0, :], rhs=W["Ub"][:, h, :], start=True, stop=False),
                             [bTM3, B["Ub"]], [bPB[5]])
                        S.op("pe", "matmul", dict(out=ho, lhsT=TM3[:, h, n, 1, :], rhs=TM3[:, h, n, 2, :], start=False, stop=True),
                             [bTM3], [bPB[5]])
                        if h % 2 == 1:
                            yield
                    S.op("act", "activation", dict(out=W["yT"][:, :, n * CH:(n + 1) * CH], in_=PB[5][:, 0:256].rearrange("p (c t) -> p c t", t=64), func=AF.Copy),
                         [bPB[5]], byT)
                    for hh in range(2):
                        pb = hh * 64
                        S.op("dve", "tensor_tensor", dict(out=W["yT"][pb:pb + 64, :, n * CH:(n + 1) * CH],
                                                          in0=PB[3 + hh][pb:pb + 64, 256:512].rearrange("p (c t) -> p c t", t=64),
                                                          in1=W["yT"][pb:pb + 64, :, n * CH:(n + 1) * CH], op=ALU.add), [bPB[3 + hh]] + byT, byT)
                    for hp in range(4):
                        S.op("dve", "scalar_tensor_tensor", dict(out=W["H"][:, hp, :], in0=W["H"][:, hp, :], scalar=gCt[:, hp, n:n + 1],
                                                                  in1=PB[5][:, 256 + hp * 64:256 + (hp + 1) * 64], op0=ALU.mult, op1=ALU.add),
                             [B["H"], bgC, bPB[5]], [B["H"]])
                    S.op("act", "activation", dict(out=W["Hb"][:], in_=W["H"][:], func=AF.Copy), [B["H"]], [B["Hb"]])
                    yield
                def gate_mm(cc):
                    S.op("pe", "matmul", dict(out=PB[5][:, (cc % 2) * 256:(cc % 2) * 256 + TM], lhsT=W["g2b"][:, cc * 128:(cc + 1) * 128], rhs=sgd[:],
                                              start=True, stop=True), [B["g2b"], bsgd], [bPB[5]])
                yield
                for c0 in (0, 2):
                    ccs = (c0, c0 + 1)
                    m1 = {c0: 6, c0 + 1: 3}
                    m2 = {c0: 7, c0 + 1: 4}
                    tmp = {c0: W["pp0"][:], c0 + 1: W["pp1"][:]}
                    btmp = {c0: B["pp0"], c0 + 1: B["pp1"]}
                    for cc in ccs:
                        S.op("pe", "matmul", dict(out=PB[m1[cc]][:, 0:TM], lhsT=C["bones"][:], rhs=W["yT"][:, cc, :], start=True, stop=True),
                             [bC, byT[cc]], [bPB[m1[cc]]])
                    if c0 == 0:
                        gate_mm(0)
                        gate_mm(1)
                    yield
                    for cc in ccs:
                        S.op("dve", "scalar_tensor_tensor", dict(out=W["yT"][:, cc, :], in0=PB[m1[cc]][:, 0:TM], scalar=-1.0 / 64, in1=W["yT"][:, cc, :],
                                                                  op0=ALU.mult, op1=ALU.add), [bPB[m1[cc]], byT[cc]], [byT[cc]])
                        S.op("act", "activation", dict(out=tmp[cc], in_=W["yT"][:, cc, :], func=AF.Square), [byT[cc]], [btmp[cc]])
                    yield
                    yield
                    for cc in ccs:
                        S.op("pe", "matmul", dict(out=PB[m2[cc]][:, 0:TM], lhsT=C["bones"][:], rhs=tmp[cc], start=True, stop=True), [bC, btmp[cc]], [bPB[m2[cc]]])
                    yield
                    for cc in ccs:
                        S.op("dve", "tensor_scalar", dict(out=tmp[cc], in0=PB[m2[cc]][:, 0:TM], scalar1=1.0 / 64, scalar2=LNX_EPS, op0=ALU.mult, op1=ALU.add),
                             [bPB[m2[cc]]], [btmp[cc]])
                        S.op("act", "activation", dict(out=tmp[cc], in_=tmp[cc], func=AF.Ln), [btmp[cc]], [btmp[cc]])
                        S.op("act", "activation", dict(out=tmp[cc], in_=tmp[cc], func=AF.Exp, scale=-0.5), [btmp[cc]], [btmp[cc]])
                    yield
                    for cc in ccs:
                        yn = W["yT"][:, cc, :]
                        S.op("dve", "tensor_tensor", dict(out=yn, in0=yn, in1=tmp[cc], op=ALU.mult), [byT[cc], btmp[cc]], [byT[cc]])
                        S.op("dve", "tensor_scalar", dict(out=yn, in0=yn, scalar1=col(CV_LG, cc), scalar2=col(CV_LB, cc), op0=ALU.mult, op1=ALU.add),
                             [byT[cc], B["cv"]], [byT[cc]])
                        S.op("pool", "tensor_tensor", dict(out=yn, in0=yn, in1=bonus[:, cc, :], op=ALU.add), [byT[cc], bbonus], [byT[cc]])
                    yield
                    for cc in ccs:
                        S.op("dve", "tensor_tensor", dict(out=W["rwT"][:, cc, :], in0=PB[5][:, (cc % 2) * 256:(cc % 2) * 256 + TM], in1=W["yT"][:, cc, :], op=ALU.mult),
                             [bPB[5], byT[cc]], [B["rwT"]])
                    if c0 == 0:
                        gate_mm(2)
                        gate_mm(3)
                    yield
                if dbg and "rw" in dbg_out and it == dbg.get("_dbgtile", 0):
                    S.op("pool", "tensor_copy", dict(out=W["yT"][:], in_=W["rwT"][:]), [B["rwT"]], byT)
                    S.dma(dbg_out["rw"], W["yT"][:], reads=byT, writes=[b_out])
                    S.op("pool", "tensor_copy", dict(out=W["yT"][0:64, :, :].rearrange("p a b -> p (a b)"), in_=attT[:, 0:4, :].rearrange("p a b -> p (a b)")),
                         [battT], byT)
                    S.dma(dbg_out["att"], W["yT"][0:64, :, :], reads=byT, writes=[b_out])
                kk_ = 0
                for j in range(SUBM):
                    for nh in range(2):
                        xo = W["xos%d" % (kk_ % 2)]
                        bxo = B["xos%d" % (kk_ % 2)]
                        kk_ += 1
                        r0 = tok0 + j * 128
                        S.dma(xo[:], src[r0:r0 + 128, nh * 512:(nh + 1) * 512], reads=[bsrc], writes=[bxo])
                        k = bbank()
                        for h in range(8):
                            S.op("pe", "matmul", dict(out=PB[k][:, :], lhsT=attT[:, h, j * 128:(j + 1) * 128], rhs=W["WoA"][:, h, nh * 512:(nh + 1) * 512],
                                                      start=(h == 0), stop=False), [battT, B["WoA"]], [bPB[k]])
                        for cc in range(4):
                            S.op("pe", "matmul", dict(out=PB[k][:, :], lhsT=W["rwT"][:, cc, j * 128:(j + 1) * 128], rhs=W["WoR"][:, cc, nh * 512:(nh + 1) * 512],
                                                      start=False, stop=(cc == 3)), [B["rwT"], B["WoR"]], [bPB[k]])
                        S.op("dve", "tensor_tensor", dict(out=xo[:], in0=PB[k][:, :], in1=xo[:], op=ALU.add), [bPB[k], bxo], [bxo])
                        yield
                        S.dma(xs[r0:r0 + 128, nh * 512:(nh + 1) * 512], xo[:], reads=[bxo], writes=[b_xs[it]], eng="sp")

            ntm = dbg.get("_ntm", NTM) if dbg else NTM
            drive([front(0)])
            for it in range(ntm):
                drive([back(it), front(it + 1) if it + 1 < ntm else None], weights=[int(_os.environ.get("MK_RB", "2")), int(_os.environ.get("MK_RF", "1"))])
        S.barrier()

    def ffn(l, last):
        with ExitStack() as st:
            W = tiles(st, [
                ("xt0", [128, SUBF, D], F32), ("xt1", [128, SUBF, D], F32), ("ss", [128, SUBF], F32), ("rstd", [128, SUBF], F32),
                ("ss2", [128, SUBF], F32), ("rstd2", [128, SUBF], F32),
                ("xnb", [128, D], BF16), ("xnb1", [128, D], BF16), ("xT0", [128, 8, TF], BF16), ("xT1", [128, 8, TF], BF16),
                ("wgu0", [128, 8, 256], BF16), ("wgu1", [128, 8, 256], BF16), ("wgu2", [128, 8, 256], BF16), ("Wd", [128, NFC, D], BF16),
                ("gc", [128, 2 + TF], F32), ("gcar", [128, NFC, 2], F32),
                ("c1", [128, TF], F32), ("c2", [128, TF], F32), ("c3", [128, TF], F32), ("sl", [128, TF], F32),
                ("hT", [128, NFC, TF], BF16), ("xo", [128, SUBF, D], F32), ("cv", [128, NCV], F32), ("fingb", [128, D], F32),
            ])
            B = {k: Buf(k) for k in W}
            bw = b_wscr[l]
            cv = W["cv"]
            S.dma(cv[:], cvL[l], writes=[B["cv"]])
            S.dma(W["fingb"][:], fing.broadcast_to([128, D]), writes=[B["fingb"]])
            S.op("pool", "memset", dict(ap=W["gcar"][:], constant=0.0), [], [B["gcar"]])
            bWd = [Buf("Wd%d" % i) for i in range(NFC)]

            def head_dma(it):
                par = it % 2
                tok0 = it * TF
                xt, bxt = W["xt%d" % par], B["xt%d" % par]
                S.dma(xt[:], xs[tok0:tok0 + TF, :].rearrange("(j p) d -> p j d", p=128), reads=[b_xs[2 * it], b_xs[2 * it + 1]], writes=[bxt])

            def head_load(it):
                par = it % 2
                xt, bxt = W["xt%d" % par], B["xt%d" % par]
                for j in range(SUBF):
                    S.op("act", "activation", dict(out=W["xnb"][:], in_=xt[:, j, :], func=AF.Square, accum_out=W["ss"][:, j:j + 1]),
                         [bxt], [B["xnb"], B["ss"]])
                S.op("dve", "tensor_scalar", dict(out=W["rstd"][:], in0=W["ss"][:], scalar1=1.0 / D, scalar2=RMS_EPS, op0=ALU.mult, op1=ALU.add),
                     [B["ss"]], [B["rstd"]])
                S.op("act", "activation", dict(out=W["rstd"][:], in_=W["rstd"][:], func=AF.Sqrt), [B["rstd"]], [B["rstd"]])
                S.op("dve", "reciprocal", dict(out=W["rstd"][:], in_=W["rstd"][:]), [B["rstd"]], [B["rstd"]])

            def head_sub_a(it, j):
                par = it % 2
                xt, bxt = W["xt%d" % par], B["xt%d" % par]
                xn = "xnb" if j % 2 == 0 else "xnb1"
                S.op("dve", "tensor_scalar", dict(out=W[xn][:], in0=xt[:, j, :], scalar1=W["rstd"][:, j:j + 1], scalar2=None, op0=ALU.mult),
                     [bxt, B["rstd"]], [B[xn]])

            def head_sub_b(it, j):
                par = it % 2
                xT, bxT = W["xT%d" % par], B["xT%d" % par]
                pT = PB[4][:].bitcast(BF16)
                xn = "xnb" if j % 2 == 0 else "xnb1"
                for c in range(8):
                    S.op("pe", "transpose", dict(out=pT[:, c * 128:(c + 1) * 128], in_=W[xn][:, c * 128:(c + 1) * 128], identity=C["identb"][:]),
                         [B[xn], bC], [bPB[4]])
                S.op("act", "activation", dict(out=xT[:, :, j * 128:(j + 1) * 128], in_=pT[:, 0:1024].rearrange("p (c t) -> p c t", t=128),
                                               func=AF.Copy), [bPB[4]], [bxT])

            def ffn_gen():
                head_dma(0)
                head_load(0)
                for j in range(SUBF):
                    head_sub_a(0, j)
                    head_sub_b(0, j)
                    yield
                for it in range(NTF):
                    tok0 = it * TF
                    par = it % 2
                    xt, bxt = W["xt%d" % par], B["xt%d" % par]
                    xT, bxT = W["xT%d" % par], B["xT%d" % par]
                    for fc in range(NFC):
                        wk = "wgu%d" % (fc % 3)
                        S.dma(W[wk][:], wguB[l][fc], reads=bw["wgu"], writes=[B[wk]])
                        if fc < 11:
                            for f2 in (2 * fc, 2 * fc + 1):
                                S.dma(W["Wd"][:, f2, :], wdB[l][:, f2 * D:(f2 + 1) * D], reads=bw["wd"], writes=[bWd[f2]])
                        if it + 1 < NTF:
                            if fc == 1:
                                head_dma(it + 1)
                            if fc == 9:
                                head_load(it + 1)
                            if fc in (10, 11):
                                head_sub_a(it + 1, fc - 10)
                            if fc in (14, 16):
                                head_sub_a(it + 1, 2 + (fc - 14) // 2)
                            if fc in (13, 15, 17, 19):
                                head_sub_b(it + 1, (fc - 13) // 2)
                        pg = (fc % 2) * 2
                        pu = pg + 1
                        for c in range(8):
                            S.op("pe", "matmul", dict(out=PB[pg][:, :], lhsT=W[wk][:, c, 0:128], rhs=xT[:, c, :], start=(c == 0), stop=(c == 7)),
                                 [B[wk], bxT], [bPB[pg]])
                        for c in range(8):
                            S.op("pe", "matmul", dict(out=PB[pu][:, :], lhsT=W[wk][:, c, 128:256], rhs=xT[:, c, :], start=(c == 0), stop=(c == 7)),
                                 [B[wk], bxT], [bPB[pu]])
                        gc = W["gc"]
                        S.op("act", "activation", dict(out=gc[:, 2:2 + TF], in_=PB[pg][:, :], func=AF.Copy), [bPB[pg]], [B["gc"]])
                        S.op("pool", "tensor_copy", dict(out=gc[:, 0:2], in_=W["gcar"][:, fc, :]), [B["gcar"]], [B["gc"]])
                        S.op("pool", "tensor_scalar", dict(out=W["c1"][:], in0=gc[:, 0:TF], scalar1=cv[:, CV_CW + fc:CV_CW + fc + 1],
                                                           scalar2=cv[:, CV_CB + fc:CV_CB + fc + 1], op0=ALU.mult, op1=ALU.add), [B["gc"], B["cv"]], [B["c1"]])
                        S.op("dve", "scalar_tensor_tensor", dict(out=W["c2"][:], in0=gc[:, 1:1 + TF], scalar=cv[:, CV_CW + NFC + fc:CV_CW + NFC + fc + 1],
                                                                  in1=W["c1"][:], op0=ALU.mult, op1=ALU.add), [B["gc"], B["cv"], B["c1"]], [B["c2"]])
                        S.op("dve", "scalar_tensor_tensor", dict(out=W["c3"][:], in0=gc[:, 2:2 + TF], scalar=cv[:, CV_CW + 2 * NFC + fc:CV_CW + 2 * NFC + fc + 1],
                                                                  in1=W["c2"][:], op0=ALU.mult, op1=ALU.add), [B["gc"], B["cv"], B["c2"]], [B["c3"]])
                        S.op("pool", "tensor_copy", dict(out=W["gcar"][:, fc, :], in_=gc[:, TF:TF + 2]), [B["gc"]], [B["gcar"]])
                        S.op("act", "activation", dict(out=W["sl"][:], in_=W["c3"][:], func=AF.Silu), [B["c3"]], [B["sl"]])
                        S.op("dve", "tensor_tensor", dict(out=W["hT"][:, fc, :], in0=PB[pu][:, :], in1=W["sl"][:], op=ALU.mult), [bPB[pu], B["sl"]], [B["hT"]])
                        yield
                    kk = 0
                    for j in range(SUBF):
                        for nh in range(2):
                            k = 5 + (kk % 3)
                            kk += 1
                            for fc in range(NFC):
                                S.op("pe", "matmul", dict(out=PB[k][:, :], lhsT=W["hT"][:, fc, j * 128:(j + 1) * 128], rhs=W["Wd"][:, fc, nh * 512:(nh + 1) * 512],
                                                          start=(fc == 0), stop=(fc == NFC - 1)), [B["hT"], bWd[fc]], [bPB[k]])
                            S.op("dve", "tensor_tensor", dict(out=W["xo"][:, j, nh * 512:(nh + 1) * 512], in0=PB[k][:, :], in1=xt[:, j, nh * 512:(nh + 1) * 512],
                                                              op=ALU.add), [bPB[k], bxt], [B["xo"]])
                            yield
                    if not last:
                        S.dma(xs[tok0:tok0 + TF, :].rearrange("(j p) d -> p j d", p=128), W["xo"][:], reads=[B["xo"]], writes=[b_xs[2 * it], b_xs[2 * it + 1]], eng="sp")
                    else:
                        for j in range(SUBF):
                            S.op("act", "activation", dict(out=W["xnb"][:], in_=W["xo"][:, j, :], func=AF.Square, accum_out=W["ss2"][:, j:j + 1]),
                                 [B["xo"]], [B["xnb"], B["ss2"]])
                        S.op("dve", "tensor_scalar", dict(out=W["rstd2"][:], in0=W["ss2"][:], scalar1=1.0 / D, scalar2=RMS_EPS, op0=ALU.mult, op1=ALU.add),
                             [B["ss2"]], [B["rstd2"]])
                        S.op("act", "activation", dict(out=W["rstd2"][:], in_=W["rstd2"][:], func=AF.Sqrt), [B["rstd2"]], [B["rstd2"]])
                        S.op("dve", "reciprocal", dict(out=W["rstd2"][:], in_=W["rstd2"][:]), [B["rstd2"]], [B["rstd2"]])
                        for j in range(SUBF):
                            S.op("dve", "scalar_tensor_tensor", dict(out=W["xo"][:, j, :], in0=W["xo"][:, j, :], scalar=W["rstd2"][:, j:j + 1], in1=W["fingb"][:],
                                                                      op0=ALU.mult, op1=ALU.mult), [B["xo"], B["rstd2"], B["fingb"]], [B["xo"]])
                        S.dma(out_d[tok0:tok0 + TF, :].rearrange("(j p) d -> p j d", p=128), W["xo"][:], reads=[B["xo"]], writes=[b_out], eng="sp")

            if l + 1 < L:
                Wp = tiles(st, pre_tiles(2))
                drive([ffn_gen(), prepass_gen(l + 1, Wp, "sp", True, load_eng="act", engs=("dve",), NSTG=2)], weights=[6, 1])
            else:
                drive([ffn_gen()])
        S.barrier()

    nlayers = dbg.get("_layers", L) if dbg else L
    stop_after = dbg.get("_stop", None) if dbg else None
    for l in range(nlayers):
        if l == 0:
            prepass(l)
        if stop_after == ("prepass", l):
            break
        if l == 0:
            mixer(l, x_in, lambda it: b_x)
        else:
            mixer(l, xs, lambda it: b_xs[it])
        if stop_after == ("mixer", l):
            break
        ffn(l, last=(l == L - 1))
    S.finish()
    cst.close()
    return nc, S


def _colmajor(v):
    n = v.shape[0] // 128
    return np.ascontiguousarray(v.reshape(n, 128).T)


def host_layout(inp):
    f = np.float32
    d = {}
    w_in = inp["w_in"]
    d["winL"] = np.ascontiguousarray(w_in.reshape(L, 8, 128, NIN).transpose(0, 2, 1, 3))
    w_out = inp["w_out"]
    d["woAL"] = np.ascontiguousarray(w_out[:, :512].reshape(L, 8, 64, D).transpose(0, 2, 1, 3))
    d["woRL"] = np.ascontiguousarray(w_out[:, 512:].reshape(L, 4, 128, D).transpose(0, 2, 1, 3))
    wg = inp["ffn_w_gate"].reshape(L, 8, 128, NFC, 128)
    wu = inp["ffn_w_up"].reshape(L, 8, 128, NFC, 128)
    wgu = np.stack([wg, wu], axis=4)
    d["wguL"] = np.ascontiguousarray(wgu.reshape(L, 8, 128, 2, 11 * 256))
    d["wdL"] = np.ascontiguousarray(inp["ffn_w_down"].reshape(L, NFC, 128, D).transpose(0, 2, 1, 3).reshape(L, 128, NFC * D))
    d["w2a2L"] = np.ascontiguousarray(np.concatenate([inp["w2"], inp["a2"]], axis=1))
    d["g2L"] = np.ascontiguousarray(inp["g2"])
    d["v1L"] = np.ascontiguousarray(inp["v1"][0].reshape(4, 128, 32).transpose(1, 0, 2))
    d["v2L"] = np.ascontiguousarray(inp["v2"][0])
    cv = np.zeros((L, 128, NCV), f)
    for l in range(L):
        cv[l, :, CV_G1:CV_G1 + 8] = _colmajor(inp["norm1_g"][l])
        cv[l, :, CV_G2:CV_G2 + 8] = _colmajor(inp["norm2_g"][l])
        cv[l, :, CV_MU:CV_MU + 14] = _colmajor(inp["shift_mu"][l])
        for nm, o in (("w0", CV_W0), ("a0", CV_A0), ("k_k", CV_KK), ("k_a", CV_KA), ("r_k", CV_RK), ("lnx_g", CV_LG), ("lnx_b", CV_LB)):
            cv[l, :, o:o + 4] = _colmajor(inp[nm][l])
        if l > 0:
            cv[l, :, CV_V0:CV_V0 + 4] = _colmajor(inp["v0"][l - 1])
        for j in range(3):
            cv[l, :, CV_CW + j * NFC:CV_CW + (j + 1) * NFC] = _colmajor(inp["conv_w"][l, j])
        cv[l, :, CV_CB:CV_CB + NFC] = _colmajor(inp["conv_b"][l])
    d["cvL"] = cv
    d["sinkL"] = np.ascontiguousarray(inp["attn_sinks"].reshape(L, 1, 8))
    d["fing"] = np.ascontiguousarray(inp["final_g"].reshape(1, D))
    d["c_ident"] = np.eye(128, dtype=f)
    s = np.arange(64)[:, None]
    t = np.arange(64)[None, :]
    strict = (t > s).astype(f)
    incl = (t >= s).astype(f)
    m256 = np.concatenate([strict, incl, strict, incl], axis=1)
    d["c_maskA"] = np.ascontiguousarray(np.concatenate([m256, m256], axis=1))
    rm = np.ones((128, TM), f)
    rm[:, ::CH] = 0.0
    d["c_reset"] = rm
    d["c_identI8"] = np.ascontiguousarray(np.tile(np.eye(64, dtype=f), (1, 8)))
    bo = np.zeros((128, 128), f)
    bo[:64, :64] = 1.0
    bo[64:, 64:] = 1.0
    d["c_bones"] = bo
    s = np.arange(128)[:, None]
    t = np.arange(128)[None, :]
    NEG = -30000.0
    m_prev = np.where(s > t, 0.0, NEG).astype(f)
    m_cur = np.where(s <= t, 0.0, NEG).astype(f)
    am = np.stack([np.tile(m_prev, (1, 4)), np.tile(m_cur, (1, 4))], axis=1)
    d["c_amask"] = np.ascontiguousarray(am)
    slopes = (2.0 ** (-8.0 * np.arange(1, 9) / 8)).astype(f)
    tt = (np.arange(TM) % 128).astype(f)
    qa = np.zeros((3, 8, TM), f)
    qa[0] = slopes[:, None]
    qa[1] = -slopes[:, None] * tt[None, :]
    qa[2] = -128.0 * slopes[:, None]
    d["c_qaug"] = qa
    ka = np.zeros((3, 2, 128 + TM), f)
    ka[0] = (np.arange(128 + TM) % 128).astype(f)[None, :]
    ka[1] = 1.0
    ka[2] = 1.0
    d["c_kaug"] = ka
    return d


_CACHE = {}


def kernel(**inputs):
    inp = {k: np.asarray(v, dtype=np.float32) for k, v in inputs.items()}
    shared = host_layout(inp)
    if "nc" not in _CACHE:
        _CACHE["nc"] = build_program()[0]
    nc = _CACHE["nc"]
    x = inp["x"]
    in_maps = []
    for b in range(8):
        m = dict(shared)
        m["x"] = np.ascontiguousarray(x[b])
        in_maps.append(m)
    res = run_bass_kernel_spmd(nc, in_maps, core_ids=list(range(8)))
    out = np.stack([np.asarray(res.results[b]["out"], dtype=np.float32) for b in range(8)], axis=0)
    return out
```
